# Optimizing a Trainium2 kernel written in Bass

```python
import math
import jax, jax.numpy as jnp
from jax import lax
import numpy as np

D_MODEL = 2048
BATCH = 4
SEQ = 2048
DEPTH = 1
DEC_BATCH = 128
DEC_SEQ = 4
PAST_LEN = 2048
PAGE_SIZE = 128

MIX = D_MODEL
GLA_HEADS = 4
GLA_WIDTH = MIX // 2
GLA_DV = GLA_WIDTH // GLA_HEADS
GLA_DK = GLA_DV // 2
GLA_RANK = 16
GLA_TAU = 16.0
GLA_CHUNK = 16
NSA_HEADS = 8
NSA_KV = 2
NSA_REP = NSA_HEADS // NSA_KV
NSA_HD = (MIX - GLA_WIDTH) // NSA_HEADS
CMP_LEN = 32
CMP_STRIDE = 16
SLC_LEN = 64
N_SELECT = 16
WINDOW = 512
WIN_BLOCK = 128
SLC_QBLOCK_TOKENS = 128
D_FF = ((8 * D_MODEL // 3 + 127) // 128) * 128
CONV_W = 3
PLE_DIM = 256
EPS = 1e-6
NEG = -1e30
FORCED = 1e6
IN_SIZES = (GLA_HEADS * GLA_DK, GLA_HEADS * GLA_DK, GLA_WIDTH, GLA_WIDTH, GLA_RANK,
            NSA_HEADS * NSA_HD,
            NSA_KV * NSA_HD, NSA_KV * NSA_HD, NSA_KV * NSA_HD,
            NSA_KV * NSA_HD, NSA_KV * NSA_HD, NSA_KV * NSA_HD,
            3 * NSA_HEADS)
N_IN = sum(IN_SIZES)

kernel_name = "hymba_gla_nsa_convffn_ple_step"


def rmsnorm(x, g):
    xf = x.astype(jnp.float32)
    y = xf * lax.rsqrt(jnp.mean(xf * xf, axis=-1, keepdims=True) + EPS)
    return (y * g.astype(jnp.float32)).astype(x.dtype)


def masked_softmax(s, mask):
    s = jnp.where(mask, s.astype(jnp.float32), NEG)
    m = jnp.max(s, axis=-1, keepdims=True)
    e = jnp.where(mask, jnp.exp(s - m), 0.0)
    return e / jnp.maximum(jnp.sum(e, axis=-1, keepdims=True), 1e-30)


def gla_scan(q, k, v, log_a, s0):
    B, L, H, DK = q.shape
    DV = v.shape[-1]
    C = math.gcd(L, GLA_CHUNK)
    n = L // C

    def chunks(t):
        return t.astype(jnp.float32).reshape(B, n, C, H, t.shape[-1]).transpose(1, 0, 3, 2, 4)

    tril = jnp.tril(jnp.ones((C, C), dtype=bool))

    def step(S, inp):
        qc, kc, vc, gc = inp
        b = jnp.cumsum(gc, axis=2)
        qe = qc * jnp.exp(b)
        a = jnp.einsum('bhid,bhjd->bhij', qe, kc * jnp.exp(-b))
        a = jnp.where(tril, a, 0.0)
        o = jnp.einsum('bhij,bhjv->bhiv', a, vc) + jnp.einsum('bhid,bhdv->bhiv', qe, S)
        b_last = b[:, :, -1:, :]
        S = jnp.exp(b_last[:, :, 0, :])[..., None] * S + jnp.einsum(
            'bhjd,bhjv->bhdv', kc * jnp.exp(b_last - b), vc)
        return S, o

    S, o = lax.scan(step, s0.astype(jnp.float32),
                    (chunks(q) * (DK ** -0.5), chunks(k), chunks(v), chunks(log_a)))
    o = o.transpose(1, 0, 3, 2, 4).reshape(B, L, H, DV)
    return o, S


def compress(rows, w1, w2):
    B, T = rows.shape[:2]
    nseg = T // CMP_STRIDE
    seg = rows[:, :nseg * CMP_STRIDE].reshape(B, nseg, CMP_STRIDE, NSA_KV, NSA_HD).astype(jnp.float32)
    first = jnp.einsum('bnjgd,jde->bnge', seg, w1[:CMP_STRIDE])
    second = jnp.einsum('bnjgd,jde->bnge', seg, w1[CMP_STRIDE:])
    h = jax.nn.gelu(first[:, :-1] + second[:, 1:])
    return jnp.einsum('bnge,ef->bngf', h, w2)


def nsa_mixer(q, kc, vc, ks, vs, kw, vw, gates, past_ck, past_cv, past_sk, past_sv,
              win_k, win_v, start, keep, wk1, wk2, wv1, wv2):
    B, L = q.shape[:2]
    scale = NSA_HD ** -0.5
    qg = q.reshape(B, L, NSA_KV, NSA_REP, NSA_HD)
    kv = lambda t: t.reshape(B, L, NSA_KV, NSA_HD)
    kc, vc, ks, vs, kw, vw = kv(kc), kv(vc), kv(ks), kv(vs), kv(kw), kv(vw)
    qpos = start + jnp.arange(L)

    kc_all = jnp.concatenate([past_ck, kc], axis=1)
    vc_all = jnp.concatenate([past_cv, vc], axis=1)
    T = kc_all.shape[1]
    ck = compress(kc_all, wk1, wk2)
    cv = compress(vc_all, wv1, wv2)
    NB = ck.shape[1]
    cend = jnp.arange(NB) * CMP_STRIDE + CMP_LEN - 1
    s_c = jnp.einsum('blgrd,bngd->bgrln', qg, ck) * scale
    p_c = masked_softmax(s_c, cend[None, :] <= qpos[:, None])
    o_c = jnp.einsum('bgrln,bngd->blgrd', p_c, cv)

    ratio = SLC_LEN // CMP_STRIDE
    nslc = -(-T // SLC_LEN)
    pg = p_c.sum(axis=2)
    pad = jnp.pad(pg, ((0, 0), (0, 0), (0, 0), (1, ratio * nslc - NB)))
    pair = pad[..., :-1] + pad[..., 1:]
    imp = pair.reshape(B, NSA_KV, L, nslc, ratio).sum(-1)
    blk = jnp.arange(nslc)[None, :]
    cur = (qpos // SLC_LEN)[:, None]
    valid = blk <= cur
    forced = (blk == 0) | (blk == cur) | (blk == cur - 1)
    score = jnp.where(valid & forced, FORCED, jnp.where(valid, imp, -1.0))
    nsel = min(N_SELECT, nslc)
    _, idx = lax.top_k(score, nsel)

    ks_all = jnp.concatenate([past_sk, ks], axis=1)
    vs_all = jnp.concatenate([past_sv, vs], axis=1)
    padT = nslc * SLC_LEN - T
    to_blocks = lambda t: jnp.pad(t, ((0, 0), (0, padT), (0, 0), (0, 0))).reshape(
        B, nslc, SLC_LEN, NSA_KV, NSA_HD).transpose(0, 3, 1, 2, 4)
    kb, vb = to_blocks(ks_all), to_blocks(vs_all)
    QB = math.gcd(L, max(1, SLC_QBLOCK_TOKENS // B))
    nqb = L // QB
    q_blocks = qg.reshape(B, nqb, QB, NSA_KV, NSA_REP, NSA_HD).transpose(1, 0, 2, 3, 4, 5)
    idx_blocks = idx.reshape(B, NSA_KV, nqb, QB, nsel).transpose(2, 0, 1, 3, 4)
    pos_blocks = qpos.reshape(nqb, QB)
    bi = jnp.arange(B)[:, None, None, None]
    gi = jnp.arange(NSA_KV)[None, :, None, None]

    def sel_block(args):
        qb, ib, pb = args
        kg = kb[bi, gi, ib]
        vg = vb[bi, gi, ib]
        kpos = ib[..., None] * SLC_LEN + jnp.arange(SLC_LEN)
        mask = (kpos <= pb[None, None, :, None, None]).reshape(B, NSA_KV, 1, QB, nsel * SLC_LEN)
        s = jnp.einsum('bqgrd,bgqksd->bgrqks', qb, kg) * scale
        p = masked_softmax(s.reshape(B, NSA_KV, NSA_REP, QB, nsel * SLC_LEN), mask)
        p = p.reshape(B, NSA_KV, NSA_REP, QB, nsel, SLC_LEN)
        return jnp.einsum('bgrqks,bgqksd->bqgrd', p, vg)

    o_s = lax.map(sel_block, (q_blocks, idx_blocks, pos_blocks))
    o_s = o_s.transpose(1, 0, 2, 3, 4, 5).reshape(B, L, NSA_KV, NSA_REP, NSA_HD)

    WB = win_k.shape[1]
    kw_all = jnp.concatenate([win_k, kw], axis=1)
    vw_all = jnp.concatenate([win_v, vw], axis=1)
    kpos_all = start - WB + jnp.arange(WB + L)
    BQ = math.gcd(L, WIN_BLOCK)
    nwb = L // BQ
    key_idx = jnp.arange(nwb)[:, None] * BQ + jnp.arange(WB + BQ)[None, :]
    kwg, vwg = kw_all[:, key_idx], vw_all[:, key_idx]
    kp = kpos_all[key_idx][:, None, :]
    qp = qpos.reshape(nwb, BQ)[:, :, None]
    mask_w = (kp <= qp) & (kp > qp - WINDOW) & (kp >= 0)
    qw = qg.reshape(B, nwb, BQ, NSA_KV, NSA_REP, NSA_HD)
    s_w = jnp.einsum('bnqgrd,bnkgd->bgrnqk', qw, kwg) * scale
    p_w = masked_softmax(s_w, mask_w[None, None, None])
    o_w = jnp.einsum('bgrnqk,bnkgd->bnqgrd', p_w, vwg).reshape(B, L, NSA_KV, NSA_REP, NSA_HD)

    g = jax.nn.sigmoid(gates.astype(jnp.float32)).reshape(B, L, 3, NSA_KV, NSA_REP)[..., None]
    o = g[:, :, 0] * o_c + g[:, :, 1] * o_s + g[:, :, 2] * o_w
    o = o.reshape(B, L, NSA_HEADS * NSA_HD).astype(q.dtype)
    return o, (kc, vc, ks, vs, kw_all[:, -keep:], vw_all[:, -keep:])


def conv_ffn(xn, prefix, w_up, conv_w, conv_b, w_down):
    L = xn.shape[1]
    a, u = jnp.split(xn @ w_up, 2, axis=-1)
    a_all = jnp.concatenate([prefix.astype(a.dtype), a], axis=1)
    c = a_all[:, 0:L] * conv_w[0] + a_all[:, 1:L + 1] * conv_w[1] + a_all[:, 2:L + 2] * conv_w[2] + conv_b
    h = jax.nn.gelu(c) * u
    return h @ w_down, a_all[:, -(CONV_W - 1):]


def run_group(x, p, past_ck, past_cv, past_sk, past_sv, win_k, win_v, gla_s, conv_s, start, keep, params):
    (attn_norm_g, w_in, gla_w_a2, gla_b_a, gla_norm_g, cmp_wk1, cmp_wk2, cmp_wv1, cmp_wv2, w_out,
     ffn_norm_g, ffn_w_up, ffn_conv_w, ffn_conv_b, ffn_w_down, ple_w_proj, ple_w_gate, ple_b_gate,
     final_norm_g) = params
    B, L, _ = x.shape
    offsets = np.cumsum(IN_SIZES)[:-1].tolist()
    outs = [[] for _ in range(8)]
    h = x
    for i in range(DEPTH):
        xn = rmsnorm(h, attn_norm_g[i])
        proj = xn @ w_in[i]
        gq, gk, gv, gr, ga, nq, kc, vc, ks, vs, kw, vw, ng = jnp.split(proj, offsets, axis=-1)
        log_a = jax.nn.log_sigmoid((ga @ gla_w_a2[i] + gla_b_a[i]).astype(jnp.float32)) / GLA_TAU
        hd = lambda t, d: t.reshape(B, L, GLA_HEADS, d)
        o_gla, S = gla_scan(hd(gq, GLA_DK), hd(gk, GLA_DK), hd(gv, GLA_DV),
                            hd(log_a, GLA_DK), gla_s[i])
        o_gla = rmsnorm(o_gla, gla_norm_g[i]).reshape(B, L, GLA_WIDTH).astype(x.dtype)
        o_gla = o_gla * jax.nn.silu(gr)
        o_nsa, nsa_new = nsa_mixer(nq, kc, vc, ks, vs, kw, vw, ng, past_ck[i], past_cv[i],
                                   past_sk[i], past_sv[i], win_k[i], win_v[i], start, keep,
                                   cmp_wk1[i], cmp_wk2[i], cmp_wv1[i], cmp_wv2[i])
        h = h + jnp.concatenate([o_gla, o_nsa], axis=-1) @ w_out[i]
        f, conv_new = conv_ffn(rmsnorm(h, ffn_norm_g[i]), conv_s[i], ffn_w_up[i], ffn_conv_w[i],
                               ffn_conv_b[i], ffn_w_down[i])
        h = h + f
        h = h + jax.nn.sigmoid(h @ ple_w_gate[i] + ple_b_gate[i]) * (p[i] @ ple_w_proj[i])
        for j, t in enumerate(nsa_new + (S.astype(x.dtype), conv_new)):
            outs[j].append(t)
    y = rmsnorm(h, final_norm_g)
    return y, tuple(jnp.stack(o, axis=0) for o in outs)


def gather_pages(cache, page_table):
    g = cache[:, page_table]
    return g.reshape(g.shape[0], g.shape[1], g.shape[2] * g.shape[3], g.shape[4], g.shape[5])


def setup_inputs(seed: int = 0) -> dict:
    key = jax.random.key(seed)
    ks = iter(jax.random.split(key, 48))
    nrm = lambda shape, s: jax.random.normal(next(ks), shape, jnp.float32) * s
    gain = lambda shape: 1.0 + nrm(shape, 0.01)
    n_pages = PAST_LEN // PAGE_SIZE
    n_pool = (DEC_BATCH * n_pages * 5) // 4
    win_len = min(WINDOW, PAST_LEN)
    page_shape = (DEPTH, n_pool, PAGE_SIZE, NSA_KV, NSA_HD)
    d = {}
    d["x_prompt"] = nrm((BATCH, SEQ, D_MODEL), 1.0)
    d["x_sample"] = nrm((DEC_BATCH, DEC_SEQ, D_MODEL), 1.0)
    d["p_prompt"] = nrm((DEPTH, BATCH, SEQ, PLE_DIM), 1.0)
    d["p_sample"] = nrm((DEPTH, DEC_BATCH, DEC_SEQ, PLE_DIM), 1.0)
    d["cache_cmp_k"] = nrm(page_shape, 1.0)
    d["cache_cmp_v"] = nrm(page_shape, 1.0)
    d["cache_slc_k"] = nrm(page_shape, 1.0)
    d["cache_slc_v"] = nrm(page_shape, 1.0)
    perm = jax.random.permutation(next(ks), n_pool)[:DEC_BATCH * n_pages]
    d["page_table"] = perm.reshape(DEC_BATCH, n_pages).astype(jnp.int32)
    d["state_win_k"] = nrm((DEPTH, DEC_BATCH, win_len, NSA_KV, NSA_HD), 1.0)
    d["state_win_v"] = nrm((DEPTH, DEC_BATCH, win_len, NSA_KV, NSA_HD), 1.0)
    d["state_gla"] = nrm((DEPTH, DEC_BATCH, GLA_HEADS, GLA_DK, GLA_DV), 1.0)
    d["state_ffn_conv"] = nrm((DEPTH, DEC_BATCH, CONV_W - 1, D_FF), 1.0)
    d["attn_norm_g"] = gain((DEPTH, D_MODEL))
    d["w_in"] = nrm((DEPTH, D_MODEL, N_IN), D_MODEL ** -0.5)
    d["gla_w_a2"] = nrm((DEPTH, GLA_RANK, GLA_HEADS * GLA_DK), GLA_RANK ** -0.5)
    d["gla_b_a"] = nrm((DEPTH, GLA_HEADS * GLA_DK), 0.1)
    d["gla_norm_g"] = gain((DEPTH, GLA_DV))
    d["cmp_wk1"] = nrm((DEPTH, CMP_LEN, NSA_HD, NSA_HD), (CMP_LEN * NSA_HD) ** -0.5)
    d["cmp_wk2"] = nrm((DEPTH, NSA_HD, NSA_HD), NSA_HD ** -0.5)
    d["cmp_wv1"] = nrm((DEPTH, CMP_LEN, NSA_HD, NSA_HD), (CMP_LEN * NSA_HD) ** -0.5)
    d["cmp_wv2"] = nrm((DEPTH, NSA_HD, NSA_HD), NSA_HD ** -0.5)
    d["w_out"] = nrm((DEPTH, MIX, D_MODEL), MIX ** -0.5)
    d["ffn_norm_g"] = gain((DEPTH, D_MODEL))
    d["ffn_w_up"] = nrm((DEPTH, D_MODEL, 2 * D_FF), D_MODEL ** -0.5)
    d["ffn_conv_w"] = nrm((DEPTH, CONV_W, D_FF), CONV_W ** -0.5)
    d["ffn_conv_b"] = nrm((DEPTH, D_FF), 0.01)
    d["ffn_w_down"] = nrm((DEPTH, D_FF, D_MODEL), D_FF ** -0.5)
    d["ple_w_proj"] = nrm((DEPTH, PLE_DIM, D_MODEL), PLE_DIM ** -0.5)
    d["ple_w_gate"] = nrm((DEPTH, D_MODEL, D_MODEL), D_MODEL ** -0.5)
    d["ple_b_gate"] = nrm((DEPTH, D_MODEL), 0.01)
    d["final_norm_g"] = gain((D_MODEL,))
    return d


def reference(x_prompt, x_sample, p_prompt, p_sample, cache_cmp_k, cache_cmp_v, cache_slc_k,
              cache_slc_v, page_table, state_win_k, state_win_v, state_gla, state_ffn_conv,
              attn_norm_g, w_in, gla_w_a2, gla_b_a, gla_norm_g, cmp_wk1, cmp_wk2, cmp_wv1, cmp_wv2,
              w_out, ffn_norm_g, ffn_w_up, ffn_conv_w, ffn_conv_b, ffn_w_down, ple_w_proj,
              ple_w_gate, ple_b_gate, final_norm_g):
    params = (attn_norm_g, w_in, gla_w_a2, gla_b_a, gla_norm_g, cmp_wk1, cmp_wk2, cmp_wv1, cmp_wv2,
              w_out, ffn_norm_g, ffn_w_up, ffn_conv_w, ffn_conv_b, ffn_w_down, ple_w_proj,
              ple_w_gate, ple_b_gate, final_norm_g)
    Bp, Lp = x_prompt.shape[:2]
    dt = x_prompt.dtype
    empty = jnp.zeros((DEPTH, Bp, 0, NSA_KV, NSA_HD), dt)
    win0 = jnp.zeros((DEPTH, Bp, WINDOW, NSA_KV, NSA_HD), dt)
    gla0 = jnp.zeros((DEPTH, Bp, GLA_HEADS, GLA_DK, GLA_DV), jnp.float32)
    conv0 = jnp.zeros((DEPTH, Bp, CONV_W - 1, D_FF), dt)
    y_prompt, (ck_p, cv_p, sk_p, sv_p, wk_p, wv_p, gla_p, conv_p) = run_group(
        x_prompt, p_prompt, empty, empty, empty, empty, win0, win0, gla0, conv0,
        0, min(WINDOW, Lp), params)

    past_ck = gather_pages(cache_cmp_k, page_table)
    past_cv = gather_pages(cache_cmp_v, page_table)
    past_sk = gather_pages(cache_slc_k, page_table)
    past_sv = gather_pages(cache_slc_v, page_table)
    y_sample, (ck_s, cv_s, sk_s, sv_s, wk_s, wv_s, gla_s, conv_s) = run_group(
        x_sample, p_sample, past_ck, past_cv, past_sk, past_sv, state_win_k, state_win_v,
        state_gla, state_ffn_conv, past_ck.shape[2], state_win_k.shape[2], params)
    return (y_prompt, y_sample, ck_p, cv_p, sk_p, sv_p, wk_p, wv_p, gla_p, conv_p,
            ck_s, cv_s, sk_s, sv_s, wk_s, wv_s, gla_s, conv_s)
```

```python
import numpy as np
import concourse.bass as bass
import concourse.mybir as mybir
from concourse.bass_utils import run_bass_kernel_spmd
from contextlib import ExitStack

F32 = mybir.dt.float32
BF16 = mybir.dt.bfloat16
I32 = mybir.dt.int32
AF = mybir.ActivationFunctionType
ALU = mybir.AluOpType
AX = mybir.AxisListType

ENGS = ("pe", "act", "dve", "pool", "sp")
NDMA = 12

D = 2048
SEQ = 2048
NS = 64
NT = SEQ + NS
NB_S = 16
DFF = 5504
NFF = 43
OFF = dict(gq=0, gk=512, gv=1024, gr=2048, ga=3072, nq=3088, kc=4112, vc=4368,
           ks=4624, vs=4880, kw=5136, vw=5392, ng=5648)
NIN = 5672
EPS = 1e-6
SCALE = 128 ** -0.5
NTILES = [(0, 512), (512, 512), (1024, 512), (1536, 512), (2048, 64)]


class Res:
    __slots__ = ("w", "r", "name")

    def __init__(self, name=""):
        self.w = None
        self.r = []
        self.name = name


class Ctx:
    def __init__(self, nc):
        self.nc = nc
        self.streams = {e: [] for e in ENGS}
        self.count = {e: 0 for e in ENGS}
        self.waited = {e: {} for e in ENGS}
        self.dma_cnt = {}
        self.dma_rr = {e: 0 for e in ENGS}
        self.last_dma = {}

    def _need(self, eng, tok, waits):
        if tok is None:
            return
        key, val = tok
        if self.waited[eng].get(key, 0) >= val:
            return
        self.waited[eng][key] = val
        waits.append((key, val))

    def _deps(self, eng, reads, writes, is_dma):
        waits = []
        for r in reads:
            self._need(eng, r.w, waits)
        for w in writes:
            if w.w is not None and (is_dma or w.w[0] != eng):
                self._need(eng, w.w, waits)
            for t in w.r:
                if is_dma or t[0] != eng:
                    self._need(eng, t, waits)
        return waits

    def _commit(self, tok, reads, writes):
        for r in reads:
            if len(r.r) > 64:
                r.r = r.r[-48:]
            r.r.append(tok)
        for w in writes:
            w.w = tok
            w.r = []

    def op(self, eng, fns, reads=(), writes=()):
        if not isinstance(fns, (list, tuple)):
            fns = [fns]
        waits = self._deps(eng, reads, writes, False)
        self.count[eng] += 1
        tok = (eng, self.count[eng])
        self.streams[eng].append(("op", waits, list(fns), tok))
        self._commit(tok, reads, writes)
        return tok

    def dma(self, eng, fn, reads=(), writes=()):
        waits = self._deps(eng, reads, writes, True)
        i = self.dma_rr[eng]
        self.dma_rr[eng] = (i + 1) % NDMA
        key = ("dma", eng, i)
        n = self.dma_cnt.get(key, 0)
        if n > 0:
            self._need(eng, (key, 16 * n), waits)
        self.dma_cnt[key] = n + 1
        tok = (key, 16 * (n + 1))
        self.last_dma[key] = tok
        self.streams[eng].append(("dma", waits, [fn], tok))
        self._commit(tok, reads, writes)
        return tok

    def barrier(self):
        toks = [(e, self.count[e]) for e in ENGS if self.count[e] > 0]
        toks += list(self.last_dma.values())
        for e in ENGS:
            waits = []
            for t in toks:
                if t[0] != e:
                    self._need(e, t, waits)
            if waits:
                self.streams[e].append(("wait", waits, [], None))

    def emit(self):
        nc = self.nc
        flagged = {e: set() for e in ENGS}
        for e in ENGS:
            for kind, waits, fns, tok in self.streams[e]:
                for key, val in waits:
                    if key in flagged:
                        flagged[key].add(val)
        remap = {}
        for e in ENGS:
            n = 0
            for kind, waits, fns, tok in self.streams[e]:
                if kind == "op":
                    if tok[1] in flagged[e] or tok[1] == self.count[e]:
                        n += 1
                        remap[tok] = n
        self.n_inc = {e: sum(1 for t in remap if t[0] == e) for e in ENGS}
        with ExitStack() as es:
            sems = {}
            for e in ENGS:
                sems[e] = es.enter_context(nc.semaphore("s_" + e))
            for key in self.dma_cnt:
                sems[key] = es.enter_context(nc.semaphore("d_%s_%d" % (key[1], key[2])))
            block = es.enter_context(nc.Block())

            def run(eng_key):
                def body(eng):
                    for kind, waits, fns, tok in self.streams[eng_key]:
                        for key, val in waits:
                            if key in flagged:
                                val = remap[(key, val)]
                            eng.wait_ge(sems[key], val)
                        inst = None
                        for f in fns:
                            inst = f(eng)
                        if kind == "op":
                            if tok in remap:
                                inst.then_inc(sems[eng_key], 1)
                        elif kind == "dma":
                            inst.then_inc(sems[tok[0]], 16)
                return body

            block.tensor(run("pe"))
            block.scalar(run("act"))
            block.vector(run("dve"))
            block.gpsimd(run("pool"))
            block.sync(run("sp"))


class Arena:
    def __init__(self, t, nwords):
        self.t = t
        self.n = nwords
        self.top = 0

    def alloc(self, free_shape, dt, parts=128):
        n = 1
        for s in free_shape:
            n *= s
        esz = 4 if dt in (F32, I32) else 2
        words = (n * esz + 3) // 4
        words = (words + 7) // 8 * 8
        off = self.top
        self.top += words
        assert self.top <= self.n, "SBUF arena overflow: %d > %d words" % (self.top, self.n)
        ap = self.t[0:parts, off:off + words]
        if dt != F32:
            ap = ap.bitcast(dt)
        ap = ap[:, 0:n]
        if len(free_shape) == 2:
            ap = ap.rearrange("p (a b) -> p a b", a=free_shape[0])
        elif len(free_shape) == 3:
            ap = ap.rearrange("p (a b c) -> p a b c", a=free_shape[0], b=free_shape[1])
        elif len(free_shape) == 4:
            ap = ap.rearrange("p (a b c d) -> p a b c d", a=free_shape[0], b=free_shape[1], c=free_shape[2])
        return ap


def make_consts():
    k = np.arange(128)
    cb = {}
    cf = {}
    cb["ident"] = np.eye(128)
    cb["ones"] = np.ones((128, 128))
    cb["tri"] = (k[:, None] <= k[None, :]).astype(np.float64)
    q = np.arange(512)
    caus = np.zeros((128, 4, 512))
    for r in range(4):
        caus[:, r, :] = ((128 * r + k)[:, None] <= q[None, :])
    cb["caus"] = caus.reshape(128, -1)
    wm = np.zeros((128, 8, 512))
    for r in range(8):
        key = 128 * (r - 4) + k
        wm[:, r, :] = (key[:, None] <= q[None, :]) & (key[:, None] > q[None, :] - 512)
    cb["wmask"] = wm.reshape(128, -1)
    ex = np.zeros((128, 16, 128))
    for kt in range(16):
        for kk in range(128):
            ex[2 * kt + kk // 64, kt, kk] = 1
    cb["expand"] = ex.reshape(128, -1)
    p = np.arange(128)
    exs = np.zeros((128, 128))
    blk_of_p = 2 * (p % 16) + (p // 16) // 4
    exs[blk_of_p, p] = 1
    cb["expand_s"] = exs
    oh = np.zeros((128, 24, 128))
    for r in range(24):
        oh[r, r, :] = 1
    cb["onehot"] = oh.reshape(128, -1)
    n = np.arange(127)

    def pairing(nblk):
        m = np.zeros((128, nblk + 1))
        for j in range(nblk):
            m[:127, j] = ((4 * j <= n + 1) & (n + 1 <= 4 * j + 3)).astype(float) + \
                         ((4 * j <= n) & (n <= 4 * j + 3)).astype(float)
        m[:127, nblk] = 1
        return m
    cb["mp"] = pairing(32)
    cb["ms"] = pairing(33)
    nm = np.zeros((128, 16, 4))
    for bb in range(16):
        for l2 in range(4):
            for l in range(4):
                if l2 <= l:
                    nm[bb * 4 + l2, bb, l] = 1
    cb["newmask"] = nm.reshape(128, -1)
    wms = np.zeros((128, 4, 4))
    for blk in range(4):
        for l in range(4):
            wms[:, blk, l] = (128 * blk + k >= l + 1)
    cb["winmask_s"] = wms.reshape(128, -1)
    cm = np.zeros((128, 2048))
    for nn in range(127):
        cm[nn, :] = (16 * nn + 31 <= np.arange(2048))
    cb["cmask"] = cm
    tri_s = np.zeros((128, 128))
    for j in range(64):
        for i in range(64):
            if j // 4 == i // 4 and j <= i:
                tri_s[j, i] = 1
    cb["tri_s"] = tri_s
    bm = np.zeros((128, 16))
    for j in range(64):
        bm[j, j // 4] = 1
    cf["bm_s"] = bm
    cf["ident"] = np.eye(128)
    cf["ones"] = np.ones((128, 128))
    cf["tri"] = cb["tri"]
    cf["tri_s"] = tri_s
    A = np.zeros((128, 16, 32))
    B = np.zeros((128, 16, 32))
    for s in range(16):
        qq = s * 128 + k
        cur = qq // 64
        blk = np.arange(32)[None, :]
        valid = blk <= cur[:, None]
        forced = (blk == 0) | (blk == cur[:, None]) | (blk == cur[:, None] - 1)
        A[:, s, :] = valid & ~forced
        B[:, s, :] = np.where(valid & forced, 1e6, np.where(valid, 0.0, -1.0))
    cf["A_p"] = A.reshape(128, -1)
    cf["B_p"] = B.reshape(128, -1)
    As = np.ones((128, 33))
    Bs = np.zeros((128, 33))
    for j in (0, 31, 32):
        As[:, j] = 0
        Bs[:, j] = 1e6
    cf["A_s"] = As
    cf["B_s"] = Bs
    cf["cvec"] = (p // 16).astype(np.float64)[:, None]
    lay_b, lay_f = {}, {}
    o = 0
    for kk, v in cb.items():
        lay_b[kk] = (o, v.shape[1])
        o += v.shape[1]
    nb = o
    o = 0
    for kk, v in cf.items():
        lay_f[kk] = (o, v.shape[1])
        o += v.shape[1]
    nf = o
    cbm = np.concatenate([v for v in cb.values()], axis=1).astype(np.float32)
    cfm = np.concatenate([v for v in cf.values()], axis=1).astype(np.float32)
    return cbm, cfm, lay_b, lay_f, nb, nf


CONSTS = make_consts()


def build(stop_after=99):
    nc = bass.Bass("TRN2", target_bir_lowering=False)
    cbm, cfm, lay_b, lay_f, nb, nf = CONSTS

    def din(name, shape, dt=F32):
        return nc.dram_tensor(name, list(shape), dt, kind="ExternalInput").ap()

    def dout(name, shape, dt=F32):
        return nc.dram_tensor(name, list(shape), dt, kind="ExternalOutput").ap()

    xp = din("xp", [SEQ, D]); xs = din("xs", [NS, D])
    swk = din("swk", [16, 512, 256]); swv = din("swv", [16, 512, 256])
    sgla = din("sgla", [16, 4, 128, 256])
    attn_g = din("attn_g", [D]); w_in = din("w_in", [D, NIN])
    wa2 = din("wa2", [16, 512]); b_a = din("b_a", [512]); glag = din("glag", [256])
    cck = din("cck", [2560 * 8, 4096]); ccv = din("ccv", [2560 * 8, 4096])
    csk = din("csk", [2560 * 8, 4096]); csv = din("csv", [2560 * 8, 4096])
    pt = din("pt", [16, 16], I32)
    wk1 = din("wk1", [32, 128, 128]); wk2 = din("wk2", [128, 128])
    wv1 = din("wv1", [32, 128, 128]); wv2 = din("wv2", [128, 128])
    pp = din("pp", [SEQ, 256]); ps_ = din("ps", [NS, 256]); sconv = din("sconv", [32, DFF])
    w_out = din("w_out", [D, D]); ffn_g = din("ffn_g", [D]); w_up = din("w_up", [D, 2 * DFF])
    convw = din("convw", [3, DFF]); convb = din("convb", [DFF]); w_dn = din("w_dn", [DFF, D])
    plep = din("plep", [256, D]); pleg = din("pleg", [D, D]); pleb = din("pleb", [D])
    fin_g = din("fin_g", [D])
    cb_d = din("cb", [128, nb]); cf_d = din("cf", [128, nf])

    y_p = dout("y_p", [SEQ, D]); y_s = dout("y_s", [NS, D])
    kvp = [dout(n, [SEQ, 256]) for n in ("ckp", "cvp", "skp", "svp")]
    wkp = dout("wkp", [512, 256]); wvp = dout("wvp", [512, 256])
    glap = dout("glap", [4, 128, 256]); convp = dout("convp", [2, DFF])
    kvs = [dout(n, [NS, 256]) for n in ("cks", "cvs", "sks", "svs")]
    wks = dout("wks", [16, 512, 256]); wvs = dout("wvs", [16, 512, 256])
    glas = dout("glas", [16, 4, 128, 256]); convs = dout("convs", [16, 2, DFF])
    oscr = nc.dram_tensor("oscr", [D, NT], BF16, kind="Internal").ap()

    c = Ctx(nc)
    out_toks = []
    es = ExitStack()
    TOTAL_WORDS = 52000
    arena_t = es.enter_context(nc.sbuf_tensor("arena", [128, TOTAL_WORDS], F32))
    ar = Arena(arena_t, TOTAL_WORDS)
    banks = [es.enter_context(nc.psum_tensor("bank%d" % i, [128, 512], F32)) for i in range(8)]
    RB = [Res("bank%d" % i) for i in range(8)]

    def bk(i):
        return banks[i][:]

    def bkb(i):
        return banks[i][:].bitcast(BF16)

    cbt = ar.alloc([nb], BF16); R_cb = Res("cb")
    cft = ar.alloc([nf], F32); R_cf = Res("cf")
    c.dma("pool", lambda e: e.dma_start(out=cbt, in_=cb_d[:, :]), writes=[R_cb])
    c.dma("sp", lambda e: e.dma_start(out=cft, in_=cf_d[:, :]), writes=[R_cf])

    def CB(name, parts=128):
        o, n = lay_b[name]
        return cbt[0:parts, o:o + n]

    def CF(name, parts=128):
        o, n = lay_f[name]
        return cft[0:parts, o:o + n]

    ident_b = CB("ident"); ones_b = CB("ones")
    ident_f = CF("ident"); ones_f = CF("ones")

    def mms(out, pairs, reads, writes):
        n = len(pairs)
        fns = [(lambda e, i=i, l=l, r=r: e.matmul(out, lhsT=l, rhs=r, start=(i == 0), stop=(i == n - 1)))
               for i, (l, r) in enumerate(pairs)]
        return c.op("pe", fns, reads, writes)

    def tposes(items, reads, writes):
        fns = [(lambda e, o=o, i=i, d=d: e.transpose(out=o, in_=i, identity=d)) for (o, i, d) in items]
        return c.op("pe", fns, reads, writes)

    def wload(eng, dst, src, Rw):
        return c.dma(eng, lambda e: e.dma_start(out=dst, in_=src), writes=[Rw])

    mark_persist = ar.top

    def finish():
        waits = []
        for t in out_toks:
            c._need("sp", t, waits)
        c.streams["sp"].append(("wait", waits, [], None))
        c.barrier()
        c.emit()
        es.close()
        return nc

    xnT = ar.alloc([16, NT], BF16)
    R_xnT = [Res("xnT%d" % i) for i in range(17)]
    mark_p2 = ar.top
    gbA = ar.alloc([D], F32); R_gbA = Res()
    c.dma("sp", lambda e: e.dma_start(out=gbA, in_=attn_g.partition_broadcast(128)), writes=[R_gbA])
    xst = [ar.alloc([D], F32) for _ in range(2)]; R_xst = [Res() for _ in range(2)]
    xnb = [ar.alloc([D], BF16) for _ in range(2)]; R_xnb = [Res() for _ in range(2)]
    junk = ar.alloc([D], BF16); R_junk = Res()
    ssq = [ar.alloc([1], F32) for _ in range(2)]; R_ssq = [Res() for _ in range(2)]
    rsd = [ar.alloc([1], F32) for _ in range(2)]; R_rsd = [Res() for _ in range(2)]

    def rms_tile(tt, src_rows, rows):
        s = tt % 2
        c.dma("sp", lambda e: e.dma_start(out=xst[s][0:rows], in_=src_rows), writes=[R_xst[s]])
        c.op("pool", lambda e: e.memset(ssq[s][0:rows], 0.0), writes=[R_ssq[s]])
        c.op("act", lambda e: e.activation(out=junk[0:rows], in_=xst[s][0:rows], func=AF.Square,
                                           accum_out=ssq[s][0:rows]),
             reads=[R_xst[s]], writes=[R_junk, R_ssq[s]])
        c.op("act", lambda e: e.activation(out=rsd[s][0:rows], in_=ssq[s][0:rows], func=AF.Sqrt,
                                           scale=1.0 / D, bias=EPS),
             reads=[R_ssq[s]], writes=[R_rsd[s]])
        c.op("dve", lambda e: e.reciprocal(out=rsd[s][0:rows], in_=rsd[s][0:rows]),
             reads=[R_rsd[s]], writes=[R_rsd[s]])
        c.op("dve", lambda e: e.scalar_tensor_tensor(out=xnb[s][0:rows], in0=xst[s][0:rows],
                                                     scalar=rsd[s][0:rows, 0:1], in1=gbA[0:rows],
                                                     op0=ALU.mult, op1=ALU.mult),
             reads=[R_xst[s], R_rsd[s], R_gbA], writes=[R_xnb[s]])
        b0 = 2 * s
        pv = [bkb(b0).rearrange("p (a b) -> p a b", a=8), bkb(b0 + 1).rearrange("p (a b) -> p a b", a=8)]
        items = [(pv[k // 8][:, k % 8, 0:rows], xnb[s][0:rows, k * 128:(k + 1) * 128], ident_b[0:rows, 0:rows])
                 for k in range(16)]
        tposes(items, [R_xnb[s], R_cb], [RB[b0], RB[b0 + 1]])
        for hh in range(2):
            eng = "act" if hh == 0 else "dve"
            if eng == "act":
                c.op("act", lambda e, hh=hh: e.copy(out=xnT[:, 8 * hh:8 * hh + 8, tt * 128:tt * 128 + rows],
                                                    in_=pv[hh][:, :, 0:rows]),
                     reads=[RB[b0 + hh]], writes=[R_xnT[tt]])
            else:
                c.op("dve", lambda e, hh=hh: e.tensor_copy(out=xnT[:, 8 * hh:8 * hh + 8, tt * 128:tt * 128 + rows],
                                                           in_=pv[hh][:, :, 0:rows]),
                     reads=[RB[b0 + hh]], writes=[R_xnT[tt]])

    for tt in range(16):
        rms_tile(tt, xp[tt * 128:(tt + 1) * 128, :], 128)
    rms_tile(16, xs[:, :], 64)

    c.barrier()
    ar.top = mark_p2
    wq = [ar.alloc([16, 128], BF16) for _ in range(2)]; R_wq = [Res(), Res()]
    wqi = [0]

    def proj_fm(col0, ncols, dests, Rd, evac):
        sl = wqi[0] % 2
        wqi[0] += 1
        for k4 in range(4):
            wload("pool", wq[sl][:, 4 * k4:4 * k4 + 4, 0:ncols],
                  w_in[512 * k4:512 * k4 + 512, col0:col0 + ncols].rearrange("(kt p) n -> p kt n", p=128), R_wq[sl])
        for ni, (n0, N) in enumerate(NTILES):
            b = 6 + (ni % 2)
            tts = sorted(set([n0 // 128 + i for i in range((N + 127) // 128)]))
            mms(bk(b)[0:ncols, 0:N], [(wq[sl][:, kt, 0:ncols], xnT[:, kt, n0:n0 + N]) for kt in range(16)],
                [R_wq[sl]] + [R_xnT[t] for t in tts], [RB[b]])
            evac(dests(n0, N), bk(b)[0:ncols, 0:N], [RB[b]], Rd)

    def ev_copy(out, pin, rb, Rd):
        c.op("act", lambda e: e.copy(out=out, in_=pin), reads=[], writes=rb + [Rd])

    def ev_silu(out, pin, rb, Rd):
        c.op("act", lambda e: e.activation(out=out, in_=pin, func=AF.Silu), reads=[], writes=rb + [Rd])

    gaT = ar.alloc([NT], F32); R_gaT = Res()
    wa2_sb = ar.alloc([512], F32); ba_sb = ar.alloc([512], F32); glag_sb = ar.alloc([2], F32); R_gw = Res()
    c.dma("sp", lambda e: e.dma_start(out=wa2_sb[0:16], in_=wa2[:, :]), writes=[R_gw])
    c.dma("sp", lambda e: e.dma_start(out=ba_sb[0:1], in_=b_a.rearrange("(o n) -> o n", o=1)), writes=[R_gw])
    c.dma("sp", lambda e: e.dma_start(out=glag_sb, in_=glag.rearrange("(t p) -> p t", p=128), allow_slow_non_contiguous=True), writes=[R_gw])
    proj_fm(OFF["ga"], 16, lambda n0, N: gaT[0:16, n0:n0 + N], R_gaT, ev_copy)

    qT = ar.alloc([NT], BF16); kT = ar.alloc([NT], BF16); vT = ar.alloc([2, NT], BF16); srT = ar.alloc([2, NT], BF16)
    R_q, R_k, R_v, R_sr = Res(), Res(), Res(), Res()
    ost = ar.alloc([2, NT], BF16); R_ost = Res()
    Sst = ar.alloc([256], F32); R_S = Res(); Sbf = ar.alloc([256], BF16); R_Sbf = Res()
    Sall = ar.alloc([16, 256], F32); R_Sall = Res(); Sallb = ar.alloc([16, 256], BF16); R_Sallb = Res()
    e1 = ar.alloc([128], F32); spt = ar.alloc([128], F32); R_e1, R_sp = Res(), Res()
    BTs = ar.alloc([128], F32); R_BTs = Res()
    eb = ar.alloc([128], F32); einv = ar.alloc([128], F32); ekl = ar.alloc([128], F32); nbc = ar.alloc([1], F32)
    R_eb, R_einv, R_ekl, R_nb = Res(), Res(), Res(), Res()
    qe = ar.alloc([128], BF16); ke = ar.alloc([128], BF16); kl = ar.alloc([128], BF16)
    R_qe, R_ke, R_kl = Res(), Res(), Res()
    kltok = ar.alloc([128], BF16); vtk = ar.alloc([256], BF16); R_kltok, R_vtk = Res(), Res()
    ATs = ar.alloc([128], BF16); R_ATs = Res()
    sqo = ar.alloc([2, 128], F32); R_sqo = Res()
    sdo = ar.alloc([128], F32); R_sdo = Res()
    tmpo = ar.alloc([128], F32); R_tmpo = Res()
    klm = ar.alloc([16, 128], BF16); R_klm = Res()
    snew = [ar.alloc([256], F32) for _ in range(2)]; R_snew = [Res(), Res()]
    tri_f = CF("tri"); tri_b = CB("tri"); tris_f = CF("tri_s"); tris_b = CB("tri_s"); bm_s = CF("bm_s")

    c.op("pool", lambda e: e.memset(ost, 0.0), writes=[R_ost])
    for r8 in range(4):
        for t in range(2):
            c.dma("pool", lambda e, r8=r8, t=t: e.dma_start(
                out=oscr[1024 + r8 * 256 + t * 128:1024 + r8 * 256 + (t + 1) * 128, :], in_=ost[:, t, :]), reads=[R_ost])
    for h in range(4):
        proj_fm(OFF["gq"] + h * 128, 128, lambda n0, N: qT[:, n0:n0 + N], R_q, ev_copy)
        proj_fm(OFF["gk"] + h * 128, 128, lambda n0, N: kT[:, n0:n0 + N], R_k, ev_copy)
        for t in range(2):
            proj_fm(OFF["gv"] + h * 256 + t * 128, 128, lambda n0, N, t=t: vT[:, t, n0:n0 + N], R_v, ev_copy)
            proj_fm(OFF["gr"] + h * 256 + t * 128, 128, lambda n0, N, t=t: srT[:, t, n0:n0 + N], R_sr, ev_silu)
        c.op("pool", lambda e: e.memset(Sst, 0.0), writes=[R_S])
        c.op("pool", lambda e: e.memset(Sbf, 0.0), writes=[R_Sbf])
        c.dma("sp", lambda e, h=h: e.dma_start(out=Sall, in_=sgla[:, h, :, :].rearrange("b d v -> d b v")),
              writes=[R_Sall])
        c.op("act", lambda e: e.copy(out=Sallb, in_=Sall), reads=[R_Sall], writes=[R_Sallb])
        for ci in range(17):
            smp = (ci == 16)
            T = 64 if smp else 128
            c0 = ci * 128
            cols = slice(c0, c0 + T)
            mms(bk(0)[0:T, 0:128], [(gaT[0:16, cols], wa2_sb[0:16, h * 128:(h + 1) * 128]),
                                    (ones_f[0:1, 0:T], ba_sb[0:1, h * 128:(h + 1) * 128])],
                [R_gaT, R_gw, R_cf], [RB[0]])
            c.op("act", lambda e, T=T: e.activation(out=e1[0:T], in_=bk(0)[0:T, 0:128], func=AF.Exp, scale=-1.0),
                 writes=[RB[0], R_e1])
            c.op("act", lambda e, T=T: e.activation(out=spt[0:T], in_=e1[0:T], func=AF.Ln, scale=1.0, bias=1.0),
                 reads=[R_e1], writes=[R_sp])
            trf = tris_f if smp else tri_f
            mms(bk(1)[:, 0:T], [(spt[0:T, :], trf[0:T, 0:T])], [R_sp, R_cf], [RB[1]])
            c.op("act", lambda e, T=T: e.copy(out=BTs[:, 0:T], in_=bk(1)[:, 0:T]), writes=[RB[1], R_BTs])
            c.op("act", lambda e, T=T: e.activation(out=eb[:, 0:T], in_=BTs[:, 0:T], func=AF.Exp, scale=-1.0 / 16),
                 reads=[R_BTs], writes=[R_eb])
            c.op("act", lambda e, T=T: e.activation(out=einv[:, 0:T], in_=BTs[:, 0:T], func=AF.Exp, scale=1.0 / 16),
                 reads=[R_BTs], writes=[R_einv])
            if not smp:
                c.op("dve", lambda e, T=T: e.tensor_scalar(out=nbc, in0=BTs[:, T - 1:T], scalar1=-1.0 / 16,
                                                           scalar2=None, op0=ALU.mult),
                     reads=[R_BTs], writes=[R_nb])
                c.op("act", lambda e, T=T: e.activation(out=ekl[:, 0:T], in_=BTs[:, 0:T], func=AF.Exp,
                                                        scale=1.0 / 16, bias=nbc[:, 0:1]),
                     reads=[R_BTs, R_nb], writes=[R_ekl])
            else:
                b3 = BTs[:, 0:64].rearrange("p (b l) -> p b l", l=4)
                c.op("dve", lambda e, b3=b3: e.tensor_tensor(
                    out=ekl[:, 0:64].rearrange("p (b l) -> p b l", l=4), in0=b3,
                    in1=b3[:, :, 3:4].broadcast_to([128, 16, 4]), op=ALU.subtract),
                    reads=[R_BTs], writes=[R_ekl])
                c.op("act", lambda e: e.activation(out=ekl[:, 0:64], in_=ekl[:, 0:64], func=AF.Exp, scale=1.0 / 16),
                     reads=[R_ekl], writes=[R_ekl])
            c.op("dve", lambda e, T=T, cols=cols: e.scalar_tensor_tensor(
                out=qe[:, 0:T], in0=qT[:, cols], scalar=SCALE, in1=eb[:, 0:T], op0=ALU.mult, op1=ALU.mult),
                reads=[R_q, R_eb], writes=[R_qe])
            c.op("dve", lambda e, T=T, cols=cols: e.tensor_tensor(out=ke[:, 0:T], in0=kT[:, cols], in1=einv[:, 0:T],
                                                                  op=ALU.mult),
                 reads=[R_k, R_einv], writes=[R_ke])
            c.op("dve", lambda e, T=T, cols=cols: e.tensor_tensor(out=kl[:, 0:T], in0=kT[:, cols], in1=ekl[:, 0:T],
                                                                  op=ALU.mult),
                 reads=[R_k, R_ekl], writes=[R_kl])
            pb2 = bkb(2)
            tposes([(pb2[0:T, 0:128], kl[:, 0:T], ident_b),
                    (pb2[0:T, 128:256], vT[:, 0, cols], ident_b),
                    (pb2[0:T, 256:384], vT[:, 1, cols], ident_b)], [R_kl, R_v, R_cb], [RB[2]])
            c.op("act", lambda e, T=T: e.copy(out=kltok[0:T], in_=pb2[0:T, 0:128]), writes=[RB[2], R_kltok])
            c.op("act", lambda e, T=T: e.copy(out=vtk[0:T], in_=pb2[0:T, 128:384]), writes=[RB[2], R_vtk])
            mms(bk(3)[0:T, 0:T], [(ke[:, 0:T], qe[:, 0:T])], [R_ke, R_qe], [RB[3]])
            trb = tris_b if smp else tri_b
            c.op("dve", lambda e, T=T, trb=trb: e.tensor_tensor(out=ATs[0:T, 0:T], in0=bk(3)[0:T, 0:T],
                                                                in1=trb[0:T, 0:T], op=ALU.mult),
                 reads=[R_cb], writes=[RB[3], R_ATs])
            for t in range(2):
                pairs = [(vtk[0:T, t * 128:(t + 1) * 128], ATs[0:T, 0:T])]
                if not smp:
                    pairs.append((Sbf[:, t * 128:(t + 1) * 128], qe[:, 0:T]))
                    mms(bk(4)[:, t * 128:t * 128 + T], pairs, [R_vtk, R_ATs, R_Sbf, R_qe], [RB[4]])
                else:
                    fns = [lambda e, t=t: e.matmul(bk(4)[:, t * 128:t * 128 + 64], lhsT=vtk[0:64, t * 128:(t + 1) * 128],
                                                   rhs=ATs[0:64, 0:64], start=True, stop=False)]
                    for bb in range(16):
                        fns.append(lambda e, t=t, bb=bb: e.matmul(
                            bk(4)[:, t * 128 + 4 * bb:t * 128 + 4 * bb + 4], lhsT=Sallb[:, bb, t * 128:(t + 1) * 128],
                            rhs=qe[:, 4 * bb:4 * bb + 4], start=False, stop=(bb == 15)))
                    c.op("pe", fns, [R_vtk, R_ATs, R_Sallb, R_qe], [RB[4]])
            if not smp:
                mms(bk(5)[:, 0:256], [(kltok[0:T, :], vtk[0:T, :])], [R_kltok, R_vtk], [RB[5]])
                c.op("dve", lambda e, T=T: e.scalar_tensor_tensor(out=Sst, in0=Sst, scalar=eb[:, T - 1:T],
                                                                  in1=bk(5)[:, 0:256], op0=ALU.mult, op1=ALU.add),
                     reads=[R_eb], writes=[RB[5], R_S])
                c.op("act", lambda e: e.copy(out=Sbf, in_=Sst), reads=[R_S], writes=[R_Sbf])
            else:
                for bb in range(16):
                    c.op("dve", lambda e, bb=bb: e.tensor_scalar(out=klm[0:64, bb, :], in0=kltok[0:64, :],
                                                                 scalar1=bm_s[0:64, bb:bb + 1], scalar2=None,
                                                                 op0=ALU.mult),
                         reads=[R_kltok, R_cf], writes=[R_klm])
                for bb in range(16):
                    pbk = 5 + (bb % 2) * 2
                    sn = bb % 2
                    mms(bk(pbk)[:, 0:256], [(klm[0:64, bb, :], vtk[0:64, :])], [R_klm, R_vtk], [RB[pbk]])
                    c.op("dve", lambda e, bb=bb, pbk=pbk, sn=sn: e.scalar_tensor_tensor(
                        out=snew[sn], in0=Sall[:, bb, :], scalar=eb[:, 4 * bb + 3:4 * bb + 4], in1=bk(pbk)[:, 0:256],
                        op0=ALU.mult, op1=ALU.add),
                        reads=[R_Sall, R_eb], writes=[RB[pbk], R_snew[sn]])
                    out_toks.append(c.dma("pool", lambda e, bb=bb, sn=sn, h=h: e.dma_start(out=glas[bb, h, :, :],
                                                                                          in_=snew[sn]),
                                          reads=[R_snew[sn]]))
            c.op("act", lambda e, T=T: e.activation(
                out=sqo[:, :, 0:T], in_=bk(4)[:, 0:256].rearrange("p (t i) -> p t i", t=2)[:, :, 0:T], func=AF.Square),
                writes=[RB[4], R_sqo])
            mms(bk(6)[:, 0:T], [(ones_f, sqo[:, 0, 0:T]), (ones_f, sqo[:, 1, 0:T])], [R_sqo, R_cf], [RB[6]])
            c.op("act", lambda e, T=T: e.activation(out=sdo[:, 0:T], in_=bk(6)[:, 0:T], func=AF.Sqrt,
                                                    scale=1.0 / 256, bias=EPS),
                 writes=[RB[6], R_sdo])
            c.op("dve", lambda e, T=T: e.reciprocal(out=sdo[:, 0:T], in_=sdo[:, 0:T]), reads=[R_sdo], writes=[R_sdo])
            for t in range(2):
                c.op("dve", lambda e, T=T, t=t: e.scalar_tensor_tensor(
                    out=tmpo[:, 0:T], in0=bk(4)[:, t * 128:t * 128 + T], scalar=glag_sb[:, t:t + 1], in1=sdo[:, 0:T],
                    op0=ALU.mult, op1=ALU.mult),
                    reads=[R_sdo, R_gw], writes=[RB[4], R_tmpo])
                c.op("dve", lambda e, T=T, t=t, cols=cols: e.tensor_tensor(
                    out=ost[:, t, cols], in0=tmpo[:, 0:T], in1=srT[:, t, cols], op=ALU.mult),
                    reads=[R_tmpo, R_sr], writes=[R_ost])
        out_toks.append(c.dma("pool", lambda e, h=h: e.dma_start(out=glap[h, :, :], in_=Sst), reads=[R_S]))
        for t in range(2):
            c.dma("pool", lambda e, h=h, t=t: e.dma_start(
                out=oscr[h * 256 + t * 128:h * 256 + (t + 1) * 128, :], in_=ost[:, t, :]), reads=[R_ost])

    if stop_after < 2:
        return finish()
    import os
    DBG = int(os.environ.get("DBG", "255"))
    OQ = os.environ.get("OQ", "pool")
    c.barrier()
    ar.top = mark_p2
    vtok = [ar.alloc([17, 256], BF16) for _ in range(2)]
    R_vtok = [Res(), Res()]
    mark_nsa = ar.top
    wkv = [ar.alloc([16, 512], BF16) for _ in range(2)]; R_wkv = [Res(), Res()]
    stg = [ar.alloc([512], F32) for _ in range(3)]; R_stg = [Res() for _ in range(3)]
    si = 0
    for ch in range(3):
        ws = ch % 2
        col0 = OFF["kc"] + 512 * ch
        for k4 in range(8):
            wload("pool", wkv[ws][:, 2 * k4:2 * k4 + 2, :],
                  w_in[256 * k4:256 * k4 + 256, col0:col0 + 512].rearrange("(kt p) n -> p kt n", p=128), R_wkv[ws])
        for tt in range(17):
            rows = 128 if tt < 16 else 64
            b = 4 + (tt % 2)
            mms(bk(b)[0:rows, :], [(xnT[:, kt, tt * 128:tt * 128 + rows], wkv[ws][:, kt, :]) for kt in range(16)],
                [R_xnT[tt], R_wkv[ws]], [RB[b]])
            s = si % 3
            si += 1
            c.op("act", lambda e, s=s, b=b, rows=rows: e.copy(out=stg[s][0:rows], in_=bk(b)[0:rows, :]),
                 reads=[RB[b]], writes=[R_stg[s]])
            if ch >= 1 and (DBG & 16):
                c.op("act", lambda e, b=b, rows=rows, tt=tt, ch=ch: e.copy(
                    out=vtok[ch - 1][0:rows, tt, :], in_=bk(b)[0:rows, 256:512]),
                    reads=[RB[b]], writes=[R_vtok[ch - 1]])
            for half in range(2):
                src = stg[s][0:rows, 256 * half:256 * half + 256]
                if tt < 16:
                    if not (DBG & 1):
                        continue
                    dst = kvp[2 * ch + half][tt * 128:(tt + 1) * 128, :] if ch < 2 else None
                    if ch < 2:
                        out_toks.append(c.dma(OQ, lambda e, dst=dst, src=src: e.dma_start(out=dst, in_=src),
                                              reads=[R_stg[s]]))
                    elif tt >= 12:
                        dst = (wkp, wvp)[half][(tt - 12) * 128:(tt - 11) * 128, :]
                        out_toks.append(c.dma(OQ, lambda e, dst=dst, src=src: e.dma_start(out=dst, in_=src),
                                              reads=[R_stg[s]]))
                else:
                    if ch < 2:
                        if not (DBG & 2):
                            continue
                        dst = kvs[2 * ch + half][:, :]
                        out_toks.append(c.dma(OQ, lambda e, dst=dst, src=src: e.dma_start(out=dst, in_=src),
                                              reads=[R_stg[s]]))
                    else:
                        wdst = (wks, wvs)[half]
                        for bb in range(16 if (DBG & 4) else 0):
                            out_toks.append(c.dma(
                                OQ, lambda e, bb=bb, wdst=wdst, s=s, half=half: e.dma_start(
                                    out=wdst[bb, 508:512, :],
                                    in_=stg[s][4 * bb:4 * bb + 4, 256 * half:256 * half + 256]),
                                reads=[R_stg[s]]))
    wbn = [ar.alloc([4, 256], F32) for _ in range(2)]; R_wbn = [Res(), Res()]
    wi = 0
    for (srcw, dstw) in ((swk, wks), (swv, wvs)):
        for bb in range(16 if (DBG & 8) else 0):
            s_ = wi % 2
            wi += 1
            c.dma("pool", lambda e, srcw=srcw, bb=bb, s_=s_: e.dma_start(
                out=wbn[s_][0:127], in_=srcw[bb, 4:512, :].rearrange("(p r) c -> p r c", r=4)),
                writes=[R_wbn[s_]])
            out_toks.append(c.dma("pool", lambda e, dstw=dstw, bb=bb, s_=s_: e.dma_start(
                out=dstw[bb, 0:508, :].rearrange("(p r) c -> p r c", r=4), in_=wbn[s_][0:127]),
                reads=[R_wbn[s_]]))
    import os as _os
    if int(_os.environ.get("NSA", "1")):
        c.barrier()
        ar.top = mark_nsa
        wq = [ar.alloc([16, 128], BF16) for _ in range(2)]; R_wq = [Res(), Res()]
        qTn = ar.alloc([4, NT], BF16); R_qn = Res()
        ksT = ar.alloc([NT], BF16); kwT = ar.alloc([NT], BF16); R_ks, R_kw = Res(), Res()
        sigG = ar.alloc([NT], BF16); R_sigG = Res()
        qs = ar.alloc([8, 64], BF16); kss = ar.alloc([2, 64], BF16); kws = ar.alloc([2, 64], BF16); R_qs = Res()
        mark_s = ar.top
        oacc = ar.alloc([4, SEQ], BF16); R_oacc = Res()
        selT = ar.alloc([SEQ], BF16); R_selT = Res()
        impa = ar.alloc([16, 32], F32); R_imp = Res()
        ckT = ar.alloc([128], BF16); cvs = ar.alloc([128], BF16); R_ck, R_cv = Res(), Res()
        Et = [ar.alloc([512], BF16) for _ in range(2)]; R_Et = [Res(), Res()]
        zt = [ar.alloc([512], F32)]; R_zt = [Res()]
        osum = ar.alloc([512], F32); R_osum = Res()
        sc1 = ar.alloc([32], F32); sc2 = ar.alloc([32], F32); m8 = ar.alloc([8], F32); m8b = ar.alloc([8], F32)
        selb = ar.alloc([32], BF16); rz = ar.alloc([4], F32)
        R_sc1, R_sc2, R_m8, R_m8b, R_selb, R_rz = Res(), Res(), Res(), Res(), Res(), Res()
        mark_x = ar.top
        cmask = CB("cmask"); caus = CB("caus").rearrange("p (r q) -> p r q", r=4)
        wmask = CB("wmask").rearrange("p (r q) -> p r q", r=8)
        expand = CB("expand").rearrange("p (k n) -> p k n", k=16)
        onehot = CB("onehot").rearrange("p (r n) -> p r n", r=24)
        mp = CB("mp"); A_p = CF("A_p").rearrange("p (s j) -> p s j", s=16); B_p = CF("B_p").rearrange("p (s j) -> p s j", s=16)
        proj_fm(OFF["ng"], 24, lambda n0, N: sigG[0:24, n0:n0 + N], R_sigG,
                lambda out, pin, rb, Rd: c.op("act", lambda e: e.activation(out=out, in_=pin, func=AF.Sigmoid),
                                              writes=rb + [Rd]))
        for g in range(2):
            c.barrier()
            ar.top = mark_x
            kcT = ar.alloc([NT], BF16); vcT = ar.alloc([NT], BF16); R_kc, R_vc = Res(), Res()
            w1k = ar.alloc([32, 128], BF16); w1v = w1k; w2k = ar.alloc([128], BF16); w2v = ar.alloc([128], BF16)
            R_w1 = Res()
            hk = ar.alloc([128], BF16); hv = ar.alloc([128], BF16); R_hk, R_hv = Res(), Res()
            c.dma("pool", lambda e: e.dma_start(out=w2k, in_=wk2[:, :]), writes=[R_w1])
            c.dma("pool", lambda e: e.dma_start(out=w2v, in_=wv2[:, :]), writes=[R_w1])
            for hh in range(4):
                proj_fm(OFF["nq"] + g * 512 + hh * 128, 128, lambda n0, N, hh=hh: qTn[:, hh, n0:n0 + N], R_qn, ev_copy)
            proj_fm(OFF["kc"] + g * 128, 128, lambda n0, N: kcT[:, n0:n0 + N], R_kc, ev_copy)
            proj_fm(OFF["vc"] + g * 128, 128, lambda n0, N: vcT[:, n0:n0 + N], R_vc, ev_copy)
            proj_fm(OFF["ks"] + g * 128, 128, lambda n0, N: ksT[:, n0:n0 + N], R_ks, ev_copy)
            proj_fm(OFF["kw"] + g * 128, 128, lambda n0, N: kwT[:, n0:n0 + N], R_kw, ev_copy)
            c.op("act", lambda e, g=g: e.copy(out=qs[:, 4 * g:4 * g + 4, :], in_=qTn[:, :, SEQ:NT]), reads=[R_qn], writes=[R_qs])
            c.op("act", lambda e, g=g: e.copy(out=kss[:, g, :], in_=ksT[:, SEQ:NT]), reads=[R_ks], writes=[R_qs])
            c.op("act", lambda e, g=g: e.copy(out=kws[:, g, :], in_=kwT[:, SEQ:NT]), reads=[R_kw], writes=[R_qs])
            for (srcT, w1, hdst, Rs, Rh, src_) in ((kcT, w1k, hk, R_kc, R_hk, wk1), (vcT, w1v, hv, R_vc, R_hv, wv1)):
                for j8 in range(4):
                    c.dma("pool", lambda e, w1=w1, src_=src_, j8=j8: e.dma_start(
                        out=w1[:, 8 * j8:8 * j8 + 8, :], in_=src_[8 * j8:8 * j8 + 8, :, :].rearrange("j d e -> d j e")),
                        writes=[R_w1])
                mms(bk(0)[:, 0:127], [(w1[:, j, :], srcT[:, j:j + 2017:16]) for j in range(32)], [R_w1, Rs], [RB[0]])
                c.op("act", lambda e, hdst=hdst: e.activation(out=hdst[:, 0:127], in_=bk(0)[:, 0:127], func=AF.Gelu_apprx_tanh),
                     writes=[RB[0], Rh])
            mms(bk(1)[:, 0:127], [(w2k[:, :], hk[:, 0:127])], [R_w1, R_hk], [RB[1]])
            c.op("act", lambda e: e.copy(out=ckT[:, 0:127], in_=bk(1)[:, 0:127]), writes=[RB[1], R_ck])
            mms(bk(2)[0:127, 0:128], [(hv[:, 0:127], w2v[:, :])], [R_w1, R_hv], [RB[2]])
            c.op("act", lambda e: e.copy(out=cvs[0:127, :], in_=bk(2)[0:127, 0:128]), writes=[RB[2], R_cv])
            c.op("pool", lambda e: e.memset(impa, 0.0), writes=[R_imp])

            def finalize(pvb, zb, r, q0, first, dst, Rdst, srcadd=None, Rsrc=None):
                mms(bk(6)[:, 0:512], [(onehot[0:24, r, :], sigG[0:24, q0:q0 + 512])], [R_cb, R_sigG], [RB[6]])
                c.op("dve", lambda e, zb=zb: e.tensor_scalar(out=zt[0], in0=bk(zb)[:, 0:512], scalar1=1e-30, scalar2=None,
                                                            op0=ALU.max), writes=[RB[zb], R_zt[0]])
                c.op("dve", lambda e: e.reciprocal(out=zt[0], in_=zt[0]), reads=[R_zt[0]], writes=[R_zt[0]])
                c.op("dve", lambda e: e.tensor_tensor(out=zt[0], in0=bk(6)[:, 0:512], in1=zt[0], op=ALU.mult),
                     reads=[R_zt[0]], writes=[RB[6], R_zt[0]])
                if first:
                    c.op("dve", lambda e, pvb=pvb, dst=dst: e.tensor_tensor(
                        out=dst, in0=bk(pvb)[:, 0:512], in1=zt[0], op=ALU.mult),
                        reads=[R_zt[0]], writes=[RB[pvb], Rdst])
                else:
                    c.op("dve", lambda e, pvb=pvb: e.tensor_tensor(out=osum, in0=bk(pvb)[:, 0:512], in1=zt[0], op=ALU.mult),
                         reads=[R_zt[0]], writes=[RB[pvb], R_osum])
                    c.op("dve", lambda e, dst=dst, srcadd=srcadd: e.tensor_tensor(
                        out=dst, in0=srcadd, in1=osum, op=ALU.add),
                        reads=[R_osum, Rsrc], writes=[Rdst])

            ei = 0
            for hh in range(4):
                for qt in range(4):
                    q0 = qt * 512
                    sb_ = ei % 2
                    e_ = ei % 2
                    ei += 1
                    mms(bk(sb_)[0:127, 0:512], [(ckT[:, 0:127], qTn[:, hh, q0:q0 + 512])], [R_ck, R_qn], [RB[sb_]])
                    c.op("act", lambda e, sb_=sb_, e_=e_: e.activation(out=Et[e_][0:127], in_=bk(sb_)[0:127, 0:512],
                                                                     func=AF.Exp, scale=SCALE),
                         writes=[RB[sb_], R_Et[e_]])
                    c.op("dve", lambda e, e_=e_, q0=q0: e.tensor_tensor(out=Et[e_][0:127], in0=Et[e_][0:127],
                                                                      in1=cmask[0:127, q0:q0 + 512], op=ALU.mult),
                         reads=[R_cb], writes=[R_Et[e_]])
                    mms(bk(2)[:, 0:512], [(cvs[0:127, :], Et[e_][0:127])], [R_cv, R_Et[e_]], [RB[2]])
                    mms(bk(3)[:, 0:512], [(ones_b[0:127, :], Et[e_][0:127])], [R_cb, R_Et[e_]], [RB[3]])
                    fns = [lambda e, e_=e_, s4=s4: e.matmul(bk(4)[:, 64 * s4:64 * s4 + 33], lhsT=Et[e_][0:127, 128 * s4:128 * s4 + 128],
                                                           rhs=mp[0:127, 0:33], start=True, stop=True) for s4 in range(4)]
                    c.op("pe", fns, [R_Et[e_], R_cb], [RB[4]])
                    em = bk(4)[:, 0:256].rearrange("p (s j) -> p s j", s=4)
                    c.op("dve", lambda e, em=em: e.tensor_scalar(out=rz, in0=em[:, :, 32], scalar1=1e-30, scalar2=None, op0=ALU.max),
                         writes=[RB[4], R_rz])
                    c.op("dve", lambda e: e.reciprocal(out=rz, in_=rz), reads=[R_rz], writes=[R_rz])
                    for s4 in range(4):
                        c.op("dve", lambda e, em=em, s4=s4, qt=qt: e.scalar_tensor_tensor(
                            out=impa[:, 4 * qt + s4, :], in0=em[:, s4, 0:32], scalar=rz[:, s4:s4 + 1], in1=impa[:, 4 * qt + s4, :],
                            op0=ALU.mult, op1=ALU.add), reads=[R_rz], writes=[RB[4], R_imp])
                    finalize(2, 3, 0 * 8 + g * 4 + hh, q0, True, oacc[:, hh, q0:q0 + 512], R_oacc)
            for s16 in range(16):
                c.op("dve", lambda e, s16=s16: e.tensor_tensor(out=sc1, in0=impa[:, s16, :], in1=A_p[:, s16, :], op=ALU.mult),
                     reads=[R_imp, R_cf], writes=[R_sc1])
                c.op("dve", lambda e, s16=s16: e.tensor_tensor(out=sc1, in0=sc1, in1=B_p[:, s16, :], op=ALU.add),
                     reads=[R_sc1, R_cf], writes=[R_sc1])
                c.op("dve", lambda e: e.max(out=m8, in_=sc1), reads=[R_sc1], writes=[R_m8])
                c.op("dve", lambda e: e.match_replace(out=sc2, in_to_replace=m8, in_values=sc1, imm_value=-2.0),
                     reads=[R_sc1, R_m8], writes=[R_sc2])
                c.op("dve", lambda e: e.max(out=m8b, in_=sc2), reads=[R_sc2], writes=[R_m8b])
                c.op("dve", lambda e: e.tensor_scalar(out=selb, in0=sc1, scalar1=m8b[:, 7:8], scalar2=None, op0=ALU.is_ge),
                     reads=[R_sc1, R_m8b], writes=[R_selb])
                tposes([(bkb(5)[0:32, 0:128], selb[:, :], ident_b)], [R_selb, R_cb], [RB[5]])
                c.op("act", lambda e, s16=s16: e.copy(out=selT[0:32, s16 * 128:(s16 + 1) * 128], in_=bkb(5)[0:32, 0:128]),
                     writes=[RB[5], R_selT])
            c.barrier()
            ar.top = mark_x
            msk = ar.alloc([16, 512], BF16); R_msk = Res()
            ostq = [ar.alloc([512], BF16) for _ in range(2)]; R_ostq = [Res(), Res()]
            for qt in range(4):
                q0 = qt * 512
                nk = 4 * qt + 4
                for kt in range(nk):
                    mms(bk(7)[:, 0:512], [(expand[0:32, kt, :], selT[0:32, q0:q0 + 512])], [R_cb, R_selT], [RB[7]])
                    if kt >= 4 * qt:
                        c.op("dve", lambda e, kt=kt, qt=qt: e.tensor_tensor(out=msk[:, kt, :], in0=bk(7)[:, 0:512],
                                                                           in1=caus[:, kt - 4 * qt, :], op=ALU.mult),
                             reads=[R_cb], writes=[RB[7], R_msk])
                    else:
                        c.op("act", lambda e, kt=kt: e.copy(out=msk[:, kt, :], in_=bk(7)[:, 0:512]), writes=[RB[7], R_msk])
                for hh in range(4):
                    for br in (1, 2):
                        kts = list(range(nk)) if br == 1 else list(range(max(0, 4 * qt - 4), nk))
                        kT_ = ksT if br == 1 else kwT
                        Rk_ = R_ks if br == 1 else R_kw
                        pvb, zb = (2, 3) if br == 1 else (4, 5)
                        for ii, kt in enumerate(kts):
                            sb_ = ei % 2
                            e_ = ei % 2
                            ei += 1
                            mms(bk(sb_)[:, 0:512], [(kT_[:, kt * 128:(kt + 1) * 128], qTn[:, hh, q0:q0 + 512])], [Rk_, R_qn], [RB[sb_]])
                            c.op("act", lambda e, sb_=sb_, e_=e_: e.activation(out=Et[e_], in_=bk(sb_)[:, 0:512], func=AF.Exp, scale=SCALE),
                                 writes=[RB[sb_], R_Et[e_]])
                            mk = msk[:, kt, :] if br == 1 else wmask[:, kt - 4 * qt + 4, :]
                            c.op("dve", lambda e, e_=e_, mk=mk: e.tensor_tensor(out=Et[e_], in0=Et[e_], in1=mk, op=ALU.mult),
                                 reads=[R_msk, R_cb], writes=[R_Et[e_]])
                            vop = vtok[br - 1][:, kt, g * 128:(g + 1) * 128]
                            first, last = (ii == 0), (ii == len(kts) - 1)
                            c.op("pe", [lambda e, vop=vop, e_=e_, pvb=pvb, first=first, last=last: e.matmul(
                                bk(pvb)[:, 0:512], lhsT=vop, rhs=Et[e_], start=first, stop=last),
                                lambda e, e_=e_, zb=zb, first=first, last=last: e.matmul(
                                    bk(zb)[:, 0:512], lhsT=ones_b, rhs=Et[e_], start=first, stop=last)],
                                [R_vtok[br - 1], R_Et[e_], R_cb], [RB[pvb], RB[zb]])
                        if br == 1:
                            finalize(pvb, zb, 1 * 8 + g * 4 + hh, q0, False, oacc[:, hh, q0:q0 + 512], R_oacc,
                                     oacc[:, hh, q0:q0 + 512], R_oacc)
                        else:
                            oq = (qt * 4 + hh) % 2
                            finalize(pvb, zb, 2 * 8 + g * 4 + hh, q0, False, ostq[oq], R_ostq[oq],
                                     oacc[:, hh, q0:q0 + 512], R_oacc)
                            row0 = 1024 + (g * 4 + hh) * 128
                            c.dma("pool", lambda e, oq=oq, row0=row0, q0=q0: e.dma_start(
                                out=oscr[row0:row0 + 128, q0:q0 + 512], in_=ostq[oq]), reads=[R_ostq[oq]])
    if int(_os.environ.get("NSA", "1")) and int(_os.environ.get("NSAS", "1")):
        c.barrier()
        nsa_top = ar.top
        ar.top = mark_persist
        raw = ar.alloc([4096], F32); R_raw = Res()
        rawb = ar.alloc([16, 256], BF16); R_rawb = Res()
        KT = ar.alloc([16, 2, 128], BF16); R_KT = Res()
        wraw = ar.alloc([4, 256], F32); R_wraw = Res()
        wrb = ar.alloc([4, 256], BF16); R_wrb = Res()
        KwT = ar.alloc([4, 2, 128], BF16); R_KwT = Res()
        assert ar.top <= mark_p2, "sample NSA tiles overflow xnT region: %d > %d" % (ar.top, mark_p2)
        ar.top = mark_s
        w1ks = ar.alloc([32, 128], BF16); w1vs = ar.alloc([32, 128], BF16); w2ks = ar.alloc([128], BF16); w2vs = ar.alloc([128], BF16)
        R_w1s = Res()
        hks = ar.alloc([2, 128], BF16); R_hks = Res()
        ckTs = ar.alloc([2, 128], BF16); cvss = ar.alloc([2, 128], BF16); R_ckTs, R_cvss = Res(), Res()
        Eall = ar.alloc([8, 64], BF16); R_Eall = Res()
        PVt = ar.alloc([3, 8, 64], F32); Zt = ar.alloc([3, 8, 64], F32); R_PVt, R_Zt = Res(), Res()
        ptT8 = ar.alloc([16], I32); idxf = ar.alloc([16], F32); idx = ar.alloc([16], I32); R_idx = Res()
        Es2 = [ar.alloc([16, 4], BF16) for _ in range(2)]; En2 = [ar.alloc([4], BF16) for _ in range(2)]
        Esum2 = [ar.alloc([4], F32) for _ in range(2)]; Ew2 = [ar.alloc([4, 4], BF16) for _ in range(2)]
        R_Es2 = [Res(), Res()]; R_En2 = [Res(), Res()]; R_Esum2 = [Res(), Res()]; R_Ew2 = [Res(), Res()]
        imps = ar.alloc([2, 33], F32); rzs = ar.alloc([8], F32); R_imps, R_rzs = Res(), Res()
        scs = ar.alloc([33], F32); scs2 = ar.alloc([33], F32); m8s = ar.alloc([8], F32); m8sb = ar.alloc([8], F32)
        sels = ar.alloc([33], BF16); selTs = ar.alloc([2, 64], BF16); masks = ar.alloc([2, 64], BF16)
        R_scs, R_scs2, R_m8s, R_m8sb, R_sels, R_selTs, R_masks = Res(), Res(), Res(), Res(), Res(), Res(), Res()
        osm = ar.alloc([8, 64], F32); Fs = ar.alloc([8, 64], F32); osb = ar.alloc([8, 64], BF16); R_osm, R_Fs, R_osb = Res(), Res(), Res()
        ms_ = CB("ms"); A_s = CF("A_s"); B_s = CF("B_s"); cvec = CF("cvec"); expand_s = CB("expand_s")
        newmask = CB("newmask").rearrange("p (b l) -> p b l", b=16); winmask_s = CB("winmask_s").rearrange("p (k l) -> p k l", k=4)
        onehot = CB("onehot").rearrange("p (r n) -> p r n", r=24)
        for (dst_, src_) in ((w1ks, wk1), (w1vs, wv1)):
            for j8 in range(4):
                c.dma("pool", lambda e, dst_=dst_, src_=src_, j8=j8: e.dma_start(
                    out=dst_[:, 8 * j8:8 * j8 + 8, :], in_=src_[8 * j8:8 * j8 + 8, :, :].rearrange("j d e -> d j e")),
                    writes=[R_w1s])
        c.dma("pool", lambda e: e.dma_start(out=w2ks, in_=wk2[:, :]), writes=[R_w1s])
        c.dma("pool", lambda e: e.dma_start(out=w2vs, in_=wv2[:, :]), writes=[R_w1s])
        for c8 in range(8):
            c.dma("sp", lambda e, c8=c8: e.dma_start(out=ptT8[16 * c8:16 * c8 + 16, :], in_=pt.rearrange("b j -> j b"),
                                                    allow_slow_non_contiguous=True), writes=[R_idx])
        c.op("dve", lambda e: e.tensor_copy(out=idxf, in_=ptT8), reads=[R_idx], writes=[R_idx])
        c.op("dve", lambda e: e.tensor_scalar(out=idxf, in0=idxf, scalar1=8.0, scalar2=cvec[:, 0:1], op0=ALU.mult, op1=ALU.add),
             reads=[R_idx, R_cf], writes=[R_idx])
        c.op("dve", lambda e: e.tensor_copy(out=idx, in_=idxf), reads=[R_idx], writes=[R_idx])

        def gather(cache, b):
            c.dma("pool", lambda e, cache=cache, b=b: e.indirect_dma_start(
                out=raw, out_offset=None, in_=cache[:, :],
                in_offset=bass.IndirectOffsetOnAxis(ap=idx[:, b:b + 1], axis=0)), reads=[R_idx], writes=[R_raw])
            c.op("act", lambda e: e.copy(out=rawb.rearrange("p j c -> p (j c)"), in_=raw), reads=[R_raw], writes=[R_rawb])

        def transposeK(unpermute):
            for grp in range(4):
                bnk = grp % 2
                items = []
                for i8 in range(8):
                    jj, g = (grp * 8 + i8) // 2, (grp * 8 + i8) % 2
                    items.append((bkb(bnk)[:, i8 * 128:(i8 + 1) * 128], rawb[:, jj, g * 128:(g + 1) * 128], ident_b))
                tposes(items, [R_rawb, R_cb], [RB[bnk]])
                for i8 in range(8):
                    jj, g = (grp * 8 + i8) // 2, (grp * 8 + i8) % 2
                    src = bkb(bnk)[:, i8 * 128:(i8 + 1) * 128]
                    if unpermute:
                        c.op("act", lambda e, jj=jj, g=g, src=src: e.copy(
                            out=KT[:, jj, g, :].rearrange("d (j c) -> d c j", c=8), in_=src.rearrange("d (c j) -> d c j", c=8)),
                            writes=[RB[bnk], R_KT])
                    else:
                        c.op("act", lambda e, jj=jj, g=g, src=src: e.copy(out=KT[:, jj, g, :], in_=src), writes=[RB[bnk], R_KT])

        def compress_h(w1):
            for g in range(2):
                mms(bk(2)[:, g * 128:g * 128 + 127], [(w1[:, j, :], KT[:, j % 16, g, (j // 16):(j // 16) + 127]) for j in range(32)],
                    [R_w1s, R_KT], [RB[2]])
            c.op("act", lambda e: e.activation(out=hks[:, :, 0:127], in_=bk(2)[:, 0:256].rearrange("p (g n) -> p g n", g=2)[:, :, 0:127],
                                               func=AF.Gelu_apprx_tanh), writes=[RB[2], R_hks])

        for b in range(NB_S):
            gather(cck, b)
            transposeK(True)
            compress_h(w1ks)
            for g in range(2):
                mms(bk(3)[:, g * 128:g * 128 + 127], [(w2ks[:, :], hks[:, g, 0:127])], [R_w1s, R_hks], [RB[3]])
            c.op("act", lambda e: e.copy(out=ckTs[:, :, 0:127], in_=bk(3)[:, 0:256].rearrange("p (g n) -> p g n", g=2)[:, :, 0:127]),
                 writes=[RB[3], R_ckTs])
            gather(ccv, b)
            transposeK(True)
            compress_h(w1vs)
            for g in range(2):
                mms(bk(3)[0:127, g * 128:(g + 1) * 128], [(hks[:, g, 0:127], w2vs[:, :])], [R_w1s, R_hks], [RB[3]])
            c.op("act", lambda e: e.copy(out=cvss[0:127].rearrange("p g n -> p (g n)"), in_=bk(3)[0:127, 0:256]), writes=[RB[3], R_cvss])
            fns = [lambda e, hd=hd, b=b: e.matmul(bk(4)[0:127, hd * 4:hd * 4 + 4], lhsT=ckTs[:, hd // 4, 0:127],
                                                 rhs=qs[:, hd, 4 * b:4 * b + 4], start=True, stop=True) for hd in range(8)]
            c.op("pe", fns, [R_ckTs, R_qs], [RB[4]])
            c.op("act", lambda e, b=b: e.activation(out=Eall[0:127, :, 4 * b:4 * b + 4],
                                                    in_=bk(4)[0:127, 0:32].rearrange("p (h l) -> p h l", h=8), func=AF.Exp, scale=SCALE),
                 writes=[RB[4], R_Eall])
            fns = []
            for hd in range(8):
                fns.append(lambda e, hd=hd, b=b: e.matmul(bk(5)[:, hd * 4:hd * 4 + 4], lhsT=cvss[0:127, hd // 4, :],
                                                          rhs=Eall[0:127, hd, 4 * b:4 * b + 4], start=True, stop=True))
                fns.append(lambda e, hd=hd, b=b: e.matmul(bk(6)[:, hd * 4:hd * 4 + 4], lhsT=ones_b[0:127, :],
                                                          rhs=Eall[0:127, hd, 4 * b:4 * b + 4], start=True, stop=True))
            c.op("pe", fns, [R_cvss, R_Eall, R_cb], [RB[5], RB[6]])
            c.op("act", lambda e, b=b: e.copy(out=PVt[:, 0, :, 4 * b:4 * b + 4], in_=bk(5)[:, 0:32].rearrange("p (h l) -> p h l", h=8)),
                 writes=[RB[5], R_PVt])
            c.op("act", lambda e, b=b: e.copy(out=Zt[:, 0, :, 4 * b:4 * b + 4], in_=bk(6)[:, 0:32].rearrange("p (h l) -> p h l", h=8)),
                 writes=[RB[6], R_Zt])
        fns = [lambda e, hd=hd: e.matmul(bk(4)[0:64, hd * 64:hd * 64 + 34], lhsT=Eall[0:127, hd, :], rhs=ms_[0:127, 0:34],
                                         start=True, stop=True) for hd in range(8)]
        c.op("pe", fns, [R_Eall, R_cb], [RB[4]])
        ems = bk(4)[0:64, :].rearrange("p (h j) -> p h j", h=8)
        c.op("dve", lambda e: e.tensor_scalar(out=rzs[0:64], in0=ems[:, :, 33], scalar1=1e-30, scalar2=None, op0=ALU.max),
             writes=[RB[4], R_rzs])
        c.op("dve", lambda e: e.reciprocal(out=rzs[0:64], in_=rzs[0:64]), reads=[R_rzs], writes=[R_rzs])
        c.op("pool", lambda e: e.memset(imps, 0.0), writes=[R_imps])
        for hd in range(8):
            c.op("dve", lambda e, hd=hd: e.scalar_tensor_tensor(out=imps[0:64, hd // 4, :], in0=ems[:, hd, 0:33],
                                                               scalar=rzs[0:64, hd:hd + 1], in1=imps[0:64, hd // 4, :],
                                                               op0=ALU.mult, op1=ALU.add),
                 reads=[R_rzs], writes=[RB[4], R_imps])
        for g in range(2):
            c.op("dve", lambda e, g=g: e.tensor_tensor(out=scs[0:64], in0=imps[0:64, g, :], in1=A_s[0:64], op=ALU.mult),
                 reads=[R_imps, R_cf], writes=[R_scs])
            c.op("dve", lambda e: e.tensor_tensor(out=scs[0:64], in0=scs[0:64], in1=B_s[0:64], op=ALU.add),
                 reads=[R_scs, R_cf], writes=[R_scs])
            c.op("dve", lambda e: e.max(out=m8s[0:64], in_=scs[0:64]), reads=[R_scs], writes=[R_m8s])
            c.op("dve", lambda e: e.match_replace(out=scs2[0:64], in_to_replace=m8s[0:64], in_values=scs[0:64], imm_value=-2.0),
                 reads=[R_scs, R_m8s], writes=[R_scs2])
            c.op("dve", lambda e: e.max(out=m8sb[0:64], in_=scs2[0:64]), reads=[R_scs2], writes=[R_m8sb])
            c.op("dve", lambda e: e.tensor_scalar(out=sels[0:64], in0=scs[0:64], scalar1=m8sb[0:64, 7:8], scalar2=None, op0=ALU.is_ge),
                 reads=[R_scs, R_m8sb], writes=[R_sels])
            tposes([(bkb(5)[0:33, 0:64], sels[0:64, :], ident_b[0:64, 0:64])], [R_sels, R_cb], [RB[5]])
            c.op("act", lambda e, g=g: e.copy(out=selTs[0:33, g, :], in_=bkb(5)[0:33, 0:64]), writes=[RB[5], R_selTs])
            mms(bk(6)[:, 0:64], [(expand_s[0:33, :], selTs[0:33, g, :])], [R_cb, R_selTs], [RB[6]])
            c.op("act", lambda e, g=g: e.copy(out=masks[:, g, :], in_=bk(6)[:, 0:64]), writes=[RB[6], R_masks])
        for b in range(NB_S):
            gather(csk, b)
            transposeK(False)
            gather(csv, b)
            for (srcw, kv) in ((swk, 0), (swv, 1)):
                c.dma("sp", lambda e, srcw=srcw, b=b: e.dma_start(out=wraw, in_=srcw[b, :, :].rearrange("(k r) c -> r k c", r=128)),
                      writes=[R_wraw])
                if kv == 0:
                    c.op("act", lambda e: e.copy(out=wrb, in_=wraw), reads=[R_wraw], writes=[R_wrb])
                    tposes([(bkb(7)[:, i8 * 128:(i8 + 1) * 128], wrb[:, i8 // 2, (i8 % 2) * 128:(i8 % 2 + 1) * 128], ident_b)
                            for i8 in range(8)], [R_wrb, R_cb], [RB[7]])
                    c.op("act", lambda e: e.copy(out=KwT.rearrange("p k g n -> p (k g n)"), in_=bkb(7)[:, 0:1024]),
                         writes=[RB[7], R_KwT])
                else:
                    c.op("act", lambda e: e.copy(out=wrb, in_=wraw), reads=[R_wraw, R_wrb], writes=[R_wrb])
            for hd in range(8):
                g = hd // 4
                par = hd % 2
                SB_, PB_, ZB_ = (4, 3)[par], (5, 2)[par], (6, 7)[par]
                Es, En, Esum, Ew = Es2[par], En2[par], Esum2[par], Ew2[par]
                R_Es, R_En, R_Esum, R_Ew = R_Es2[par], R_En2[par], R_Esum2[par], R_Ew2[par]
                q4 = qs[:, hd, 4 * b:4 * b + 4]
                fns = [lambda e, Es=Es, En=En, Esum=Esum, Ew=Ew, SB_=SB_, PB_=PB_, ZB_=ZB_, jj=jj, g=g, q4=q4: e.matmul(bk(SB_)[:, jj * 4:jj * 4 + 4], lhsT=KT[:, jj, g, :], rhs=q4,
                                                            start=True, stop=True) for jj in range(16)]
                fns.append(lambda e, Es=Es, En=En, Esum=Esum, Ew=Ew, SB_=SB_, PB_=PB_, ZB_=ZB_, g=g, q4=q4: e.matmul(bk(SB_)[0:64, 64:68], lhsT=kss[:, g, :], rhs=q4, start=True, stop=True))
                c.op("pe", fns, [R_KT, R_qs], [RB[SB_]])
                c.op("act", lambda e, Es=Es, En=En, Esum=Esum, Ew=Ew, SB_=SB_, PB_=PB_, ZB_=ZB_: e.activation(out=Es, in_=bk(SB_)[:, 0:64].rearrange("p (j l) -> p j l", j=16), func=AF.Exp, scale=SCALE),
                     writes=[RB[SB_], R_Es])
                c.op("act", lambda e, Es=Es, En=En, Esum=Esum, Ew=Ew, SB_=SB_, PB_=PB_, ZB_=ZB_: e.activation(out=En[0:64], in_=bk(SB_)[0:64, 64:68], func=AF.Exp, scale=SCALE),
                     writes=[RB[SB_], R_En])
                c.op("dve", lambda e, Es=Es, En=En, Esum=Esum, Ew=Ew, SB_=SB_, PB_=PB_, ZB_=ZB_, g=g, b=b: e.tensor_tensor(
                    out=Es, in0=Es, in1=masks[:, g, 4 * b:4 * b + 4].unsqueeze(1).broadcast_to([128, 16, 4]), op=ALU.mult),
                    reads=[R_masks], writes=[R_Es])
                c.op("dve", lambda e, Es=Es, En=En, Esum=Esum, Ew=Ew, SB_=SB_, PB_=PB_, ZB_=ZB_, b=b: e.tensor_tensor(out=En[0:64], in0=En[0:64], in1=newmask[0:64, b, :], op=ALU.mult),
                     reads=[R_cb], writes=[R_En])
                c.op("dve", lambda e, Es=Es, En=En, Esum=Esum, Ew=Ew, SB_=SB_, PB_=PB_, ZB_=ZB_: e.tensor_reduce(out=Esum, in_=Es.rearrange("p j l -> p l j"), axis=AX.X, op=ALU.add),
                     reads=[R_Es], writes=[R_Esum])
                fns = [lambda e, Es=Es, En=En, Esum=Esum, Ew=Ew, SB_=SB_, PB_=PB_, ZB_=ZB_, jj=jj, g=g: e.matmul(bk(PB_)[:, 0:4], lhsT=rawb[:, jj, g * 128:(g + 1) * 128], rhs=Es[:, jj, :],
                                                     start=(jj == 0), stop=False) for jj in range(16)]
                fns.append(lambda e, Es=Es, En=En, Esum=Esum, Ew=Ew, SB_=SB_, PB_=PB_, ZB_=ZB_, g=g: e.matmul(bk(PB_)[:, 0:4], lhsT=vtok[0][0:64, 16, g * 128:(g + 1) * 128], rhs=En[0:64],
                                                   start=False, stop=True))
                fns.append(lambda e, Es=Es, En=En, Esum=Esum, Ew=Ew, SB_=SB_, PB_=PB_, ZB_=ZB_: e.matmul(bk(ZB_)[:, 0:4], lhsT=ones_f, rhs=Esum, start=True, stop=False))
                fns.append(lambda e, Es=Es, En=En, Esum=Esum, Ew=Ew, SB_=SB_, PB_=PB_, ZB_=ZB_: e.matmul(bk(ZB_)[:, 0:4], lhsT=ones_b[0:64, :], rhs=En[0:64], start=False, stop=True))
                c.op("pe", fns, [R_rawb, R_Es, R_En, R_Esum, R_vtok[0], R_cb], [RB[PB_], RB[ZB_]])
                c.op("act", lambda e, Es=Es, En=En, Esum=Esum, Ew=Ew, SB_=SB_, PB_=PB_, ZB_=ZB_, hd=hd, b=b: e.copy(out=PVt[:, 1, hd, 4 * b:4 * b + 4], in_=bk(PB_)[:, 0:4]), writes=[RB[PB_], R_PVt])
                c.op("act", lambda e, Es=Es, En=En, Esum=Esum, Ew=Ew, SB_=SB_, PB_=PB_, ZB_=ZB_, hd=hd, b=b: e.copy(out=Zt[:, 1, hd, 4 * b:4 * b + 4], in_=bk(ZB_)[:, 0:4]), writes=[RB[ZB_], R_Zt])
                fns = [lambda e, Es=Es, En=En, Esum=Esum, Ew=Ew, SB_=SB_, PB_=PB_, ZB_=ZB_, k=k, g=g, q4=q4: e.matmul(bk(SB_)[:, 128 + k * 4:128 + k * 4 + 4], lhsT=KwT[:, k, g, :], rhs=q4,
                                                          start=True, stop=True) for k in range(4)]
                fns.append(lambda e, Es=Es, En=En, Esum=Esum, Ew=Ew, SB_=SB_, PB_=PB_, ZB_=ZB_, g=g, q4=q4: e.matmul(bk(SB_)[0:64, 192:196], lhsT=kws[:, g, :], rhs=q4, start=True, stop=True))
                c.op("pe", fns, [R_KwT, R_qs], [RB[SB_]])
                c.op("act", lambda e, Es=Es, En=En, Esum=Esum, Ew=Ew, SB_=SB_, PB_=PB_, ZB_=ZB_: e.activation(out=Ew, in_=bk(SB_)[:, 128:144].rearrange("p (k l) -> p k l", k=4), func=AF.Exp, scale=SCALE),
                     writes=[RB[SB_], R_Ew])
                c.op("act", lambda e, Es=Es, En=En, Esum=Esum, Ew=Ew, SB_=SB_, PB_=PB_, ZB_=ZB_: e.activation(out=En[0:64], in_=bk(SB_)[0:64, 192:196], func=AF.Exp, scale=SCALE),
                     writes=[RB[SB_], R_En])
                c.op("dve", lambda e, Es=Es, En=En, Esum=Esum, Ew=Ew, SB_=SB_, PB_=PB_, ZB_=ZB_: e.tensor_tensor(out=Ew, in0=Ew, in1=winmask_s, op=ALU.mult), reads=[R_cb], writes=[R_Ew])
                c.op("dve", lambda e, Es=Es, En=En, Esum=Esum, Ew=Ew, SB_=SB_, PB_=PB_, ZB_=ZB_, b=b: e.tensor_tensor(out=En[0:64], in0=En[0:64], in1=newmask[0:64, b, :], op=ALU.mult),
                     reads=[R_cb], writes=[R_En])
                c.op("dve", lambda e, Es=Es, En=En, Esum=Esum, Ew=Ew, SB_=SB_, PB_=PB_, ZB_=ZB_: e.tensor_reduce(out=Esum, in_=Ew.rearrange("p k l -> p l k"), axis=AX.X, op=ALU.add),
                     reads=[R_Ew], writes=[R_Esum])
                fns = [lambda e, Es=Es, En=En, Esum=Esum, Ew=Ew, SB_=SB_, PB_=PB_, ZB_=ZB_, k=k, g=g: e.matmul(bk(PB_)[:, 0:4], lhsT=wrb[:, k, g * 128:(g + 1) * 128], rhs=Ew[:, k, :],
                                                   start=(k == 0), stop=False) for k in range(4)]
                fns.append(lambda e, Es=Es, En=En, Esum=Esum, Ew=Ew, SB_=SB_, PB_=PB_, ZB_=ZB_, g=g: e.matmul(bk(PB_)[:, 0:4], lhsT=vtok[1][0:64, 16, g * 128:(g + 1) * 128], rhs=En[0:64],
                                                   start=False, stop=True))
                fns.append(lambda e, Es=Es, En=En, Esum=Esum, Ew=Ew, SB_=SB_, PB_=PB_, ZB_=ZB_: e.matmul(bk(ZB_)[:, 0:4], lhsT=ones_f, rhs=Esum, start=True, stop=False))
                fns.append(lambda e, Es=Es, En=En, Esum=Esum, Ew=Ew, SB_=SB_, PB_=PB_, ZB_=ZB_: e.matmul(bk(ZB_)[:, 0:4], lhsT=ones_b[0:64, :], rhs=En[0:64], start=False, stop=True))
                c.op("pe", fns, [R_wrb, R_Ew, R_En, R_Esum, R_vtok[1], R_cb], [RB[PB_], RB[ZB_]])
                c.op("act", lambda e, Es=Es, En=En, Esum=Esum, Ew=Ew, SB_=SB_, PB_=PB_, ZB_=ZB_, hd=hd, b=b: e.copy(out=PVt[:, 2, hd, 4 * b:4 * b + 4], in_=bk(PB_)[:, 0:4]), writes=[RB[PB_], R_PVt])
                c.op("act", lambda e, Es=Es, En=En, Esum=Esum, Ew=Ew, SB_=SB_, PB_=PB_, ZB_=ZB_, hd=hd, b=b: e.copy(out=Zt[:, 2, hd, 4 * b:4 * b + 4], in_=bk(ZB_)[:, 0:4]), writes=[RB[ZB_], R_Zt])
        for br in range(3):
            fns = [lambda e, hd=hd, br=br: e.matmul(bk(7)[:, hd * 64:(hd + 1) * 64], lhsT=onehot[0:24, br * 8 + hd, :],
                                                   rhs=sigG[0:24, SEQ:NT], start=True, stop=True) for hd in range(8)]
            c.op("pe", fns, [R_cb, R_sigG], [RB[7]])
            c.op("dve", lambda e, br=br: e.tensor_scalar(out=Fs, in0=Zt[:, br], scalar1=1e-30, scalar2=None, op0=ALU.max),
                 reads=[R_Zt], writes=[R_Fs])
            c.op("dve", lambda e: e.reciprocal(out=Fs, in_=Fs), reads=[R_Fs], writes=[R_Fs])
            c.op("dve", lambda e: e.tensor_tensor(out=Fs, in0=bk(7)[:, :].rearrange("p (h n) -> p h n", h=8), in1=Fs, op=ALU.mult),
                 reads=[R_Fs], writes=[RB[7], R_Fs])
            c.op("dve", lambda e, br=br: e.tensor_tensor(out=Fs, in0=PVt[:, br], in1=Fs, op=ALU.mult), reads=[R_PVt, R_Fs], writes=[R_Fs])
            if br == 0:
                c.op("dve", lambda e: e.tensor_copy(out=osm, in_=Fs), reads=[R_Fs], writes=[R_osm])
            else:
                c.op("dve", lambda e: e.tensor_tensor(out=osm, in0=osm, in1=Fs, op=ALU.add), reads=[R_Fs, R_osm], writes=[R_osm])
        c.op("act", lambda e: e.copy(out=osb, in_=osm), reads=[R_osm], writes=[R_osb])
        for hd in range(8):
            c.dma("pool", lambda e, hd=hd: e.dma_start(out=oscr[1024 + hd * 128:1024 + (hd + 1) * 128, SEQ:NT], in_=osb[:, hd, :]),
                  reads=[R_osb])
        ar.top = nsa_top

    if stop_after < 4:
        return finish()
    c.barrier()
    ar.top = mark_persist
    hT = ar.alloc([16, 512], F32); R_hT = Res()
    actb = ar.alloc([16, 512], BF16); R_actb = Res()
    hff = ar.alloc([NFF, 512], BF16); R_hff = Res()
    sqb = hff[:, 0:16, :]; R_sqb = R_hff
    wsl = [ar.alloc([8192], BF16) for _ in range(3)]; R_wsl = [Res(), Res(), Res()]
    xs3 = [ar.alloc([D], F32) for _ in range(2)]; R_xs3 = [Res(), Res()]
    aext = ar.alloc([520], F32); R_aext = Res()
    cv1 = ar.alloc([512], F32); R_cv1 = Res()
    pst = ar.alloc([256], F32); R_pst = Res()
    geb = pst.bitcast(BF16); R_geb = R_pst
    sg = ar.alloc([512], F32); R_sg = Res()
    sd3 = ar.alloc([512], F32); R_sd3 = Res()
    pT3 = ar.alloc([2, 512], BF16); R_pT3 = Res()
    carry = ar.alloc([NFF, 2], F32); R_carry = Res()
    stf = ar.alloc([128], F32); R_stf = Res()
    cst = sd3; R_cst = R_sd3
    colg = ar.alloc([16], F32); colf = ar.alloc([16], F32); colb = ar.alloc([16], F32)
    colc = ar.alloc([4, NFF], F32); R_col = Res()
    rowst = ar.alloc([128], F32); R_rowst = Res()

    def load_cols(dst_ap, src1d, nrow):
        c.dma("sp", lambda e: e.dma_start(out=rowst[0:nrow], in_=src1d.rearrange("(t p) -> t p", p=128)), writes=[R_rowst])
        tposes([(bk(0)[:, 0:nrow], rowst[0:nrow, :], ident_f[0:nrow, 0:nrow])], [R_rowst, R_cf], [RB[0]])
        c.op("act", lambda e: e.copy(out=dst_ap, in_=bk(0)[:, 0:nrow]), writes=[RB[0], R_col])

    load_cols(colg, ffn_g, 16)
    load_cols(colf, fin_g, 16)
    load_cols(colb, pleb, 16)
    for j in range(3):
        load_cols(colc[:, j, :], convw[j, :], NFF)
    load_cols(colc[:, 3, :], convb, NFF)
    c.op("pool", lambda e: e.memset(carry, 0.0), writes=[R_carry])
    wsi = [0]

    def load_w(src2d, nkt, col0, ncols=128):
        sl = wsi[0] % 3
        wsi[0] += 1
        view = wsl[sl][:, 0:nkt * ncols].rearrange("p (k n) -> p k n", k=nkt)
        k0 = 0
        while k0 < nkt:
            kk = min(4, nkt - k0)
            wload("pool", view[:, k0:k0 + kk, :],
                  src2d[128 * k0:128 * (k0 + kk), col0:col0 + ncols].rearrange("(kt p) n -> p kt n", p=128), R_wsl[sl])
            k0 += kk
        return sl, view

    xsi = [0]
    for ti, (n0, N) in enumerate(NTILES):
        smp = (ti == 4)
        c.dma("sp", lambda e, N=N, n0=n0: e.dma_start(
            out=actb[:, :, 0:N], in_=oscr[:, n0:n0 + N].rearrange("(kt p) n -> p kt n", p=128)), writes=[R_actb])
        nblk = (N + 127) // 128
        for bi in range(nblk):
            rows = min(128, N - bi * 128)
            xi = xsi[0] % 2
            xsi[0] += 1
            srcx = xs[:, :] if smp else xp[n0 + bi * 128:n0 + bi * 128 + 128, :]
            c.dma("sp", lambda e, N=N, xi=xi, srcx=srcx, rows=rows: e.dma_start(out=xs3[xi][0:rows], in_=srcx),
                  writes=[R_xs3[xi]])
            for q4 in range(4):
                b = q4 % 2
                tposes([(bk(b)[:, j * 128:j * 128 + rows], xs3[xi][0:rows, (4 * q4 + j) * 128:(4 * q4 + j + 1) * 128],
                         ident_f[0:rows, 0:rows]) for j in range(4)], [R_xs3[xi], R_cf], [RB[b]])
                c.op("act", lambda e, N=N, b=b, q4=q4, bi=bi, rows=rows: e.copy(
                    out=hT[:, 4 * q4:4 * q4 + 4, bi * 128:bi * 128 + rows],
                    in_=bk(b)[:, :].rearrange("p (j t) -> p j t", j=4)[:, :, 0:rows]),
                    writes=[RB[b], R_hT])
            srcp = ps_[:, :] if smp else pp[n0 + bi * 128:n0 + bi * 128 + 128, :]
            c.dma("sp", lambda e, N=N, srcp=srcp, rows=rows: e.dma_start(out=pst[0:rows], in_=srcp), writes=[R_pst])
            tposes([(bk(2)[:, j * 128:j * 128 + rows], pst[0:rows, j * 128:(j + 1) * 128], ident_f[0:rows, 0:rows])
                    for j in range(2)], [R_pst, R_cf], [RB[2]])
            c.op("act", lambda e, N=N, bi=bi, rows=rows: e.copy(
                out=pT3[:, :, bi * 128:bi * 128 + rows],
                in_=bk(2)[:, 0:256].rearrange("p (j t) -> p j t", j=2)[:, :, 0:rows]), writes=[RB[2], R_pT3])
        for m in range(16):
            if m % 4 == 0:
                sl, wv_ = load_w(w_out, 16, m * 128, 512)
            mo = (m % 4) * 128
            b = 4 + (m % 2)
            mms(bk(b)[:, 0:N], [(wv_[:, kt, mo:mo + 128], actb[:, kt, 0:N]) for kt in range(16)], [R_wsl[sl], R_actb], [RB[b]])
            c.op("dve", lambda e, N=N, b=b, m=m: e.tensor_tensor(out=hT[:, m, 0:N], in0=bk(b)[:, 0:N], in1=hT[:, m, 0:N],
                                                                 op=ALU.add), writes=[RB[b], R_hT])

        def rms3(gcol, out_bf):
            c.op("act", lambda e, N=N, n0=n0, smp=smp: e.activation(out=sqb[:, :, 0:N], in_=hT[:, :, 0:N], func=AF.Square),
                 reads=[R_hT], writes=[R_sqb])
            mms(bk(6)[:, 0:N], [(ones_b, sqb[:, m, 0:N]) for m in range(16)], [R_sqb, R_cb], [RB[6]])
            c.op("act", lambda e, N=N, n0=n0, smp=smp: e.activation(out=sd3[:, 0:N], in_=bk(6)[:, 0:N], func=AF.Sqrt, scale=1.0 / D, bias=EPS),
                 writes=[RB[6], R_sd3])
            c.op("dve", lambda e, N=N, n0=n0, smp=smp: e.reciprocal(out=sd3[:, 0:N], in_=sd3[:, 0:N]), reads=[R_sd3], writes=[R_sd3])
            for m in range(16):
                if out_bf:
                    c.op("dve", lambda e, N=N, m=m: e.scalar_tensor_tensor(
                        out=actb[:, m, 0:N], in0=hT[:, m, 0:N], scalar=gcol[:, m:m + 1], in1=sd3[:, 0:N],
                        op0=ALU.mult, op1=ALU.mult), reads=[R_hT, R_sd3, R_col], writes=[R_actb])
                else:
                    c.op("dve", lambda e, N=N, m=m: e.scalar_tensor_tensor(
                        out=hT[:, m, 0:N], in0=hT[:, m, 0:N], scalar=gcol[:, m:m + 1], in1=sd3[:, 0:N],
                        op0=ALU.mult, op1=ALU.mult), reads=[R_sd3, R_col], writes=[R_hT])

        rms3(colg, True)
        for f in range(NFF):
            if f % 4 == 0:
                nf_ = min(4, NFF - f)
                sa, wa_ = load_w(w_up, 16, f * 128, nf_ * 128)
                su, wu_ = load_w(w_up, 16, DFF + f * 128, nf_ * 128)
            fo = (f % 4) * 128
            mms(bk(0)[:, 0:N], [(wa_[:, kt, fo:fo + 128], actb[:, kt, 0:N]) for kt in range(16)], [R_wsl[sa], R_actb], [RB[0]])
            mms(bk(1)[:, 0:N], [(wu_[:, kt, fo:fo + 128], actb[:, kt, 0:N]) for kt in range(16)], [R_wsl[su], R_actb], [RB[1]])
            w0 = colc[:, 0, f:f + 1]; w1 = colc[:, 1, f:f + 1]; w2 = colc[:, 2, f:f + 1]; bb_ = colc[:, 3, f:f + 1]
            if not smp:
                c.op("act", lambda e, N=N, n0=n0, smp=smp: e.copy(out=aext[:, 2:2 + N], in_=bk(0)[:, 0:N]), writes=[RB[0], R_aext])
                c.op("act", lambda e, N=N, f=f: e.copy(out=aext[:, 0:2], in_=carry[:, f, :]), reads=[R_carry], writes=[R_aext])
                c.op("act", lambda e, N=N, f=f: e.copy(out=carry[:, f, :], in_=aext[:, N:N + 2]), reads=[R_aext],
                     writes=[R_carry])
                v0, v1, v2, vo = aext[:, 0:N], aext[:, 1:N + 1], aext[:, 2:N + 2], cv1[:, 0:N]
            else:
                a6 = aext[:, 0:96].rearrange("p (b s) -> p b s", s=6)
                c.op("act", lambda e, N=N, a6=a6: e.copy(out=a6[:, :, 2:6], in_=bk(0)[:, 0:64].rearrange("p (b l) -> p b l", l=4)),
                     writes=[RB[0], R_aext])
                c.dma("sp", lambda e, N=N, f=f: e.dma_start(out=stf[0:32], in_=sconv[:, f * 128:(f + 1) * 128]), writes=[R_stf])
                tposes([(bk(2)[:, 0:32], stf[0:32, :], ident_f[0:32, 0:32])], [R_stf, R_cf], [RB[2]])
                c.op("act", lambda e, N=N, a6=a6: e.copy(out=a6[:, :, 0:2], in_=bk(2)[:, 0:32].rearrange("p (b s) -> p b s", s=2)),
                     writes=[RB[2], R_aext])
                v0, v1, v2 = a6[:, :, 0:4], a6[:, :, 1:5], a6[:, :, 2:6]
                vo = cv1[:, 0:64].rearrange("p (b l) -> p b l", l=4)
            c.op("dve", lambda e, N=N, v0=v0, vo=vo, w0=w0, bb_=bb_: e.tensor_scalar(out=vo, in0=v0, scalar1=w0, scalar2=bb_,
                                                                               op0=ALU.mult, op1=ALU.add),
                 reads=[R_aext, R_col], writes=[R_cv1])
            c.op("dve", lambda e, N=N, v1=v1, vo=vo, w1=w1: e.scalar_tensor_tensor(out=vo, in0=v1, scalar=w1, in1=vo,
                                                                             op0=ALU.mult, op1=ALU.add),
                 reads=[R_aext, R_col, R_cv1], writes=[R_cv1])
            c.op("dve", lambda e, N=N, v2=v2, vo=vo, w2=w2: e.scalar_tensor_tensor(out=vo, in0=v2, scalar=w2, in1=vo,
                                                                             op0=ALU.mult, op1=ALU.add),
                 reads=[R_aext, R_col, R_cv1], writes=[R_cv1])
            c.op("act", lambda e, N=N, n0=n0, smp=smp: e.activation(out=geb[:, 0:N], in_=cv1[:, 0:N], func=AF.Gelu_apprx_tanh),
                 reads=[R_cv1], writes=[R_geb])
            c.op("dve", lambda e, N=N, f=f: e.tensor_tensor(out=hff[:, f, 0:N], in0=bk(1)[:, 0:N], in1=geb[:, 0:N], op=ALU.mult),
                 reads=[R_geb], writes=[RB[1], R_hff])
            if ti == 3 or smp:
                g4 = f % 4
                if smp:
                    c.op("act", lambda e, N=N, n0=n0, smp=smp: e.copy(
                        out=sg[:, 0:32].rearrange("p (b s) -> p b s", s=2),
                        in_=aext[:, 0:96].rearrange("p (b s) -> p b s", s=6)[:, :, 4:6]), reads=[R_aext], writes=[R_sg])
                    src_t = sg[:, 0:32]
                    nr = 32
                else:
                    src_t = aext[:, N:N + 2]
                    nr = 2
                tposes([(bk(3)[0:nr, g4 * 128:(g4 + 1) * 128], src_t, ident_f)], [R_aext, R_sg, R_cf], [RB[3]])
                if g4 == 3 or f == NFF - 1:
                    f0 = f - g4
                    wd = (g4 + 1) * 128
                    c.op("act", lambda e, N=N, nr=nr, wd=wd: e.copy(out=cst[0:nr, 0:wd], in_=bk(3)[0:nr, 0:wd]),
                         writes=[RB[3], R_cst])
                    dstc = (convs.rearrange("b s n -> (b s) n") if smp else convp)[:, f0 * 128:f0 * 128 + wd]
                    out_toks.append(c.dma("pool", lambda e, N=N, dstc=dstc, nr=nr, wd=wd: e.dma_start(out=dstc, in_=cst[0:nr, 0:wd]),
                                          reads=[R_cst]))
        for m in range(16):
            sl, wd_ = load_w(w_dn, NFF, m * 128)
            b = 4 + (m % 2)
            mms(bk(b)[:, 0:N], [(wd_[:, kt, :], hff[:, kt, 0:N]) for kt in range(NFF)], [R_wsl[sl], R_hff], [RB[b]])
            c.op("dve", lambda e, N=N, b=b, m=m: e.tensor_tensor(out=hT[:, m, 0:N], in0=bk(b)[:, 0:N], in1=hT[:, m, 0:N],
                                                            op=ALU.add), writes=[RB[b], R_hT])
        c.op("act", lambda e, N=N, n0=n0, smp=smp: e.copy(out=actb[:, :, 0:N], in_=hT[:, :, 0:N]), reads=[R_hT], writes=[R_actb])
        for m in range(16):
            if m % 4 == 0:
                sl, wg_ = load_w(pleg, 16, m * 128, 512)
                sp_, wp_ = load_w(plep, 2, m * 128, 512)
            mo = (m % 4) * 128
            b = 4 + (m % 2)
            mms(bk(b)[:, 0:N], [(wg_[:, kt, mo:mo + 128], actb[:, kt, 0:N]) for kt in range(16)], [R_wsl[sl], R_actb], [RB[b]])
            c.op("act", lambda e, N=N, b=b, m=m: e.activation(out=sg[:, 0:N], in_=bk(b)[:, 0:N], func=AF.Sigmoid,
                                                         bias=colb[:, m:m + 1], scale=1.0),
                 reads=[R_col], writes=[RB[b], R_sg])
            mms(bk(7)[:, 0:N], [(wp_[:, kt, mo:mo + 128], pT3[:, kt, 0:N]) for kt in range(2)], [R_wsl[sp_], R_pT3], [RB[7]])
            c.op("dve", lambda e, N=N, n0=n0, smp=smp: e.tensor_tensor(out=sg[:, 0:N], in0=bk(7)[:, 0:N], in1=sg[:, 0:N], op=ALU.mult),
                 reads=[R_sg], writes=[RB[7], R_sg])
            c.op("dve", lambda e, N=N, m=m: e.tensor_tensor(out=hT[:, m, 0:N], in0=hT[:, m, 0:N], in1=sg[:, 0:N], op=ALU.add),
                 reads=[R_sg], writes=[R_hT])
        rms3(colf, False)
        for bi in range(nblk):
            rows = min(128, N - bi * 128)
            xi = xsi[0] % 2
            xsi[0] += 1
            for q4 in range(4):
                b = q4 % 2
                tposes([(bk(b)[0:rows, j * 128:(j + 1) * 128], hT[:, 4 * q4 + j, bi * 128:bi * 128 + rows], ident_f)
                        for j in range(4)], [R_hT, R_cf], [RB[b]])
                c.op("act", lambda e, N=N, b=b, q4=q4, xi=xi, rows=rows: e.copy(out=xs3[xi][0:rows, 512 * q4:512 * q4 + 512],
                                                                          in_=bk(b)[0:rows, :]),
                     writes=[RB[b], R_xs3[xi]])
            dsty = y_s[:, :] if smp else y_p[n0 + bi * 128:n0 + bi * 128 + 128, :]
            out_toks.append(c.dma("pool", lambda e, N=N, dsty=dsty, xi=xi, rows=rows: e.dma_start(out=dsty, in_=xs3[xi][0:rows]),
                                  reads=[R_xs3[xi]]))

    return finish()


_CACHE = {}


def kernel(**inp):
    f32 = np.float32
    cbm, cfm = CONSTS[0], CONSTS[1]
    if "nc" not in _CACHE:
        import os
        _CACHE["nc"] = build(int(os.environ.get("STOP_AFTER", "99")))
    nc = _CACHE["nc"]
    A = lambda a: np.ascontiguousarray(a)
    shared = dict(
        attn_g=A(inp["attn_norm_g"][0]), w_in=A(inp["w_in"][0]), wa2=A(inp["gla_w_a2"][0]), b_a=A(inp["gla_b_a"][0]),
        glag=A(inp["gla_norm_g"][0]), cb=cbm, cf=cfm,
        cck=A(inp["cache_cmp_k"][0].reshape(2560 * 8, 4096)), ccv=A(inp["cache_cmp_v"][0].reshape(2560 * 8, 4096)),
        csk=A(inp["cache_slc_k"][0].reshape(2560 * 8, 4096)), csv=A(inp["cache_slc_v"][0].reshape(2560 * 8, 4096)),
        wk1=A(inp["cmp_wk1"][0]), wk2=A(inp["cmp_wk2"][0]), wv1=A(inp["cmp_wv1"][0]), wv2=A(inp["cmp_wv2"][0]),
        w_out=A(inp["w_out"][0]), ffn_g=A(inp["ffn_norm_g"][0]),
        w_up=A(inp["ffn_w_up"][0]), convw=A(inp["ffn_conv_w"][0]), convb=A(inp["ffn_conv_b"][0]),
        w_dn=A(inp["ffn_w_down"][0]), plep=A(inp["ple_w_proj"][0]), pleg=A(inp["ple_w_gate"][0]),
        pleb=A(inp["ple_b_gate"][0]), fin_g=A(inp["final_norm_g"]))
    in_maps = []
    for cix in range(8):
        b = cix % 4
        sl = slice(16 * cix, 16 * cix + 16)
        m = dict(shared)
        m.update(
            xp=A(inp["x_prompt"][b]), xs=A(inp["x_sample"][sl].reshape(NS, D)),
            swk=A(inp["state_win_k"][0, sl].reshape(16, 512, 256)), swv=A(inp["state_win_v"][0, sl].reshape(16, 512, 256)),
            sgla=A(inp["state_gla"][0, sl]), pt=A(inp["page_table"][sl].astype(np.int32)),
            pp=A(inp["p_prompt"][0, b]), ps=A(inp["p_sample"][0, sl].reshape(NS, 256)),
            sconv=A(inp["state_ffn_conv"][0, sl].reshape(32, DFF)))
        in_maps.append(m)
    res = run_bass_kernel_spmd(nc, in_maps, core_ids=list(range(8)))
    R = res.results
    _CACHE["raw"] = R

    def pcat(name, shape):
        return np.stack([np.asarray(R[b][name]).reshape(shape) for b in range(4)], 0)[None]

    def scat(name, shape):
        return np.concatenate([np.asarray(R[cx][name]).reshape((16,) + shape) for cx in range(8)], 0)[None]

    y_prompt = np.stack([np.asarray(R[b]["y_p"]) for b in range(4)], 0)
    y_sample = np.concatenate([np.asarray(R[cx]["y_s"]).reshape(16, 4, D) for cx in range(8)], 0)
    outs = [y_prompt, y_sample]
    for n in ("ckp", "cvp", "skp", "svp"):
        outs.append(pcat(n, (SEQ, 2, 128)))
    outs.append(pcat("wkp", (512, 2, 128)))
    outs.append(pcat("wvp", (512, 2, 128)))
    outs.append(pcat("glap", (4, 128, 256)))
    outs.append(pcat("convp", (2, DFF)))
    for n in ("cks", "cvs", "sks", "svs"):
        outs.append(scat(n, (4, 2, 128)))
    outs.append(scat("wks", (512, 2, 128)))
    outs.append(scat("wvs", (512, 2, 128)))
    outs.append(scat("glas", (4, 128, 256)))
    outs.append(scat("convs", (2, DFF)))
    return tuple(o.astype(f32) for o in outs)
```

```python
import numpy as np
import concourse.bass as bass
import concourse.mybir as mybir
from concourse.bass_utils import run_bass_kernel_spmd
from contextlib import ExitStack

F32 = mybir.dt.float32
BF16 = mybir.dt.bfloat16
I32 = mybir.dt.int32
AF = mybir.ActivationFunctionType
ALU = mybir.AluOpType
AX = mybir.AxisListType

ENGS = ("pe", "act", "dve", "pool", "sp")
NDMA = 12

D = 2048
SEQ = 2048
NS = 64
NT = SEQ + NS
NB_S = 16
DFF = 5504
NFF = 43
OFF = dict(gq=0, gk=512, gv=1024, gr=2048, ga=3072, nq=3088, kc=4112, vc=4368,
           ks=4624, vs=4880, kw=5136, vw=5392, ng=5648)
NIN = 5672
EPS = 1e-6
SCALE = 128 ** -0.5
NTILES = [(0, 512), (512, 512), (1024, 512), (1536, 512), (2048, 64)]


class Res:
    __slots__ = ("w", "r", "name")

    def __init__(self, name=""):
        self.w = None
        self.r = []
        self.name = name


class Ctx:
    def __init__(self, nc):
        self.nc = nc
        self.streams = {e: [] for e in ENGS}
        self.count = {e: 0 for e in ENGS}
        self.waited = {e: {} for e in ENGS}
        self.dma_cnt = {}
        self.dma_rr = {e: 0 for e in ENGS}
        self.last_dma = {}

    def _need(self, eng, tok, waits):
        if tok is None:
            return
        key, val = tok
        if self.waited[eng].get(key, 0) >= val:
            return
        self.waited[eng][key] = val
        waits.append((key, val))

    def _deps(self, eng, reads, writes, is_dma):
        waits = []
        for r in reads:
            self._need(eng, r.w, waits)
        for w in writes:
            if w.w is not None and (is_dma or w.w[0] != eng):
                self._need(eng, w.w, waits)
            for t in w.r:
                if is_dma or t[0] != eng:
                    self._need(eng, t, waits)
        return waits

    def _commit(self, tok, reads, writes):
        for r in reads:
            if len(r.r) > 64:
                r.r = r.r[-48:]
            r.r.append(tok)
        for w in writes:
            w.w = tok
            w.r = []

    def op(self, eng, fns, reads=(), writes=()):
        if not isinstance(fns, (list, tuple)):
            fns = [fns]
        waits = self._deps(eng, reads, writes, False)
        self.count[eng] += 1
        tok = (eng, self.count[eng])
        self.streams[eng].append(("op", waits, list(fns), tok))
        self._commit(tok, reads, writes)
        return tok

    def dma(self, eng, fn, reads=(), writes=()):
        waits = self._deps(eng, reads, writes, True)
        i = self.dma_rr[eng]
        self.dma_rr[eng] = (i + 1) % NDMA
        key = ("dma", eng, i)
        n = self.dma_cnt.get(key, 0)
        if n > 0:
            self._need(eng, (key, 16 * n), waits)
        self.dma_cnt[key] = n + 1
        tok = (key, 16 * (n + 1))
        self.last_dma[key] = tok
        self.streams[eng].append(("dma", waits, [fn], tok))
        self._commit(tok, reads, writes)
        return tok

    def barrier(self):
        toks = [(e, self.count[e]) for e in ENGS if self.count[e] > 0]
        toks += list(self.last_dma.values())
        for e in ENGS:
            waits = []
            for t in toks:
                if t[0] != e:
                    self._need(e, t, waits)
            if waits:
                self.streams[e].append(("wait", waits, [], None))

    def emit(self):
        nc = self.nc
        flagged = {e: set() for e in ENGS}
        for e in ENGS:
            for kind, waits, fns, tok in self.streams[e]:
                for key, val in waits:
                    if key in flagged:
                        flagged[key].add(val)
        remap = {}
        for e in ENGS:
            n = 0
            for kind, waits, fns, tok in self.streams[e]:
                if kind == "op":
                    if tok[1] in flagged[e] or tok[1] == self.count[e]:
                        n += 1
                        remap[tok] = n
        self.n_inc = {e: sum(1 for t in remap if t[0] == e) for e in ENGS}
        with ExitStack() as es:
            sems = {}
            for e in ENGS:
                sems[e] = es.enter_context(nc.semaphore("s_" + e))
            for key in self.dma_cnt:
                sems[key] = es.enter_context(nc.semaphore("d_%s_%d" % (key[1], key[2])))
            block = es.enter_context(nc.Block())

            def run(eng_key):
                def body(eng):
                    for kind, waits, fns, tok in self.streams[eng_key]:
                        for key, val in waits:
                            if key in flagged:
                                val = remap[(key, val)]
                            eng.wait_ge(sems[key], val)
                        inst = None
                        for f in fns:
                            inst = f(eng)
                        if kind == "op":
                            if tok in remap:
                                inst.then_inc(sems[eng_key], 1)
                        elif kind == "dma":
                            inst.then_inc(sems[tok[0]], 16)
                return body

            block.tensor(run("pe"))
            block.scalar(run("act"))
            block.vector(run("dve"))
            block.gpsimd(run("pool"))
            block.sync(run("sp"))


class Arena:
    def __init__(self, t, nwords):
        self.t = t
        self.n = nwords
        self.top = 0

    def alloc(self, free_shape, dt, parts=128):
        n = 1
        for s in free_shape:
            n *= s
        esz = 4 if dt in (F32, I32) else 2
        words = (n * esz + 3) // 4
        words = (words + 7) // 8 * 8
        off = self.top
        self.top += words
        assert self.top <= self.n, "SBUF arena overflow: %d > %d words" % (self.top, self.n)
        ap = self.t[0:parts, off:off + words]
        if dt != F32:
            ap = ap.bitcast(dt)
        ap = ap[:, 0:n]
        if len(free_shape) == 2:
            ap = ap.rearrange("p (a b) -> p a b", a=free_shape[0])
        elif len(free_shape) == 3:
            ap = ap.rearrange("p (a b c) -> p a b c", a=free_shape[0], b=free_shape[1])
        elif len(free_shape) == 4:
            ap = ap.rearrange("p (a b c d) -> p a b c d", a=free_shape[0], b=free_shape[1], c=free_shape[2])
        return ap


def make_consts():
    k = np.arange(128)
    cb = {}
    cf = {}
    cb["ident"] = np.eye(128)
    cb["ones"] = np.ones((128, 128))
    cb["tri"] = (k[:, None] <= k[None, :]).astype(np.float64)
    q = np.arange(512)
    caus = np.zeros((128, 4, 512))
    for r in range(4):
        caus[:, r, :] = ((128 * r + k)[:, None] <= q[None, :])
    cb["caus"] = caus.reshape(128, -1)
    wm = np.zeros((128, 8, 512))
    for r in range(8):
        key = 128 * (r - 4) + k
        wm[:, r, :] = (key[:, None] <= q[None, :]) & (key[:, None] > q[None, :] - 512)
    cb["wmask"] = wm.reshape(128, -1)
    ex = np.zeros((128, 16, 128))
    for kt in range(16):
        for kk in range(128):
            ex[2 * kt + kk // 64, kt, kk] = 1
    cb["expand"] = ex.reshape(128, -1)
    p = np.arange(128)
    exs = np.zeros((128, 128))
    blk_of_p = 2 * (p % 16) + (p // 16) // 4
    exs[blk_of_p, p] = 1
    cb["expand_s"] = exs
    oh = np.zeros((128, 24, 128))
    for r in range(24):
        oh[r, r, :] = 1
    cb["onehot"] = oh.reshape(128, -1)
    n = np.arange(127)

    def pairing(nblk):
        m = np.zeros((128, nblk + 1))
        for j in range(nblk):
            m[:127, j] = ((4 * j <= n + 1) & (n + 1 <= 4 * j + 3)).astype(float) + \
                         ((4 * j <= n) & (n <= 4 * j + 3)).astype(float)
        m[:127, nblk] = 1
        return m
    cb["mp"] = pairing(32)
    cb["ms"] = pairing(33)
    nm = np.zeros((128, 16, 4))
    for bb in range(16):
        for l2 in range(4):
            for l in range(4):
                if l2 <= l:
                    nm[bb * 4 + l2, bb, l] = 1
    cb["newmask"] = nm.reshape(128, -1)
    wms = np.zeros((128, 4, 4))
    for blk in range(4):
        for l in range(4):
            wms[:, blk, l] = (128 * blk + k >= l + 1)
    cb["winmask_s"] = wms.reshape(128, -1)
    cm = np.zeros((128, 2048))
    for nn in range(127):
        cm[nn, :] = (16 * nn + 31 <= np.arange(2048))
    cb["cmask"] = cm
    tri_s = np.zeros((128, 128))
    for j in range(64):
        for i in range(64):
            if j // 4 == i // 4 and j <= i:
                tri_s[j, i] = 1
    cb["tri_s"] = tri_s
    bm = np.zeros((128, 16))
    for j in range(64):
        bm[j, j // 4] = 1
    cf["bm_s"] = bm
    cf["ident"] = np.eye(128)
    cf["ones"] = np.ones((128, 128))
    cf["tri"] = cb["tri"]
    cf["tri_s"] = tri_s
    A = np.zeros((128, 16, 32))
    B = np.zeros((128, 16, 32))
    for s in range(16):
        qq = s * 128 + k
        cur = qq // 64
        blk = np.arange(32)[None, :]
        valid = blk <= cur[:, None]
        forced = (blk == 0) | (blk == cur[:, None]) | (blk == cur[:, None] - 1)
        A[:, s, :] = valid & ~forced
        B[:, s, :] = np.where(valid & forced, 1e6, np.where(valid, 0.0, -1.0))
    cf["A_p"] = A.reshape(128, -1)
    cf["B_p"] = B.reshape(128, -1)
    As = np.ones((128, 33))
    Bs = np.zeros((128, 33))
    for j in (0, 31, 32):
        As[:, j] = 0
        Bs[:, j] = 1e6
    cf["A_s"] = As
    cf["B_s"] = Bs
    cf["cvec"] = (p // 16).astype(np.float64)[:, None]
    lay_b, lay_f = {}, {}
    o = 0
    for kk, v in cb.items():
        lay_b[kk] = (o, v.shape[1])
        o += v.shape[1]
    nb = o
    o = 0
    for kk, v in cf.items():
        lay_f[kk] = (o, v.shape[1])
        o += v.shape[1]
    nf = o
    cbm = np.concatenate([v for v in cb.values()], axis=1).astype(np.float32)
    cfm = np.concatenate([v for v in cf.values()], axis=1).astype(np.float32)
    return cbm, cfm, lay_b, lay_f, nb, nf


CONSTS = make_consts()


def build(stop_after=99):
    nc = bass.Bass("TRN2", target_bir_lowering=False)
    cbm, cfm, lay_b, lay_f, nb, nf = CONSTS

    def din(name, shape, dt=F32):
        return nc.dram_tensor(name, list(shape), dt, kind="ExternalInput").ap()

    def dout(name, shape, dt=F32):
        return nc.dram_tensor(name, list(shape), dt, kind="ExternalOutput").ap()

    xp = din("xp", [SEQ, D]); xs = din("xs", [NS, D])
    swk = din("swk", [16, 512, 256]); swv = din("swv", [16, 512, 256])
    sgla = din("sgla", [16, 4, 128, 256])
    attn_g = din("attn_g", [D]); w_in = din("w_in", [D, NIN])
    wa2 = din("wa2", [16, 512]); b_a = din("b_a", [512]); glag = din("glag", [256])
    cck = din("cck", [2560 * 8, 4096]); ccv = din("ccv", [2560 * 8, 4096])
    csk = din("csk", [2560 * 8, 4096]); csv = din("csv", [2560 * 8, 4096])
    pt = din("pt", [16, 16], I32)
    wk1 = din("wk1", [32, 128, 128]); wk2 = din("wk2", [128, 128])
    wv1 = din("wv1", [32, 128, 128]); wv2 = din("wv2", [128, 128])
    pp = din("pp", [SEQ, 256]); ps_ = din("ps", [NS, 256]); sconv = din("sconv", [32, DFF])
    w_out = din("w_out", [D, D]); ffn_g = din("ffn_g", [D]); w_up = din("w_up", [D, 2 * DFF])
    convw = din("convw", [3, DFF]); convb = din("convb", [DFF]); w_dn = din("w_dn", [DFF, D])
    plep = din("plep", [256, D]); pleg = din("pleg", [D, D]); pleb = din("pleb", [D])
    fin_g = din("fin_g", [D])
    cb_d = din("cb", [128, nb]); cf_d = din("cf", [128, nf])

    y_p = dout("y_p", [SEQ, D]); y_s = dout("y_s", [NS, D])
    kvp = [dout(n, [SEQ, 256]) for n in ("ckp", "cvp", "skp", "svp")]
    wkp = dout("wkp", [512, 256]); wvp = dout("wvp", [512, 256])
    glap = dout("glap", [4, 128, 256]); convp = dout("convp", [2, DFF])
    kvs = [dout(n, [NS, 256]) for n in ("cks", "cvs", "sks", "svs")]
    wks = dout("wks", [16, 512, 256]); wvs = dout("wvs", [16, 512, 256])
    glas = dout("glas", [16, 4, 128, 256]); convs = dout("convs", [16, 2, DFF])
    oscr = nc.dram_tensor("oscr", [D, NT], BF16, kind="Internal").ap()

    c = Ctx(nc)
    out_toks = []
    es = ExitStack()
    TOTAL_WORDS = 52000
    arena_t = es.enter_context(nc.sbuf_tensor("arena", [128, TOTAL_WORDS], F32))
    ar = Arena(arena_t, TOTAL_WORDS)
    banks = [es.enter_context(nc.psum_tensor("bank%d" % i, [128, 512], F32)) for i in range(8)]
    RB = [Res("bank%d" % i) for i in range(8)]

    def bk(i):
        return banks[i][:]

    def bkb(i):
        return banks[i][:].bitcast(BF16)

    cbt = ar.alloc([nb], BF16); R_cb = Res("cb")
    cft = ar.alloc([nf], F32); R_cf = Res("cf")
    c.dma("pool", lambda e: e.dma_start(out=cbt, in_=cb_d[:, :]), writes=[R_cb])
    c.dma("sp", lambda e: e.dma_start(out=cft, in_=cf_d[:, :]), writes=[R_cf])

    def CB(name, parts=128):
        o, n = lay_b[name]
        return cbt[0:parts, o:o + n]

    def CF(name, parts=128):
        o, n = lay_f[name]
        return cft[0:parts, o:o + n]

    ident_b = CB("ident"); ones_b = CB("ones")
    ident_f = CF("ident"); ones_f = CF("ones")

    def mms(out, pairs, reads, writes):
        n = len(pairs)
        fns = [(lambda e, i=i, l=l, r=r: e.matmul(out, lhsT=l, rhs=r, start=(i == 0), stop=(i == n - 1)))
               for i, (l, r) in enumerate(pairs)]
        return c.op("pe", fns, reads, writes)

    def tposes(items, reads, writes):
        fns = [(lambda e, o=o, i=i, d=d: e.transpose(out=o, in_=i, identity=d)) for (o, i, d) in items]
        return c.op("pe", fns, reads, writes)

    def wload(eng, dst, src, Rw):
        return c.dma(eng, lambda e: e.dma_start(out=dst, in_=src), writes=[Rw])

    mark_persist = ar.top

    def finish():
        waits = []
        for t in out_toks:
            c._need("sp", t, waits)
        c.streams["sp"].append(("wait", waits, [], None))
        c.barrier()
        c.emit()
        es.close()
        return nc

    xnT = ar.alloc([16, NT], BF16)
    R_xnT = [Res("xnT%d" % i) for i in range(17)]
    mark_p2 = ar.top
    gbA = ar.alloc([D], F32); R_gbA = Res()
    c.dma("sp", lambda e: e.dma_start(out=gbA, in_=attn_g.partition_broadcast(128)), writes=[R_gbA])
    xst = [ar.alloc([D], F32) for _ in range(2)]; R_xst = [Res() for _ in range(2)]
    xnb = [ar.alloc([D], BF16) for _ in range(2)]; R_xnb = [Res() for _ in range(2)]
    junk = ar.alloc([D], BF16); R_junk = Res()
    ssq = [ar.alloc([1], F32) for _ in range(2)]; R_ssq = [Res() for _ in range(2)]
    rsd = [ar.alloc([1], F32) for _ in range(2)]; R_rsd = [Res() for _ in range(2)]

    def rms_tile(tt, src_rows, rows):
        s = tt % 2
        c.dma("sp", lambda e: e.dma_start(out=xst[s][0:rows], in_=src_rows), writes=[R_xst[s]])
        c.op("pool", lambda e: e.memset(ssq[s][0:rows], 0.0), writes=[R_ssq[s]])
        c.op("act", lambda e: e.activation(out=junk[0:rows], in_=xst[s][0:rows], func=AF.Square,
                                           accum_out=ssq[s][0:rows]),
             reads=[R_xst[s]], writes=[R_junk, R_ssq[s]])
        c.op("act", lambda e: e.activation(out=rsd[s][0:rows], in_=ssq[s][0:rows], func=AF.Sqrt,
                                           scale=1.0 / D, bias=EPS),
             reads=[R_ssq[s]], writes=[R_rsd[s]])
        c.op("dve", lambda e: e.reciprocal(out=rsd[s][0:rows], in_=rsd[s][0:rows]),
             reads=[R_rsd[s]], writes=[R_rsd[s]])
        c.op("dve", lambda e: e.scalar_tensor_tensor(out=xnb[s][0:rows], in0=xst[s][0:rows],
                                                     scalar=rsd[s][0:rows, 0:1], in1=gbA[0:rows],
                                                     op0=ALU.mult, op1=ALU.mult),
             reads=[R_xst[s], R_rsd[s], R_gbA], writes=[R_xnb[s]])
        b0 = 2 * s
        pv = [bkb(b0).rearrange("p (a b) -> p a b", a=8), bkb(b0 + 1).rearrange("p (a b) -> p a b", a=8)]
        items = [(pv[k // 8][:, k % 8, 0:rows], xnb[s][0:rows, k * 128:(k + 1) * 128], ident_b[0:rows, 0:rows])
                 for k in range(16)]
        tposes(items, [R_xnb[s], R_cb], [RB[b0], RB[b0 + 1]])
        for hh in range(2):
            eng = "act" if hh == 0 else "dve"
            if eng == "act":
                c.op("act", lambda e, hh=hh: e.copy(out=xnT[:, 8 * hh:8 * hh + 8, tt * 128:tt * 128 + rows],
                                                    in_=pv[hh][:, :, 0:rows]),
                     reads=[RB[b0 + hh]], writes=[R_xnT[tt]])
            else:
                c.op("dve", lambda e, hh=hh: e.tensor_copy(out=xnT[:, 8 * hh:8 * hh + 8, tt * 128:tt * 128 + rows],
                                                           in_=pv[hh][:, :, 0:rows]),
                     reads=[RB[b0 + hh]], writes=[R_xnT[tt]])

    for tt in range(16):
        rms_tile(tt, xp[tt * 128:(tt + 1) * 128, :], 128)
    rms_tile(16, xs[:, :], 64)

    c.barrier()
    ar.top = mark_p2
    wq = [ar.alloc([16, 128], BF16) for _ in range(2)]; R_wq = [Res(), Res()]
    wqi = [0]

    def proj_fm(col0, ncols, dests, Rd, evac):
        sl = wqi[0] % 2
        wqi[0] += 1
        for k4 in range(4):
            wload("pool", wq[sl][:, 4 * k4:4 * k4 + 4, 0:ncols],
                  w_in[512 * k4:512 * k4 + 512, col0:col0 + ncols].rearrange("(kt p) n -> p kt n", p=128), R_wq[sl])
        for ni, (n0, N) in enumerate(NTILES):
            b = 6 + (ni % 2)
            tts = sorted(set([n0 // 128 + i for i in range((N + 127) // 128)]))
            mms(bk(b)[0:ncols, 0:N], [(wq[sl][:, kt, 0:ncols], xnT[:, kt, n0:n0 + N]) for kt in range(16)],
                [R_wq[sl]] + [R_xnT[t] for t in tts], [RB[b]])
            evac(dests(n0, N), bk(b)[0:ncols, 0:N], [RB[b]], Rd)

    def ev_copy(out, pin, rb, Rd):
        c.op("act", lambda e: e.copy(out=out, in_=pin), reads=[], writes=rb + [Rd])

    def ev_silu(out, pin, rb, Rd):
        c.op("act", lambda e: e.activation(out=out, in_=pin, func=AF.Silu), reads=[], writes=rb + [Rd])

    gaT = ar.alloc([NT], F32); R_gaT = Res()
    wa2_sb = ar.alloc([512], F32); ba_sb = ar.alloc([512], F32); glag_sb = ar.alloc([2], F32); R_gw = Res()
    c.dma("sp", lambda e: e.dma_start(out=wa2_sb[0:16], in_=wa2[:, :]), writes=[R_gw])
    c.dma("sp", lambda e: e.dma_start(out=ba_sb[0:1], in_=b_a.rearrange("(o n) -> o n", o=1)), writes=[R_gw])
    c.dma("sp", lambda e: e.dma_start(out=glag_sb, in_=glag.rearrange("(t p) -> p t", p=128), allow_slow_non_contiguous=True), writes=[R_gw])
    proj_fm(OFF["ga"], 16, lambda n0, N: gaT[0:16, n0:n0 + N], R_gaT, ev_copy)

    qT = ar.alloc([NT], BF16); kT = ar.alloc([NT], BF16); vT = ar.alloc([2, NT], BF16); srT = ar.alloc([2, NT], BF16)
    R_q, R_k, R_v, R_sr = Res(), Res(), Res(), Res()
    ost = ar.alloc([2, NT], BF16); R_ost = Res()
    Sst = ar.alloc([256], F32); R_S = Res(); Sbf = ar.alloc([256], BF16); R_Sbf = Res()
    Sall = ar.alloc([16, 256], F32); R_Sall = Res(); Sallb = ar.alloc([16, 256], BF16); R_Sallb = Res()
    e1 = ar.alloc([128], F32); spt = ar.alloc([128], F32); R_e1, R_sp = Res(), Res()
    BTs = ar.alloc([128], F32); R_BTs = Res()
    eb = ar.alloc([128], F32); einv = ar.alloc([128], F32); ekl = ar.alloc([128], F32); nbc = ar.alloc([1], F32)
    R_eb, R_einv, R_ekl, R_nb = Res(), Res(), Res(), Res()
    qe = ar.alloc([128], BF16); ke = ar.alloc([128], BF16); kl = ar.alloc([128], BF16)
    R_qe, R_ke, R_kl = Res(), Res(), Res()
    kltok = ar.alloc([128], BF16); vtk = ar.alloc([256], BF16); R_kltok, R_vtk = Res(), Res()
    ATs = ar.alloc([128], BF16); R_ATs = Res()
    sqo = ar.alloc([2, 128], F32); R_sqo = Res()
    sdo = ar.alloc([128], F32); R_sdo = Res()
    tmpo = ar.alloc([128], F32); R_tmpo = Res()
    klm = ar.alloc([16, 128], BF16); R_klm = Res()
    snew = [ar.alloc([256], F32) for _ in range(2)]; R_snew = [Res(), Res()]
    tri_f = CF("tri"); tri_b = CB("tri"); tris_f = CF("tri_s"); tris_b = CB("tri_s"); bm_s = CF("bm_s")

    c.op("pool", lambda e: e.memset(ost, 0.0), writes=[R_ost])
    for r8 in range(4):
        for t in range(2):
            c.dma("pool", lambda e, r8=r8, t=t: e.dma_start(
                out=oscr[1024 + r8 * 256 + t * 128:1024 + r8 * 256 + (t + 1) * 128, :], in_=ost[:, t, :]), reads=[R_ost])
    for h in range(4):
        proj_fm(OFF["gq"] + h * 128, 128, lambda n0, N: qT[:, n0:n0 + N], R_q, ev_copy)
        proj_fm(OFF["gk"] + h * 128, 128, lambda n0, N: kT[:, n0:n0 + N], R_k, ev_copy)
        for t in range(2):
            proj_fm(OFF["gv"] + h * 256 + t * 128, 128, lambda n0, N, t=t: vT[:, t, n0:n0 + N], R_v, ev_copy)
            proj_fm(OFF["gr"] + h * 256 + t * 128, 128, lambda n0, N, t=t: srT[:, t, n0:n0 + N], R_sr, ev_silu)
        c.op("pool", lambda e: e.memset(Sst, 0.0), writes=[R_S])
        c.op("pool", lambda e: e.memset(Sbf, 0.0), writes=[R_Sbf])
        c.dma("sp", lambda e, h=h: e.dma_start(out=Sall, in_=sgla[:, h, :, :].rearrange("b d v -> d b v")),
              writes=[R_Sall])
        c.op("act", lambda e: e.copy(out=Sallb, in_=Sall), reads=[R_Sall], writes=[R_Sallb])
        for ci in range(17):
            smp = (ci == 16)
            T = 64 if smp else 128
            c0 = ci * 128
            cols = slice(c0, c0 + T)
            mms(bk(0)[0:T, 0:128], [(gaT[0:16, cols], wa2_sb[0:16, h * 128:(h + 1) * 128]),
                                    (ones_f[0:1, 0:T], ba_sb[0:1, h * 128:(h + 1) * 128])],
                [R_gaT, R_gw, R_cf], [RB[0]])
            c.op("act", lambda e, T=T: e.activation(out=e1[0:T], in_=bk(0)[0:T, 0:128], func=AF.Exp, scale=-1.0),
                 writes=[RB[0], R_e1])
            c.op("act", lambda e, T=T: e.activation(out=spt[0:T], in_=e1[0:T], func=AF.Ln, scale=1.0, bias=1.0),
                 reads=[R_e1], writes=[R_sp])
            trf = tris_f if smp else tri_f
            mms(bk(1)[:, 0:T], [(spt[0:T, :], trf[0:T, 0:T])], [R_sp, R_cf], [RB[1]])
            c.op("act", lambda e, T=T: e.copy(out=BTs[:, 0:T], in_=bk(1)[:, 0:T]), writes=[RB[1], R_BTs])
            c.op("act", lambda e, T=T: e.activation(out=eb[:, 0:T], in_=BTs[:, 0:T], func=AF.Exp, scale=-1.0 / 16),
                 reads=[R_BTs], writes=[R_eb])
            c.op("act", lambda e, T=T: e.activation(out=einv[:, 0:T], in_=BTs[:, 0:T], func=AF.Exp, scale=1.0 / 16),
                 reads=[R_BTs], writes=[R_einv])
            if not smp:
                c.op("dve", lambda e, T=T: e.tensor_scalar(out=nbc, in0=BTs[:, T - 1:T], scalar1=-1.0 / 16,
                                                           scalar2=None, op0=ALU.mult),
                     reads=[R_BTs], writes=[R_nb])
                c.op("act", lambda e, T=T: e.activation(out=ekl[:, 0:T], in_=BTs[:, 0:T], func=AF.Exp,
                                                        scale=1.0 / 16, bias=nbc[:, 0:1]),
                     reads=[R_BTs, R_nb], writes=[R_ekl])
            else:
                b3 = BTs[:, 0:64].rearrange("p (b l) -> p b l", l=4)
                c.op("dve", lambda e, b3=b3: e.tensor_tensor(
                    out=ekl[:, 0:64].rearrange("p (b l) -> p b l", l=4), in0=b3,
                    in1=b3[:, :, 3:4].broadcast_to([128, 16, 4]), op=ALU.subtract),
                    reads=[R_BTs], writes=[R_ekl])
                c.op("act", lambda e: e.activation(out=ekl[:, 0:64], in_=ekl[:, 0:64], func=AF.Exp, scale=1.0 / 16),
                     reads=[R_ekl], writes=[R_ekl])
            c.op("dve", lambda e, T=T, cols=cols: e.scalar_tensor_tensor(
                out=qe[:, 0:T], in0=qT[:, cols], scalar=SCALE, in1=eb[:, 0:T], op0=ALU.mult, op1=ALU.mult),
                reads=[R_q, R_eb], writes=[R_qe])
            c.op("dve", lambda e, T=T, cols=cols: e.tensor_tensor(out=ke[:, 0:T], in0=kT[:, cols], in1=einv[:, 0:T],
                                                                  op=ALU.mult),
                 reads=[R_k, R_einv], writes=[R_ke])
            c.op("dve", lambda e, T=T, cols=cols: e.tensor_tensor(out=kl[:, 0:T], in0=kT[:, cols], in1=ekl[:, 0:T],
                                                                  op=ALU.mult),
                 reads=[R_k, R_ekl], writes=[R_kl])
            pb2 = bkb(2)
            tposes([(pb2[0:T, 0:128], kl[:, 0:T], ident_b),
                    (pb2[0:T, 128:256], vT[:, 0, cols], ident_b),
                    (pb2[0:T, 256:384], vT[:, 1, cols], ident_b)], [R_kl, R_v, R_cb], [RB[2]])
            c.op("act", lambda e, T=T: e.copy(out=kltok[0:T], in_=pb2[0:T, 0:128]), writes=[RB[2], R_kltok])
            c.op("act", lambda e, T=T: e.copy(out=vtk[0:T], in_=pb2[0:T, 128:384]), writes=[RB[2], R_vtk])
            mms(bk(3)[0:T, 0:T], [(ke[:, 0:T], qe[:, 0:T])], [R_ke, R_qe], [RB[3]])
            trb = tris_b if smp else tri_b
            c.op("dve", lambda e, T=T, trb=trb: e.tensor_tensor(out=ATs[0:T, 0:T], in0=bk(3)[0:T, 0:T],
                                                                in1=trb[0:T, 0:T], op=ALU.mult),
                 reads=[R_cb], writes=[RB[3], R_ATs])
            for t in range(2):
                pairs = [(vtk[0:T, t * 128:(t + 1) * 128], ATs[0:T, 0:T])]
                if not smp:
                    pairs.append((Sbf[:, t * 128:(t + 1) * 128], qe[:, 0:T]))
                    mms(bk(4)[:, t * 128:t * 128 + T], pairs, [R_vtk, R_ATs, R_Sbf, R_qe], [RB[4]])
                else:
                    fns = [lambda e, t=t: e.matmul(bk(4)[:, t * 128:t * 128 + 64], lhsT=vtk[0:64, t * 128:(t + 1) * 128],
                                                   rhs=ATs[0:64, 0:64], start=True, stop=False)]
                    for bb in range(16):
                        fns.append(lambda e, t=t, bb=bb: e.matmul(
                            bk(4)[:, t * 128 + 4 * bb:t * 128 + 4 * bb + 4], lhsT=Sallb[:, bb, t * 128:(t + 1) * 128],
                            rhs=qe[:, 4 * bb:4 * bb + 4], start=False, stop=(bb == 15)))
                    c.op("pe", fns, [R_vtk, R_ATs, R_Sallb, R_qe], [RB[4]])
            if not smp:
                mms(bk(5)[:, 0:256], [(kltok[0:T, :], vtk[0:T, :])], [R_kltok, R_vtk], [RB[5]])
                c.op("dve", lambda e, T=T: e.scalar_tensor_tensor(out=Sst, in0=Sst, scalar=eb[:, T - 1:T],
                                                                  in1=bk(5)[:, 0:256], op0=ALU.mult, op1=ALU.add),
                     reads=[R_eb], writes=[RB[5], R_S])
                c.op("act", lambda e: e.copy(out=Sbf, in_=Sst), reads=[R_S], writes=[R_Sbf])
            else:
                for bb in range(16):
                    c.op("dve", lambda e, bb=bb: e.tensor_scalar(out=klm[0:64, bb, :], in0=kltok[0:64, :],
                                                                 scalar1=bm_s[0:64, bb:bb + 1], scalar2=None,
                                                                 op0=ALU.mult),
                         reads=[R_kltok, R_cf], writes=[R_klm])
                for bb in range(16):
                    pbk = 5 + (bb % 2) * 2
                    sn = bb % 2
                    mms(bk(pbk)[:, 0:256], [(klm[0:64, bb, :], vtk[0:64, :])], [R_klm, R_vtk], [RB[pbk]])
                    c.op("dve", lambda e, bb=bb, pbk=pbk, sn=sn: e.scalar_tensor_tensor(
                        out=snew[sn], in0=Sall[:, bb, :], scalar=eb[:, 4 * bb + 3:4 * bb + 4], in1=bk(pbk)[:, 0:256],
                        op0=ALU.mult, op1=ALU.add),
                        reads=[R_Sall, R_eb], writes=[RB[pbk], R_snew[sn]])
                    out_toks.append(c.dma("pool", lambda e, bb=bb, sn=sn, h=h: e.dma_start(out=glas[bb, h, :, :],
                                                                                          in_=snew[sn]),
                                          reads=[R_snew[sn]]))
            c.op("act", lambda e, T=T: e.activation(
                out=sqo[:, :, 0:T], in_=bk(4)[:, 0:256].rearrange("p (t i) -> p t i", t=2)[:, :, 0:T], func=AF.Square),
                writes=[RB[4], R_sqo])
            mms(bk(6)[:, 0:T], [(ones_f, sqo[:, 0, 0:T]), (ones_f, sqo[:, 1, 0:T])], [R_sqo, R_cf], [RB[6]])
            c.op("act", lambda e, T=T: e.activation(out=sdo[:, 0:T], in_=bk(6)[:, 0:T], func=AF.Sqrt,
                                                    scale=1.0 / 256, bias=EPS),
                 writes=[RB[6], R_sdo])
            c.op("dve", lambda e, T=T: e.reciprocal(out=sdo[:, 0:T], in_=sdo[:, 0:T]), reads=[R_sdo], writes=[R_sdo])
            for t in range(2):
                c.op("dve", lambda e, T=T, t=t: e.scalar_tensor_tensor(
                    out=tmpo[:, 0:T], in0=bk(4)[:, t * 128:t * 128 + T], scalar=glag_sb[:, t:t + 1], in1=sdo[:, 0:T],
                    op0=ALU.mult, op1=ALU.mult),
                    reads=[R_sdo, R_gw], writes=[RB[4], R_tmpo])
                c.op("dve", lambda e, T=T, t=t, cols=cols: e.tensor_tensor(
                    out=ost[:, t, cols], in0=tmpo[:, 0:T], in1=srT[:, t, cols], op=ALU.mult),
                    reads=[R_tmpo, R_sr], writes=[R_ost])
        out_toks.append(c.dma("pool", lambda e, h=h: e.dma_start(out=glap[h, :, :], in_=Sst), reads=[R_S]))
        for t in range(2):
            c.dma("pool", lambda e, h=h, t=t: e.dma_start(
                out=oscr[h * 256 + t * 128:h * 256 + (t + 1) * 128, :], in_=ost[:, t, :]), reads=[R_ost])

    if stop_after < 2:
        return finish()
    import os
    DBG = int(os.environ.get("DBG", "255"))
    OQ = os.environ.get("OQ", "pool")
    c.barrier()
    ar.top = mark_p2
    vtok = [ar.alloc([17, 256], BF16) for _ in range(2)]
    R_vtok = [Res(), Res()]
    mark_nsa = ar.top
    wkv = [ar.alloc([16, 512], BF16) for _ in range(2)]; R_wkv = [Res(), Res()]
    stg = [ar.alloc([512], F32) for _ in range(3)]; R_stg = [Res() for _ in range(3)]
    si = 0
    for ch in range(3):
        ws = ch % 2
        col0 = OFF["kc"] + 512 * ch
        for k4 in range(8):
            wload("pool", wkv[ws][:, 2 * k4:2 * k4 + 2, :],
                  w_in[256 * k4:256 * k4 + 256, col0:col0 + 512].rearrange("(kt p) n -> p kt n", p=128), R_wkv[ws])
        for tt in range(17):
            rows = 128 if tt < 16 else 64
            b = 4 + (tt % 2)
            mms(bk(b)[0:rows, :], [(xnT[:, kt, tt * 128:tt * 128 + rows], wkv[ws][:, kt, :]) for kt in range(16)],
                [R_xnT[tt], R_wkv[ws]], [RB[b]])
            s = si % 3
            si += 1
            c.op("act", lambda e, s=s, b=b, rows=rows: e.copy(out=stg[s][0:rows], in_=bk(b)[0:rows, :]),
                 reads=[RB[b]], writes=[R_stg[s]])
            if ch >= 1 and (DBG & 16):
                c.op("act", lambda e, b=b, rows=rows, tt=tt, ch=ch: e.copy(
                    out=vtok[ch - 1][0:rows, tt, :], in_=bk(b)[0:rows, 256:512]),
                    reads=[RB[b]], writes=[R_vtok[ch - 1]])
            for half in range(2):
                src = stg[s][0:rows, 256 * half:256 * half + 256]
                if tt < 16:
                    if not (DBG & 1):
                        continue
                    dst = kvp[2 * ch + half][tt * 128:(tt + 1) * 128, :] if ch < 2 else None
                    if ch < 2:
                        out_toks.append(c.dma(OQ, lambda e, dst=dst, src=src: e.dma_start(out=dst, in_=src),
                                              reads=[R_stg[s]]))
                    elif tt >= 12:
                        dst = (wkp, wvp)[half][(tt - 12) * 128:(tt - 11) * 128, :]
                        out_toks.append(c.dma(OQ, lambda e, dst=dst, src=src: e.dma_start(out=dst, in_=src),
                                              reads=[R_stg[s]]))
                else:
                    if ch < 2:
                        if not (DBG & 2):
                            continue
                        dst = kvs[2 * ch + half][:, :]
                        out_toks.append(c.dma(OQ, lambda e, dst=dst, src=src: e.dma_start(out=dst, in_=src),
                                              reads=[R_stg[s]]))
                    else:
                        wdst = (wks, wvs)[half]
                        for bb in range(16 if (DBG & 4) else 0):
                            out_toks.append(c.dma(
                                OQ, lambda e, bb=bb, wdst=wdst, s=s, half=half: e.dma_start(
                                    out=wdst[bb, 508:512, :],
                                    in_=stg[s][4 * bb:4 * bb + 4, 256 * half:256 * half + 256]),
                                reads=[R_stg[s]]))
    wbn = [ar.alloc([4, 256], F32) for _ in range(2)]; R_wbn = [Res(), Res()]
    wi = 0
    for (srcw, dstw) in ((swk, wks), (swv, wvs)):
        for bb in range(16 if (DBG & 8) else 0):
            s_ = wi % 2
            wi += 1
            c.dma("pool", lambda e, srcw=srcw, bb=bb, s_=s_: e.dma_start(
                out=wbn[s_][0:127], in_=srcw[bb, 4:512, :].rearrange("(p r) c -> p r c", r=4)),
                writes=[R_wbn[s_]])
            out_toks.append(c.dma("pool", lambda e, dstw=dstw, bb=bb, s_=s_: e.dma_start(
                out=dstw[bb, 0:508, :].rearrange("(p r) c -> p r c", r=4), in_=wbn[s_][0:127]),
                reads=[R_wbn[s_]]))
    import os as _os
    if int(_os.environ.get("NSA", "1")):
        c.barrier()
        ar.top = mark_nsa
        wq = [ar.alloc([16, 128], BF16) for _ in range(2)]; R_wq = [Res(), Res()]
        qTn = ar.alloc([4, NT], BF16); R_qn = Res()
        ksT = ar.alloc([NT], BF16); kwT = ar.alloc([NT], BF16); R_ks, R_kw = Res(), Res()
        sigG = ar.alloc([NT], BF16); R_sigG = Res()
        qs = ar.alloc([8, 64], BF16); kss = ar.alloc([2, 64], BF16); kws = ar.alloc([2, 64], BF16); R_qs = Res()
        mark_s = ar.top
        oacc = ar.alloc([4, SEQ], BF16); R_oacc = Res()
        selT = ar.alloc([SEQ], BF16); R_selT = Res()
        impa = ar.alloc([16, 32], F32); R_imp = Res()
        ckT = ar.alloc([128], BF16); cvs = ar.alloc([128], BF16); R_ck, R_cv = Res(), Res()
        Et = [ar.alloc([512], BF16) for _ in range(2)]; R_Et = [Res(), Res()]
        zt = [ar.alloc([512], F32)]; R_zt = [Res()]
        osum = ar.alloc([512], F32); R_osum = Res()
        sc1 = ar.alloc([32], F32); sc2 = ar.alloc([32], F32); m8 = ar.alloc([8], F32); m8b = ar.alloc([8], F32)
        selb = ar.alloc([32], BF16); rz = ar.alloc([4], F32)
        R_sc1, R_sc2, R_m8, R_m8b, R_selb, R_rz = Res(), Res(), Res(), Res(), Res(), Res()
        mark_x = ar.top
        cmask = CB("cmask"); caus = CB("caus").rearrange("p (r q) -> p r q", r=4)
        wmask = CB("wmask").rearrange("p (r q) -> p r q", r=8)
        expand = CB("expand").rearrange("p (k n) -> p k n", k=16)
        onehot = CB("onehot").rearrange("p (r n) -> p r n", r=24)
        mp = CB("mp"); A_p = CF("A_p").rearrange("p (s j) -> p s j", s=16); B_p = CF("B_p").rearrange("p (s j) -> p s j", s=16)
        proj_fm(OFF["ng"], 24, lambda n0, N: sigG[0:24, n0:n0 + N], R_sigG,
                lambda out, pin, rb, Rd: c.op("act", lambda e: e.activation(out=out, in_=pin, func=AF.Sigmoid),
                                              writes=rb + [Rd]))
        for g in range(2):
            c.barrier()
            ar.top = mark_x
            kcT = ar.alloc([NT], BF16); vcT = ar.alloc([NT], BF16); R_kc, R_vc = Res(), Res()
            w1k = ar.alloc([32, 128], BF16); w1v = w1k; w2k = ar.alloc([128], BF16); w2v = ar.alloc([128], BF16)
            R_w1 = Res()
            hk = ar.alloc([128], BF16); hv = ar.alloc([128], BF16); R_hk, R_hv = Res(), Res()
            c.dma("pool", lambda e: e.dma_start(out=w2k, in_=wk2[:, :]), writes=[R_w1])
            c.dma("pool", lambda e: e.dma_start(out=w2v, in_=wv2[:, :]), writes=[R_w1])
            for hh in range(4):
                proj_fm(OFF["nq"] + g * 512 + hh * 128, 128, lambda n0, N, hh=hh: qTn[:, hh, n0:n0 + N], R_qn, ev_copy)
            proj_fm(OFF["kc"] + g * 128, 128, lambda n0, N: kcT[:, n0:n0 + N], R_kc, ev_copy)
            proj_fm(OFF["vc"] + g * 128, 128, lambda n0, N: vcT[:, n0:n0 + N], R_vc, ev_copy)
            proj_fm(OFF["ks"] + g * 128, 128, lambda n0, N: ksT[:, n0:n0 + N], R_ks, ev_copy)
            proj_fm(OFF["kw"] + g * 128, 128, lambda n0, N: kwT[:, n0:n0 + N], R_kw, ev_copy)
            c.op("act", lambda e, g=g: e.copy(out=qs[:, 4 * g:4 * g + 4, :], in_=qTn[:, :, SEQ:NT]), reads=[R_qn], writes=[R_qs])
            c.op("act", lambda e, g=g: e.copy(out=kss[:, g, :], in_=ksT[:, SEQ:NT]), reads=[R_ks], writes=[R_qs])
            c.op("act", lambda e, g=g: e.copy(out=kws[:, g, :], in_=kwT[:, SEQ:NT]), reads=[R_kw], writes=[R_qs])
            for (srcT, w1, hdst, Rs, Rh, src_) in ((kcT, w1k, hk, R_kc, R_hk, wk1), (vcT, w1v, hv, R_vc, R_hv, wv1)):
                for j8 in range(4):
                    c.dma("pool", lambda e, w1=w1, src_=src_, j8=j8: e.dma_start(
                        out=w1[:, 8 * j8:8 * j8 + 8, :], in_=src_[8 * j8:8 * j8 + 8, :, :].rearrange("j d e -> d j e")),
                        writes=[R_w1])
                mms(bk(0)[:, 0:127], [(w1[:, j, :], srcT[:, j:j + 2017:16]) for j in range(32)], [R_w1, Rs], [RB[0]])
                c.op("act", lambda e, hdst=hdst: e.activation(out=hdst[:, 0:127], in_=bk(0)[:, 0:127], func=AF.Gelu_apprx_tanh),
                     writes=[RB[0], Rh])
            mms(bk(1)[:, 0:127], [(w2k[:, :], hk[:, 0:127])], [R_w1, R_hk], [RB[1]])
            c.op("act", lambda e: e.copy(out=ckT[:, 0:127], in_=bk(1)[:, 0:127]), writes=[RB[1], R_ck])
            mms(bk(2)[0:127, 0:128], [(hv[:, 0:127], w2v[:, :])], [R_w1, R_hv], [RB[2]])
            c.op("act", lambda e: e.copy(out=cvs[0:127, :], in_=bk(2)[0:127, 0:128]), writes=[RB[2], R_cv])
            c.op("pool", lambda e: e.memset(impa, 0.0), writes=[R_imp])

            def finalize(pvb, zb, r, q0, first, dst, Rdst, srcadd=None, Rsrc=None):
                mms(bk(6)[:, 0:512], [(onehot[0:24, r, :], sigG[0:24, q0:q0 + 512])], [R_cb, R_sigG], [RB[6]])
                c.op("dve", lambda e, zb=zb: e.tensor_scalar(out=zt[0], in0=bk(zb)[:, 0:512], scalar1=1e-30, scalar2=None,
                                                            op0=ALU.max), writes=[RB[zb], R_zt[0]])
                c.op("dve", lambda e: e.reciprocal(out=zt[0], in_=zt[0]), reads=[R_zt[0]], writes=[R_zt[0]])
                c.op("dve", lambda e: e.tensor_tensor(out=zt[0], in0=bk(6)[:, 0:512], in1=zt[0], op=ALU.mult),
                     reads=[R_zt[0]], writes=[RB[6], R_zt[0]])
                if first:
                    c.op("dve", lambda e, pvb=pvb, dst=dst: e.tensor_tensor(
                        out=dst, in0=bk(pvb)[:, 0:512], in1=zt[0], op=ALU.mult),
                        reads=[R_zt[0]], writes=[RB[pvb], Rdst])
                else:
                    c.op("dve", lambda e, pvb=pvb: e.tensor_tensor(out=osum, in0=bk(pvb)[:, 0:512], in1=zt[0], op=ALU.mult),
                         reads=[R_zt[0]], writes=[RB[pvb], R_osum])
                    c.op("dve", lambda e, dst=dst, srcadd=srcadd: e.tensor_tensor(
                        out=dst, in0=srcadd, in1=osum, op=ALU.add),
                        reads=[R_osum, Rsrc], writes=[Rdst])

            ei = 0
            for hh in range(4):
                for qt in range(4):
                    q0 = qt * 512
                    sb_ = ei % 2
                    e_ = ei % 2
                    ei += 1
                    mms(bk(sb_)[0:127, 0:512], [(ckT[:, 0:127], qTn[:, hh, q0:q0 + 512])], [R_ck, R_qn], [RB[sb_]])
                    c.op("act", lambda e, sb_=sb_, e_=e_: e.activation(out=Et[e_][0:127], in_=bk(sb_)[0:127, 0:512],
                                                                     func=AF.Exp, scale=SCALE),
                         writes=[RB[sb_], R_Et[e_]])
                    c.op("dve", lambda e, e_=e_, q0=q0: e.tensor_tensor(out=Et[e_][0:127], in0=Et[e_][0:127],
                                                                      in1=cmask[0:127, q0:q0 + 512], op=ALU.mult),
                         reads=[R_cb], writes=[R_Et[e_]])
                    mms(bk(2)[:, 0:512], [(cvs[0:127, :], Et[e_][0:127])], [R_cv, R_Et[e_]], [RB[2]])
                    mms(bk(3)[:, 0:512], [(ones_b[0:127, :], Et[e_][0:127])], [R_cb, R_Et[e_]], [RB[3]])
                    fns = [lambda e, e_=e_, s4=s4: e.matmul(bk(4)[:, 64 * s4:64 * s4 + 33], lhsT=Et[e_][0:127, 128 * s4:128 * s4 + 128],
                                                           rhs=mp[0:127, 0:33], start=True, stop=True) for s4 in range(4)]
                    c.op("pe", fns, [R_Et[e_], R_cb], [RB[4]])
                    em = bk(4)[:, 0:256].rearrange("p (s j) -> p s j", s=4)
                    c.op("dve", lambda e, em=em: e.tensor_scalar(out=rz, in0=em[:, :, 32], scalar1=1e-30, scalar2=None, op0=ALU.max),
                         writes=[RB[4], R_rz])
                    c.op("dve", lambda e: e.reciprocal(out=rz, in_=rz), reads=[R_rz], writes=[R_rz])
                    for s4 in range(4):
                        c.op("dve", lambda e, em=em, s4=s4, qt=qt: e.scalar_tensor_tensor(
                            out=impa[:, 4 * qt + s4, :], in0=em[:, s4, 0:32], scalar=rz[:, s4:s4 + 1], in1=impa[:, 4 * qt + s4, :],
                            op0=ALU.mult, op1=ALU.add), reads=[R_rz], writes=[RB[4], R_imp])
                    finalize(2, 3, 0 * 8 + g * 4 + hh, q0, True, oacc[:, hh, q0:q0 + 512], R_oacc)
            for s16 in range(16):
                c.op("dve", lambda e, s16=s16: e.tensor_tensor(out=sc1, in0=impa[:, s16, :], in1=A_p[:, s16, :], op=ALU.mult),
                     reads=[R_imp, R_cf], writes=[R_sc1])
                c.op("dve", lambda e, s16=s16: e.tensor_tensor(out=sc1, in0=sc1, in1=B_p[:, s16, :], op=ALU.add),
                     reads=[R_sc1, R_cf], writes=[R_sc1])
                c.op("dve", lambda e: e.max(out=m8, in_=sc1), reads=[R_sc1], writes=[R_m8])
                c.op("dve", lambda e: e.match_replace(out=sc2, in_to_replace=m8, in_values=sc1, imm_value=-2.0),
                     reads=[R_sc1, R_m8], writes=[R_sc2])
                c.op("dve", lambda e: e.max(out=m8b, in_=sc2), reads=[R_sc2], writes=[R_m8b])
                c.op("dve", lambda e: e.tensor_scalar(out=selb, in0=sc1, scalar1=m8b[:, 7:8], scalar2=None, op0=ALU.is_ge),
                     reads=[R_sc1, R_m8b], writes=[R_selb])
                tposes([(bkb(5)[0:32, 0:128], selb[:, :], ident_b)], [R_selb, R_cb], [RB[5]])
                c.op("act", lambda e, s16=s16: e.copy(out=selT[0:32, s16 * 128:(s16 + 1) * 128], in_=bkb(5)[0:32, 0:128]),
                     writes=[RB[5], R_selT])
            c.barrier()
            ar.top = mark_x
            msk = ar.alloc([16, 512], BF16); R_msk = Res()
            ostq = [ar.alloc([512], BF16) for _ in range(2)]; R_ostq = [Res(), Res()]
            for qt in range(4):
                q0 = qt * 512
                nk = 4 * qt + 4
                for kt in range(nk):
                    mms(bk(7)[:, 0:512], [(expand[0:32, kt, :], selT[0:32, q0:q0 + 512])], [R_cb, R_selT], [RB[7]])
                    if kt >= 4 * qt:
                        c.op("dve", lambda e, kt=kt, qt=qt: e.tensor_tensor(out=msk[:, kt, :], in0=bk(7)[:, 0:512],
                                                                           in1=caus[:, kt - 4 * qt, :], op=ALU.mult),
                             reads=[R_cb], writes=[RB[7], R_msk])
                    else:
                        c.op("act", lambda e, kt=kt: e.copy(out=msk[:, kt, :], in_=bk(7)[:, 0:512]), writes=[RB[7], R_msk])
                for hh in range(4):
                    for br in (1, 2):
                        kts = list(range(nk)) if br == 1 else list(range(max(0, 4 * qt - 4), nk))
                        kT_ = ksT if br == 1 else kwT
                        Rk_ = R_ks if br == 1 else R_kw
                        pvb, zb = (2, 3) if br == 1 else (4, 5)
                        for ii, kt in enumerate(kts):
                            sb_ = ei % 2
                            e_ = ei % 2
                            ei += 1
                            mms(bk(sb_)[:, 0:512], [(kT_[:, kt * 128:(kt + 1) * 128], qTn[:, hh, q0:q0 + 512])], [Rk_, R_qn], [RB[sb_]])
                            c.op("act", lambda e, sb_=sb_, e_=e_: e.activation(out=Et[e_], in_=bk(sb_)[:, 0:512], func=AF.Exp, scale=SCALE),
                                 writes=[RB[sb_], R_Et[e_]])
                            mk = msk[:, kt, :] if br == 1 else wmask[:, kt - 4 * qt + 4, :]
                            c.op("dve", lambda e, e_=e_, mk=mk: e.tensor_tensor(out=Et[e_], in0=Et[e_], in1=mk, op=ALU.mult),
                                 reads=[R_msk, R_cb], writes=[R_Et[e_]])
                            vop = vtok[br - 1][:, kt, g * 128:(g + 1) * 128]
                            first, last = (ii == 0), (ii == len(kts) - 1)
                            c.op("pe", [lambda e, vop=vop, e_=e_, pvb=pvb, first=first, last=last: e.matmul(
                                bk(pvb)[:, 0:512], lhsT=vop, rhs=Et[e_], start=first, stop=last),
                                lambda e, e_=e_, zb=zb, first=first, last=last: e.matmul(
                                    bk(zb)[:, 0:512], lhsT=ones_b, rhs=Et[e_], start=first, stop=last)],
                                [R_vtok[br - 1], R_Et[e_], R_cb], [RB[pvb], RB[zb]])
                        if br == 1:
                            finalize(pvb, zb, 1 * 8 + g * 4 + hh, q0, False, oacc[:, hh, q0:q0 + 512], R_oacc,
                                     oacc[:, hh, q0:q0 + 512], R_oacc)
                        else:
                            oq = (qt * 4 + hh) % 2
                            finalize(pvb, zb, 2 * 8 + g * 4 + hh, q0, False, ostq[oq], R_ostq[oq],
                                     oacc[:, hh, q0:q0 + 512], R_oacc)
                            row0 = 1024 + (g * 4 + hh) * 128
                            c.dma("pool", lambda e, oq=oq, row0=row0, q0=q0: e.dma_start(
                                out=oscr[row0:row0 + 128, q0:q0 + 512], in_=ostq[oq]), reads=[R_ostq[oq]])
    if int(_os.environ.get("NSA", "1")) and int(_os.environ.get("NSAS", "1")):
        c.barrier()
        nsa_top = ar.top
        ar.top = mark_persist
        raw = ar.alloc([4096], F32); R_raw = Res()
        rawb = ar.alloc([16, 256], BF16); R_rawb = Res()
        KT = ar.alloc([16, 2, 128], BF16); R_KT = Res()
        wraw = ar.alloc([4, 256], F32); R_wraw = Res()
        wrb = ar.alloc([4, 256], BF16); R_wrb = Res()
        KwT = ar.alloc([4, 2, 128], BF16); R_KwT = Res()
        assert ar.top <= mark_p2, "sample NSA tiles overflow xnT region: %d > %d" % (ar.top, mark_p2)
        ar.top = mark_s
        w1ks = ar.alloc([32, 128], BF16); w1vs = ar.alloc([32, 128], BF16); w2ks = ar.alloc([128], BF16); w2vs = ar.alloc([128], BF16)
        R_w1s = Res()
        hks = ar.alloc([2, 128], BF16); R_hks = Res()
        ckTs = ar.alloc([2, 128], BF16); cvss = ar.alloc([2, 128], BF16); R_ckTs, R_cvss = Res(), Res()
        Eall = ar.alloc([8, 64], BF16); R_Eall = Res()
        PVt = ar.alloc([3, 8, 64], F32); Zt = ar.alloc([3, 8, 64], F32); R_PVt, R_Zt = Res(), Res()
        ptT8 = ar.alloc([16], I32); idxf = ar.alloc([16], F32); idx = ar.alloc([16], I32); R_idx = Res()
        Es2 = [ar.alloc([16, 4], BF16) for _ in range(2)]; En2 = [ar.alloc([4], BF16) for _ in range(2)]
        Esum2 = [ar.alloc([4], F32) for _ in range(2)]; Ew2 = [ar.alloc([4, 4], BF16) for _ in range(2)]
        R_Es2 = [Res(), Res()]; R_En2 = [Res(), Res()]; R_Esum2 = [Res(), Res()]; R_Ew2 = [Res(), Res()]
        imps = ar.alloc([2, 33], F32); rzs = ar.alloc([8], F32); R_imps, R_rzs = Res(), Res()
        scs = ar.alloc([33], F32); scs2 = ar.alloc([33], F32); m8s = ar.alloc([8], F32); m8sb = ar.alloc([8], F32)
        sels = ar.alloc([33], BF16); selTs = ar.alloc([2, 64], BF16); masks = ar.alloc([2, 64], BF16)
        R_scs, R_scs2, R_m8s, R_m8sb, R_sels, R_selTs, R_masks = Res(), Res(), Res(), Res(), Res(), Res(), Res()
        osm = ar.alloc([8, 64], F32); Fs = ar.alloc([8, 64], F32); osb = ar.alloc([8, 64], BF16); R_osm, R_Fs, R_osb = Res(), Res(), Res()
        ms_ = CB("ms"); A_s = CF("A_s"); B_s = CF("B_s"); cvec = CF("cvec"); expand_s = CB("expand_s")
        newmask = CB("newmask").rearrange("p (b l) -> p b l", b=16); winmask_s = CB("winmask_s").rearrange("p (k l) -> p k l", k=4)
        onehot = CB("onehot").rearrange("p (r n) -> p r n", r=24)
        for (dst_, src_) in ((w1ks, wk1), (w1vs, wv1)):
            for j8 in range(4):
                c.dma("pool", lambda e, dst_=dst_, src_=src_, j8=j8: e.dma_start(
                    out=dst_[:, 8 * j8:8 * j8 + 8, :], in_=src_[8 * j8:8 * j8 + 8, :, :].rearrange("j d e -> d j e")),
                    writes=[R_w1s])
        c.dma("pool", lambda e: e.dma_start(out=w2ks, in_=wk2[:, :]), writes=[R_w1s])
        c.dma("pool", lambda e: e.dma_start(out=w2vs, in_=wv2[:, :]), writes=[R_w1s])
        for c8 in range(8):
            c.dma("sp", lambda e, c8=c8: e.dma_start(out=ptT8[16 * c8:16 * c8 + 16, :], in_=pt.rearrange("b j -> j b"),
                                                    allow_slow_non_contiguous=True), writes=[R_idx])
        c.op("dve", lambda e: e.tensor_copy(out=idxf, in_=ptT8), reads=[R_idx], writes=[R_idx])
        c.op("dve", lambda e: e.tensor_scalar(out=idxf, in0=idxf, scalar1=8.0, scalar2=cvec[:, 0:1], op0=ALU.mult, op1=ALU.add),
             reads=[R_idx, R_cf], writes=[R_idx])
        c.op("dve", lambda e: e.tensor_copy(out=idx, in_=idxf), reads=[R_idx], writes=[R_idx])

        def gather(cache, b):
            c.dma("pool", lambda e, cache=cache, b=b: e.indirect_dma_start(
                out=raw, out_offset=None, in_=cache[:, :],
                in_offset=bass.IndirectOffsetOnAxis(ap=idx[:, b:b + 1], axis=0)), reads=[R_idx], writes=[R_raw])
            c.op("act", lambda e: e.copy(out=rawb.rearrange("p j c -> p (j c)"), in_=raw), reads=[R_raw], writes=[R_rawb])

        def transposeK(unpermute):
            for grp in range(4):
                bnk = grp % 2
                items = []
                for i8 in range(8):
                    jj, g = (grp * 8 + i8) // 2, (grp * 8 + i8) % 2
                    items.append((bkb(bnk)[:, i8 * 128:(i8 + 1) * 128], rawb[:, jj, g * 128:(g + 1) * 128], ident_b))
                tposes(items, [R_rawb, R_cb], [RB[bnk]])
                for i8 in range(8):
                    jj, g = (grp * 8 + i8) // 2, (grp * 8 + i8) % 2
                    src = bkb(bnk)[:, i8 * 128:(i8 + 1) * 128]
                    if unpermute:
                        c.op("act", lambda e, jj=jj, g=g, src=src: e.copy(
                            out=KT[:, jj, g, :].rearrange("d (j c) -> d c j", c=8), in_=src.rearrange("d (c j) -> d c j", c=8)),
                            writes=[RB[bnk], R_KT])
                    else:
                        c.op("act", lambda e, jj=jj, g=g, src=src: e.copy(out=KT[:, jj, g, :], in_=src), writes=[RB[bnk], R_KT])

        def compress_h(w1):
            for g in range(2):
                mms(bk(2)[:, g * 128:g * 128 + 127], [(w1[:, j, :], KT[:, j % 16, g, (j // 16):(j // 16) + 127]) for j in range(32)],
                    [R_w1s, R_KT], [RB[2]])
            c.op("act", lambda e: e.activation(out=hks[:, :, 0:127], in_=bk(2)[:, 0:256].rearrange("p (g n) -> p g n", g=2)[:, :, 0:127],
                                               func=AF.Gelu_apprx_tanh), writes=[RB[2], R_hks])

        for b in range(NB_S):
            gather(cck, b)
            transposeK(True)
            compress_h(w1ks)
            for g in range(2):
                mms(bk(3)[:, g * 128:g * 128 + 127], [(w2ks[:, :], hks[:, g, 0:127])], [R_w1s, R_hks], [RB[3]])
            c.op("act", lambda e: e.copy(out=ckTs[:, :, 0:127], in_=bk(3)[:, 0:256].rearrange("p (g n) -> p g n", g=2)[:, :, 0:127]),
                 writes=[RB[3], R_ckTs])
            gather(ccv, b)
            transposeK(True)
            compress_h(w1vs)
            for g in range(2):
                mms(bk(3)[0:127, g * 128:(g + 1) * 128], [(hks[:, g, 0:127], w2vs[:, :])], [R_w1s, R_hks], [RB[3]])
            c.op("act", lambda e: e.copy(out=cvss[0:127].rearrange("p g n -> p (g n)"), in_=bk(3)[0:127, 0:256]), writes=[RB[3], R_cvss])
            fns = [lambda e, hd=hd, b=b: e.matmul(bk(4)[0:127, hd * 4:hd * 4 + 4], lhsT=ckTs[:, hd // 4, 0:127],
                                                 rhs=qs[:, hd, 4 * b:4 * b + 4], start=True, stop=True) for hd in range(8)]
            c.op("pe", fns, [R_ckTs, R_qs], [RB[4]])
            c.op("act", lambda e, b=b: e.activation(out=Eall[0:127, :, 4 * b:4 * b + 4],
                                                    in_=bk(4)[0:127, 0:32].rearrange("p (h l) -> p h l", h=8), func=AF.Exp, scale=SCALE),
                 writes=[RB[4], R_Eall])
            fns = []
            for hd in range(8):
                fns.append(lambda e, hd=hd, b=b: e.matmul(bk(5)[:, hd * 4:hd * 4 + 4], lhsT=cvss[0:127, hd // 4, :],
                                                          rhs=Eall[0:127, hd, 4 * b:4 * b + 4], start=True, stop=True))
                fns.append(lambda e, hd=hd, b=b: e.matmul(bk(6)[:, hd * 4:hd * 4 + 4], lhsT=ones_b[0:127, :],
                                                          rhs=Eall[0:127, hd, 4 * b:4 * b + 4], start=True, stop=True))
            c.op("pe", fns, [R_cvss, R_Eall, R_cb], [RB[5], RB[6]])
            c.op("act", lambda e, b=b: e.copy(out=PVt[:, 0, :, 4 * b:4 * b + 4], in_=bk(5)[:, 0:32].rearrange("p (h l) -> p h l", h=8)),
                 writes=[RB[5], R_PVt])
            c.op("act", lambda e, b=b: e.copy(out=Zt[:, 0, :, 4 * b:4 * b + 4], in_=bk(6)[:, 0:32].rearrange("p (h l) -> p h l", h=8)),
                 writes=[RB[6], R_Zt])
        fns = [lambda e, hd=hd: e.matmul(bk(4)[0:64, hd * 64:hd * 64 + 34], lhsT=Eall[0:127, hd, :], rhs=ms_[0:127, 0:34],
                                         start=True, stop=True) for hd in range(8)]
        c.op("pe", fns, [R_Eall, R_cb], [RB[4]])
        ems = bk(4)[0:64, :].rearrange("p (h j) -> p h j", h=8)
        c.op("dve", lambda e: e.tensor_scalar(out=rzs[0:64], in0=ems[:, :, 33], scalar1=1e-30, scalar2=None, op0=ALU.max),
             writes=[RB[4], R_rzs])
        c.op("dve", lambda e: e.reciprocal(out=rzs[0:64], in_=rzs[0:64]), reads=[R_rzs], writes=[R_rzs])
        c.op("pool", lambda e: e.memset(imps, 0.0), writes=[R_imps])
        for hd in range(8):
            c.op("dve", lambda e, hd=hd: e.scalar_tensor_tensor(out=imps[0:64, hd // 4, :], in0=ems[:, hd, 0:33],
                                                               scalar=rzs[0:64, hd:hd + 1], in1=imps[0:64, hd // 4, :],
                                                               op0=ALU.mult, op1=ALU.add),
                 reads=[R_rzs], writes=[RB[4], R_imps])
        for g in range(2):
            c.op("dve", lambda e, g=g: e.tensor_tensor(out=scs[0:64], in0=imps[0:64, g, :], in1=A_s[0:64], op=ALU.mult),
                 reads=[R_imps, R_cf], writes=[R_scs])
            c.op("dve", lambda e: e.tensor_tensor(out=scs[0:64], in0=scs[0:64], in1=B_s[0:64], op=ALU.add),
                 reads=[R_scs, R_cf], writes=[R_scs])
            c.op("dve", lambda e: e.max(out=m8s[0:64], in_=scs[0:64]), reads=[R_scs], writes=[R_m8s])
            c.op("dve", lambda e: e.match_replace(out=scs2[0:64], in_to_replace=m8s[0:64], in_values=scs[0:64], imm_value=-2.0),
                 reads=[R_scs, R_m8s], writes=[R_scs2])
            c.op("dve", lambda e: e.max(out=m8sb[0:64], in_=scs2[0:64]), reads=[R_scs2], writes=[R_m8sb])
            c.op("dve", lambda e: e.tensor_scalar(out=sels[0:64], in0=scs[0:64], scalar1=m8sb[0:64, 7:8], scalar2=None, op0=ALU.is_ge),
                 reads=[R_scs, R_m8sb], writes=[R_sels])
            tposes([(bkb(5)[0:33, 0:64], sels[0:64, :], ident_b[0:64, 0:64])], [R_sels, R_cb], [RB[5]])
            c.op("act", lambda e, g=g: e.copy(out=selTs[0:33, g, :], in_=bkb(5)[0:33, 0:64]), writes=[RB[5], R_selTs])
            mms(bk(6)[:, 0:64], [(expand_s[0:33, :], selTs[0:33, g, :])], [R_cb, R_selTs], [RB[6]])
            c.op("act", lambda e, g=g: e.copy(out=masks[:, g, :], in_=bk(6)[:, 0:64]), writes=[RB[6], R_masks])
        for b in range(NB_S):
            gather(csk, b)
            transposeK(False)
            gather(csv, b)
            for (srcw, kv) in ((swk, 0), (swv, 1)):
                c.dma("sp", lambda e, srcw=srcw, b=b: e.dma_start(out=wraw, in_=srcw[b, :, :].rearrange("(k r) c -> r k c", r=128)),
                      writes=[R_wraw])
                if kv == 0:
                    c.op("act", lambda e: e.copy(out=wrb, in_=wraw), reads=[R_wraw], writes=[R_wrb])
                    tposes([(bkb(7)[:, i8 * 128:(i8 + 1) * 128], wrb[:, i8 // 2, (i8 % 2) * 128:(i8 % 2 + 1) * 128], ident_b)
                            for i8 in range(8)], [R_wrb, R_cb], [RB[7]])
                    c.op("act", lambda e: e.copy(out=KwT.rearrange("p k g n -> p (k g n)"), in_=bkb(7)[:, 0:1024]),
                         writes=[RB[7], R_KwT])
                else:
                    c.op("act", lambda e: e.copy(out=wrb, in_=wraw), reads=[R_wraw, R_wrb], writes=[R_wrb])
            for hd in range(8):
                g = hd // 4
                par = hd % 2
                SB_, PB_, ZB_ = (4, 3)[par], (5, 2)[par], (6, 7)[par]
                Es, En, Esum, Ew = Es2[par], En2[par], Esum2[par], Ew2[par]
                R_Es, R_En, R_Esum, R_Ew = R_Es2[par], R_En2[par], R_Esum2[par], R_Ew2[par]
                q4 = qs[:, hd, 4 * b:4 * b + 4]
                fns = [lambda e, Es=Es, En=En, Esum=Esum, Ew=Ew, SB_=SB_, PB_=PB_, ZB_=ZB_, jj=jj, g=g, q4=q4: e.matmul(bk(SB_)[:, jj * 4:jj * 4 + 4], lhsT=KT[:, jj, g, :], rhs=q4,
                                                            start=True, stop=True) for jj in range(16)]
                fns.append(lambda e, Es=Es, En=En, Esum=Esum, Ew=Ew, SB_=SB_, PB_=PB_, ZB_=ZB_, g=g, q4=q4: e.matmul(bk(SB_)[0:64, 64:68], lhsT=kss[:, g, :], rhs=q4, start=True, stop=True))
                c.op("pe", fns, [R_KT, R_qs], [RB[SB_]])
                c.op("act", lambda e, Es=Es, En=En, Esum=Esum, Ew=Ew, SB_=SB_, PB_=PB_, ZB_=ZB_: e.activation(out=Es, in_=bk(SB_)[:, 0:64].rearrange("p (j l) -> p j l", j=16), func=AF.Exp, scale=SCALE),
                     writes=[RB[SB_], R_Es])
                c.op("act", lambda e, Es=Es, En=En, Esum=Esum, Ew=Ew, SB_=SB_, PB_=PB_, ZB_=ZB_: e.activation(out=En[0:64], in_=bk(SB_)[0:64, 64:68], func=AF.Exp, scale=SCALE),
                     writes=[RB[SB_], R_En])
                c.op("dve", lambda e, Es=Es, En=En, Esum=Esum, Ew=Ew, SB_=SB_, PB_=PB_, ZB_=ZB_, g=g, b=b: e.tensor_tensor(
                    out=Es, in0=Es, in1=masks[:, g, 4 * b:4 * b + 4].unsqueeze(1).broadcast_to([128, 16, 4]), op=ALU.mult),
                    reads=[R_masks], writes=[R_Es])
                c.op("dve", lambda e, Es=Es, En=En, Esum=Esum, Ew=Ew, SB_=SB_, PB_=PB_, ZB_=ZB_, b=b: e.tensor_tensor(out=En[0:64], in0=En[0:64], in1=newmask[0:64, b, :], op=ALU.mult),
                     reads=[R_cb], writes=[R_En])
                c.op("dve", lambda e, Es=Es, En=En, Esum=Esum, Ew=Ew, SB_=SB_, PB_=PB_, ZB_=ZB_: e.tensor_reduce(out=Esum, in_=Es.rearrange("p j l -> p l j"), axis=AX.X, op=ALU.add),
                     reads=[R_Es], writes=[R_Esum])
                fns = [lambda e, Es=Es, En=En, Esum=Esum, Ew=Ew, SB_=SB_, PB_=PB_, ZB_=ZB_, jj=jj, g=g: e.matmul(bk(PB_)[:, 0:4], lhsT=rawb[:, jj, g * 128:(g + 1) * 128], rhs=Es[:, jj, :],
                                                     start=(jj == 0), stop=False) for jj in range(16)]
                fns.append(lambda e, Es=Es, En=En, Esum=Esum, Ew=Ew, SB_=SB_, PB_=PB_, ZB_=ZB_, g=g: e.matmul(bk(PB_)[:, 0:4], lhsT=vtok[0][0:64, 16, g * 128:(g + 1) * 128], rhs=En[0:64],
                                                   start=False, stop=True))
                fns.append(lambda e, Es=Es, En=En, Esum=Esum, Ew=Ew, SB_=SB_, PB_=PB_, ZB_=ZB_: e.matmul(bk(ZB_)[:, 0:4], lhsT=ones_f, rhs=Esum, start=True, stop=False))
                fns.append(lambda e, Es=Es, En=En, Esum=Esum, Ew=Ew, SB_=SB_, PB_=PB_, ZB_=ZB_: e.matmul(bk(ZB_)[:, 0:4], lhsT=ones_b[0:64, :], rhs=En[0:64], start=False, stop=True))
                c.op("pe", fns, [R_rawb, R_Es, R_En, R_Esum, R_vtok[0], R_cb], [RB[PB_], RB[ZB_]])
                c.op("act", lambda e, Es=Es, En=En, Esum=Esum, Ew=Ew, SB_=SB_, PB_=PB_, ZB_=ZB_, hd=hd, b=b: e.copy(out=PVt[:, 1, hd, 4 * b:4 * b + 4], in_=bk(PB_)[:, 0:4]), writes=[RB[PB_], R_PVt])
                c.op("act", lambda e, Es=Es, En=En, Esum=Esum, Ew=Ew, SB_=SB_, PB_=PB_, ZB_=ZB_, hd=hd, b=b: e.copy(out=Zt[:, 1, hd, 4 * b:4 * b + 4], in_=bk(ZB_)[:, 0:4]), writes=[RB[ZB_], R_Zt])
                fns = [lambda e, Es=Es, En=En, Esum=Esum, Ew=Ew, SB_=SB_, PB_=PB_, ZB_=ZB_, k=k, g=g, q4=q4: e.matmul(bk(SB_)[:, 128 + k * 4:128 + k * 4 + 4], lhsT=KwT[:, k, g, :], rhs=q4,
                                                          start=True, stop=True) for k in range(4)]
                fns.append(lambda e, Es=Es, En=En, Esum=Esum, Ew=Ew, SB_=SB_, PB_=PB_, ZB_=ZB_, g=g, q4=q4: e.matmul(bk(SB_)[0:64, 192:196], lhsT=kws[:, g, :], rhs=q4, start=True, stop=True))
                c.op("pe", fns, [R_KwT, R_qs], [RB[SB_]])
                c.op("act", lambda e, Es=Es, En=En, Esum=Esum, Ew=Ew, SB_=SB_, PB_=PB_, ZB_=ZB_: e.activation(out=Ew, in_=bk(SB_)[:, 128:144].rearrange("p (k l) -> p k l", k=4), func=AF.Exp, scale=SCALE),
                     writes=[RB[SB_], R_Ew])
                c.op("act", lambda e, Es=Es, En=En, Esum=Esum, Ew=Ew, SB_=SB_, PB_=PB_, ZB_=ZB_: e.activation(out=En[0:64], in_=bk(SB_)[0:64, 192:196], func=AF.Exp, scale=SCALE),
                     writes=[RB[SB_], R_En])
                c.op("dve", lambda e, Es=Es, En=En, Esum=Esum, Ew=Ew, SB_=SB_, PB_=PB_, ZB_=ZB_: e.tensor_tensor(out=Ew, in0=Ew, in1=winmask_s, op=ALU.mult), reads=[R_cb], writes=[R_Ew])
                c.op("dve", lambda e, Es=Es, En=En, Esum=Esum, Ew=Ew, SB_=SB_, PB_=PB_, ZB_=ZB_, b=b: e.tensor_tensor(out=En[0:64], in0=En[0:64], in1=newmask[0:64, b, :], op=ALU.mult),
                     reads=[R_cb], writes=[R_En])
                c.op("dve", lambda e, Es=Es, En=En, Esum=Esum, Ew=Ew, SB_=SB_, PB_=PB_, ZB_=ZB_: e.tensor_reduce(out=Esum, in_=Ew.rearrange("p k l -> p l k"), axis=AX.X, op=ALU.add),
                     reads=[R_Ew], writes=[R_Esum])
                fns = [lambda e, Es=Es, En=En, Esum=Esum, Ew=Ew, SB_=SB_, PB_=PB_, ZB_=ZB_, k=k, g=g: e.matmul(bk(PB_)[:, 0:4], lhsT=wrb[:, k, g * 128:(g + 1) * 128], rhs=Ew[:, k, :],
                                                   start=(k == 0), stop=False) for k in range(4)]
                fns.append(lambda e, Es=Es, En=En, Esum=Esum, Ew=Ew, SB_=SB_, PB_=PB_, ZB_=ZB_, g=g: e.matmul(bk(PB_)[:, 0:4], lhsT=vtok[1][0:64, 16, g * 128:(g + 1) * 128], rhs=En[0:64],
                                                   start=False, stop=True))
                fns.append(lambda e, Es=Es, En=En, Esum=Esum, Ew=Ew, SB_=SB_, PB_=PB_, ZB_=ZB_: e.matmul(bk(ZB_)[:, 0:4], lhsT=ones_f, rhs=Esum, start=True, stop=False))
                fns.append(lambda e, Es=Es, En=En, Esum=Esum, Ew=Ew, SB_=SB_, PB_=PB_, ZB_=ZB_: e.matmul(bk(ZB_)[:, 0:4], lhsT=ones_b[0:64, :], rhs=En[0:64], start=False, stop=True))
                c.op("pe", fns, [R_wrb, R_Ew, R_En, R_Esum, R_vtok[1], R_cb], [RB[PB_], RB[ZB_]])
                c.op("act", lambda e, Es=Es, En=En, Esum=Esum, Ew=Ew, SB_=SB_, PB_=PB_, ZB_=ZB_, hd=hd, b=b: e.copy(out=PVt[:, 2, hd, 4 * b:4 * b + 4], in_=bk(PB_)[:, 0:4]), writes=[RB[PB_], R_PVt])
                c.op("act", lambda e, Es=Es, En=En, Esum=Esum, Ew=Ew, SB_=SB_, PB_=PB_, ZB_=ZB_, hd=hd, b=b: e.copy(out=Zt[:, 2, hd, 4 * b:4 * b + 4], in_=bk(ZB_)[:, 0:4]), writes=[RB[ZB_], R_Zt])
        for br in range(3):
            fns = [lambda e, hd=hd, br=br: e.matmul(bk(7)[:, hd * 64:(hd + 1) * 64], lhsT=onehot[0:24, br * 8 + hd, :],
                                                   rhs=sigG[0:24, SEQ:NT], start=True, stop=True) for hd in range(8)]
            c.op("pe", fns, [R_cb, R_sigG], [RB[7]])
            c.op("dve", lambda e, br=br: e.tensor_scalar(out=Fs, in0=Zt[:, br], scalar1=1e-30, scalar2=None, op0=ALU.max),
                 reads=[R_Zt], writes=[R_Fs])
            c.op("dve", lambda e: e.reciprocal(out=Fs, in_=Fs), reads=[R_Fs], writes=[R_Fs])
            c.op("dve", lambda e: e.tensor_tensor(out=Fs, in0=bk(7)[:, :].rearrange("p (h n) -> p h n", h=8), in1=Fs, op=ALU.mult),
                 reads=[R_Fs], writes=[RB[7], R_Fs])
            c.op("dve", lambda e, br=br: e.tensor_tensor(out=Fs, in0=PVt[:, br], in1=Fs, op=ALU.mult), reads=[R_PVt, R_Fs], writes=[R_Fs])
            if br == 0:
                c.op("dve", lambda e: e.tensor_copy(out=osm, in_=Fs), reads=[R_Fs], writes=[R_osm])
            else:
                c.op("dve", lambda e: e.tensor_tensor(out=osm, in0=osm, in1=Fs, op=ALU.add), reads=[R_Fs, R_osm], writes=[R_osm])
        c.op("act", lambda e: e.copy(out=osb, in_=osm), reads=[R_osm], writes=[R_osb])
        for hd in range(8):
            c.dma("pool", lambda e, hd=hd: e.dma_start(out=oscr[1024 + hd * 128:1024 + (hd + 1) * 128, SEQ:NT], in_=osb[:, hd, :]),
                  reads=[R_osb])
        ar.top = nsa_top

    if stop_after < 4:
        return finish()
    c.barrier()
    ar.top = mark_persist
    hT = ar.alloc([16, 512], F32); R_hT = Res()
    actb = ar.alloc([16, 512], BF16); R_actb = Res()
    hff = ar.alloc([NFF, 512], BF16); R_hff = Res()
    sqb = hff[:, 0:16, :]; R_sqb = R_hff
    wsl = [ar.alloc([8192], BF16) for _ in range(3)]; R_wsl = [Res(), Res(), Res()]
    xs3 = [ar.alloc([D], F32) for _ in range(2)]; R_xs3 = [Res(), Res()]
    aext = ar.alloc([520], F32); R_aext = Res()
    cv1 = ar.alloc([512], F32); R_cv1 = Res()
    pst = ar.alloc([256], F32); R_pst = Res()
    geb = pst.bitcast(BF16); R_geb = R_pst
    sg = ar.alloc([512], F32); R_sg = Res()
    sd3 = ar.alloc([512], F32); R_sd3 = Res()
    pT3 = ar.alloc([2, 512], BF16); R_pT3 = Res()
    carry = ar.alloc([NFF, 2], F32); R_carry = Res()
    stf = ar.alloc([128], F32); R_stf = Res()
    cst = sd3; R_cst = R_sd3
    colg = ar.alloc([16], F32); colf = ar.alloc([16], F32); colb = ar.alloc([16], F32)
    colc = ar.alloc([4, NFF], F32); R_col = Res()
    rowst = ar.alloc([128], F32); R_rowst = Res()

    def load_cols(dst_ap, src1d, nrow):
        c.dma("sp", lambda e: e.dma_start(out=rowst[0:nrow], in_=src1d.rearrange("(t p) -> t p", p=128)), writes=[R_rowst])
        tposes([(bk(0)[:, 0:nrow], rowst[0:nrow, :], ident_f[0:nrow, 0:nrow])], [R_rowst, R_cf], [RB[0]])
        c.op("act", lambda e: e.copy(out=dst_ap, in_=bk(0)[:, 0:nrow]), writes=[RB[0], R_col])

    load_cols(colg, ffn_g, 16)
    load_cols(colf, fin_g, 16)
    load_cols(colb, pleb, 16)
    for j in range(3):
        load_cols(colc[:, j, :], convw[j, :], NFF)
    load_cols(colc[:, 3, :], convb, NFF)
    c.op("pool", lambda e: e.memset(carry, 0.0), writes=[R_carry])
    wsi = [0]

    def load_w(src2d, nkt, col0, ncols=128):
        sl = wsi[0] % 3
        wsi[0] += 1
        view = wsl[sl][:, 0:nkt * ncols].rearrange("p (k n) -> p k n", k=nkt)
        k0 = 0
        while k0 < nkt:
            kk = min(8 if ncols >= 512 else 4, nkt - k0)
            wload("pool", view[:, k0:k0 + kk, :],
                  src2d[128 * k0:128 * (k0 + kk), col0:col0 + ncols].rearrange("(kt p) n -> p kt n", p=128), R_wsl[sl])
            k0 += kk
        return sl, view

    xsi = [0]
    for ti, (n0, N) in enumerate(NTILES):
        smp = (ti == 4)
        c.dma("sp", lambda e, N=N, n0=n0: e.dma_start(
            out=actb[:, :, 0:N], in_=oscr[:, n0:n0 + N].rearrange("(kt p) n -> p kt n", p=128)), writes=[R_actb])
        nblk = (N + 127) // 128
        for bi in range(nblk):
            rows = min(128, N - bi * 128)
            xi = xsi[0] % 2
            xsi[0] += 1
            srcx = xs[:, :] if smp else xp[n0 + bi * 128:n0 + bi * 128 + 128, :]
            c.dma("sp", lambda e, N=N, xi=xi, srcx=srcx, rows=rows: e.dma_start(out=xs3[xi][0:rows], in_=srcx),
                  writes=[R_xs3[xi]])
            for q4 in range(4):
                b = q4 % 2
                tposes([(bk(b)[:, j * 128:j * 128 + rows], xs3[xi][0:rows, (4 * q4 + j) * 128:(4 * q4 + j + 1) * 128],
                         ident_f[0:rows, 0:rows]) for j in range(4)], [R_xs3[xi], R_cf], [RB[b]])
                c.op("act", lambda e, N=N, b=b, q4=q4, bi=bi, rows=rows: e.copy(
                    out=hT[:, 4 * q4:4 * q4 + 4, bi * 128:bi * 128 + rows],
                    in_=bk(b)[:, :].rearrange("p (j t) -> p j t", j=4)[:, :, 0:rows]),
                    writes=[RB[b], R_hT])
            srcp = ps_[:, :] if smp else pp[n0 + bi * 128:n0 + bi * 128 + 128, :]
            c.dma("sp", lambda e, N=N, srcp=srcp, rows=rows: e.dma_start(out=pst[0:rows], in_=srcp), writes=[R_pst])
            tposes([(bk(2)[:, j * 128:j * 128 + rows], pst[0:rows, j * 128:(j + 1) * 128], ident_f[0:rows, 0:rows])
                    for j in range(2)], [R_pst, R_cf], [RB[2]])
            c.op("act", lambda e, N=N, bi=bi, rows=rows: e.copy(
                out=pT3[:, :, bi * 128:bi * 128 + rows],
                in_=bk(2)[:, 0:256].rearrange("p (j t) -> p j t", j=2)[:, :, 0:rows]), writes=[RB[2], R_pT3])
        for m in range(16):
            if m % 4 == 0:
                sl, wv_ = load_w(w_out, 16, m * 128, 512)
            mo = (m % 4) * 128
            b = 4 + (m % 2)
            mms(bk(b)[:, 0:N], [(wv_[:, kt, mo:mo + 128], actb[:, kt, 0:N]) for kt in range(16)], [R_wsl[sl], R_actb], [RB[b]])
            c.op("dve", lambda e, N=N, b=b, m=m: e.tensor_tensor(out=hT[:, m, 0:N], in0=bk(b)[:, 0:N], in1=hT[:, m, 0:N],
                                                                 op=ALU.add), writes=[RB[b], R_hT])

        def rms3(gcol, out_bf):
            c.op("act", lambda e, N=N, n0=n0, smp=smp: e.activation(out=sqb[:, :, 0:N], in_=hT[:, :, 0:N], func=AF.Square),
                 reads=[R_hT], writes=[R_sqb])
            mms(bk(6)[:, 0:N], [(ones_b, sqb[:, m, 0:N]) for m in range(16)], [R_sqb, R_cb], [RB[6]])
            c.op("act", lambda e, N=N, n0=n0, smp=smp: e.activation(out=sd3[:, 0:N], in_=bk(6)[:, 0:N], func=AF.Sqrt, scale=1.0 / D, bias=EPS),
                 writes=[RB[6], R_sd3])
            c.op("dve", lambda e, N=N, n0=n0, smp=smp: e.reciprocal(out=sd3[:, 0:N], in_=sd3[:, 0:N]), reads=[R_sd3], writes=[R_sd3])
            for m in range(16):
                if out_bf:
                    c.op("dve", lambda e, N=N, m=m: e.scalar_tensor_tensor(
                        out=actb[:, m, 0:N], in0=hT[:, m, 0:N], scalar=gcol[:, m:m + 1], in1=sd3[:, 0:N],
                        op0=ALU.mult, op1=ALU.mult), reads=[R_hT, R_sd3, R_col], writes=[R_actb])
                else:
                    c.op("dve", lambda e, N=N, m=m: e.scalar_tensor_tensor(
                        out=hT[:, m, 0:N], in0=hT[:, m, 0:N], scalar=gcol[:, m:m + 1], in1=sd3[:, 0:N],
                        op0=ALU.mult, op1=ALU.mult), reads=[R_sd3, R_col], writes=[R_hT])

        rms3(colg, True)
        for f in range(NFF):
            if f % 4 == 0:
                nf_ = min(4, NFF - f)
                sa, wa_ = load_w(w_up, 16, f * 128, nf_ * 128)
                su, wu_ = load_w(w_up, 16, DFF + f * 128, nf_ * 128)
            fo = (f % 4) * 128
            mms(bk(0)[:, 0:N], [(wa_[:, kt, fo:fo + 128], actb[:, kt, 0:N]) for kt in range(16)], [R_wsl[sa], R_actb], [RB[0]])
            mms(bk(1)[:, 0:N], [(wu_[:, kt, fo:fo + 128], actb[:, kt, 0:N]) for kt in range(16)], [R_wsl[su], R_actb], [RB[1]])
            w0 = colc[:, 0, f:f + 1]; w1 = colc[:, 1, f:f + 1]; w2 = colc[:, 2, f:f + 1]; bb_ = colc[:, 3, f:f + 1]
            if not smp:
                c.op("act", lambda e, N=N, n0=n0, smp=smp: e.copy(out=aext[:, 2:2 + N], in_=bk(0)[:, 0:N]), writes=[RB[0], R_aext])
                c.op("act", lambda e, N=N, f=f: e.copy(out=aext[:, 0:2], in_=carry[:, f, :]), reads=[R_carry], writes=[R_aext])
                c.op("act", lambda e, N=N, f=f: e.copy(out=carry[:, f, :], in_=aext[:, N:N + 2]), reads=[R_aext],
                     writes=[R_carry])
                v0, v1, v2, vo = aext[:, 0:N], aext[:, 1:N + 1], aext[:, 2:N + 2], cv1[:, 0:N]
            else:
                a6 = aext[:, 0:96].rearrange("p (b s) -> p b s", s=6)
                c.op("act", lambda e, N=N, a6=a6: e.copy(out=a6[:, :, 2:6], in_=bk(0)[:, 0:64].rearrange("p (b l) -> p b l", l=4)),
                     writes=[RB[0], R_aext])
                c.dma("sp", lambda e, N=N, f=f: e.dma_start(out=stf[0:32], in_=sconv[:, f * 128:(f + 1) * 128]), writes=[R_stf])
                tposes([(bk(2)[:, 0:32], stf[0:32, :], ident_f[0:32, 0:32])], [R_stf, R_cf], [RB[2]])
                c.op("act", lambda e, N=N, a6=a6: e.copy(out=a6[:, :, 0:2], in_=bk(2)[:, 0:32].rearrange("p (b s) -> p b s", s=2)),
                     writes=[RB[2], R_aext])
                v0, v1, v2 = a6[:, :, 0:4], a6[:, :, 1:5], a6[:, :, 2:6]
                vo = cv1[:, 0:64].rearrange("p (b l) -> p b l", l=4)
            c.op("dve", lambda e, N=N, v0=v0, vo=vo, w0=w0, bb_=bb_: e.tensor_scalar(out=vo, in0=v0, scalar1=w0, scalar2=bb_,
                                                                               op0=ALU.mult, op1=ALU.add),
                 reads=[R_aext, R_col], writes=[R_cv1])
            c.op("dve", lambda e, N=N, v1=v1, vo=vo, w1=w1: e.scalar_tensor_tensor(out=vo, in0=v1, scalar=w1, in1=vo,
                                                                             op0=ALU.mult, op1=ALU.add),
                 reads=[R_aext, R_col, R_cv1], writes=[R_cv1])
            c.op("dve", lambda e, N=N, v2=v2, vo=vo, w2=w2: e.scalar_tensor_tensor(out=vo, in0=v2, scalar=w2, in1=vo,
                                                                             op0=ALU.mult, op1=ALU.add),
                 reads=[R_aext, R_col, R_cv1], writes=[R_cv1])
            c.op("act", lambda e, N=N, n0=n0, smp=smp: e.activation(out=geb[:, 0:N], in_=cv1[:, 0:N], func=AF.Gelu_apprx_tanh),
                 reads=[R_cv1], writes=[R_geb])
            c.op("dve", lambda e, N=N, f=f: e.tensor_tensor(out=hff[:, f, 0:N], in0=bk(1)[:, 0:N], in1=geb[:, 0:N], op=ALU.mult),
                 reads=[R_geb], writes=[RB[1], R_hff])
            if ti == 3 or smp:
                g4 = f % 4
                if smp:
                    c.op("act", lambda e, N=N, n0=n0, smp=smp: e.copy(
                        out=sg[:, 0:32].rearrange("p (b s) -> p b s", s=2),
                        in_=aext[:, 0:96].rearrange("p (b s) -> p b s", s=6)[:, :, 4:6]), reads=[R_aext], writes=[R_sg])
                    src_t = sg[:, 0:32]
                    nr = 32
                else:
                    src_t = aext[:, N:N + 2]
                    nr = 2
                tposes([(bk(3)[0:nr, g4 * 128:(g4 + 1) * 128], src_t, ident_f)], [R_aext, R_sg, R_cf], [RB[3]])
                if g4 == 3 or f == NFF - 1:
                    f0 = f - g4
                    wd = (g4 + 1) * 128
                    c.op("act", lambda e, N=N, nr=nr, wd=wd: e.copy(out=cst[0:nr, 0:wd], in_=bk(3)[0:nr, 0:wd]),
                         writes=[RB[3], R_cst])
                    dstc = (convs.rearrange("b s n -> (b s) n") if smp else convp)[:, f0 * 128:f0 * 128 + wd]
                    out_toks.append(c.dma("pool", lambda e, N=N, dstc=dstc, nr=nr, wd=wd: e.dma_start(out=dstc, in_=cst[0:nr, 0:wd]),
                                          reads=[R_cst]))
        for m in range(16):
            sl, wd_ = load_w(w_dn, NFF, m * 128)
            b = 4 + (m % 2)
            mms(bk(b)[:, 0:N], [(wd_[:, kt, :], hff[:, kt, 0:N]) for kt in range(NFF)], [R_wsl[sl], R_hff], [RB[b]])
            c.op("dve", lambda e, N=N, b=b, m=m: e.tensor_tensor(out=hT[:, m, 0:N], in0=bk(b)[:, 0:N], in1=hT[:, m, 0:N],
                                                            op=ALU.add), writes=[RB[b], R_hT])
        c.op("act", lambda e, N=N, n0=n0, smp=smp: e.copy(out=actb[:, :, 0:N], in_=hT[:, :, 0:N]), reads=[R_hT], writes=[R_actb])
        for m in range(16):
            if m % 4 == 0:
                sl, wg_ = load_w(pleg, 16, m * 128, 512)
                sp_, wp_ = load_w(plep, 2, m * 128, 512)
            mo = (m % 4) * 128
            b = 4 + (m % 2)
            mms(bk(b)[:, 0:N], [(wg_[:, kt, mo:mo + 128], actb[:, kt, 0:N]) for kt in range(16)], [R_wsl[sl], R_actb], [RB[b]])
            c.op("act", lambda e, N=N, b=b, m=m: e.activation(out=sg[:, 0:N], in_=bk(b)[:, 0:N], func=AF.Sigmoid,
                                                         bias=colb[:, m:m + 1], scale=1.0),
                 reads=[R_col], writes=[RB[b], R_sg])
            mms(bk(7)[:, 0:N], [(wp_[:, kt, mo:mo + 128], pT3[:, kt, 0:N]) for kt in range(2)], [R_wsl[sp_], R_pT3], [RB[7]])
            c.op("dve", lambda e, N=N, n0=n0, smp=smp: e.tensor_tensor(out=sg[:, 0:N], in0=bk(7)[:, 0:N], in1=sg[:, 0:N], op=ALU.mult),
                 reads=[R_sg], writes=[RB[7], R_sg])
            c.op("dve", lambda e, N=N, m=m: e.tensor_tensor(out=hT[:, m, 0:N], in0=hT[:, m, 0:N], in1=sg[:, 0:N], op=ALU.add),
                 reads=[R_sg], writes=[R_hT])
        rms3(colf, False)
        for bi in range(nblk):
            rows = min(128, N - bi * 128)
            xi = xsi[0] % 2
            xsi[0] += 1
            for q4 in range(4):
                b = q4 % 2
                tposes([(bk(b)[0:rows, j * 128:(j + 1) * 128], hT[:, 4 * q4 + j, bi * 128:bi * 128 + rows], ident_f)
                        for j in range(4)], [R_hT, R_cf], [RB[b]])
                c.op("act", lambda e, N=N, b=b, q4=q4, xi=xi, rows=rows: e.copy(out=xs3[xi][0:rows, 512 * q4:512 * q4 + 512],
                                                                          in_=bk(b)[0:rows, :]),
                     writes=[RB[b], R_xs3[xi]])
            dsty = y_s[:, :] if smp else y_p[n0 + bi * 128:n0 + bi * 128 + 128, :]
            out_toks.append(c.dma("pool", lambda e, N=N, dsty=dsty, xi=xi, rows=rows: e.dma_start(out=dsty, in_=xs3[xi][0:rows]),
                                  reads=[R_xs3[xi]]))

    return finish()


_CACHE = {}


def kernel(**inp):
    f32 = np.float32
    cbm, cfm = CONSTS[0], CONSTS[1]
    if "nc" not in _CACHE:
        import os
        _CACHE["nc"] = build(int(os.environ.get("STOP_AFTER", "99")))
    nc = _CACHE["nc"]
    A = lambda a: np.ascontiguousarray(a)
    shared = dict(
        attn_g=A(inp["attn_norm_g"][0]), w_in=A(inp["w_in"][0]), wa2=A(inp["gla_w_a2"][0]), b_a=A(inp["gla_b_a"][0]),
        glag=A(inp["gla_norm_g"][0]), cb=cbm, cf=cfm,
        cck=A(inp["cache_cmp_k"][0].reshape(2560 * 8, 4096)), ccv=A(inp["cache_cmp_v"][0].reshape(2560 * 8, 4096)),
        csk=A(inp["cache_slc_k"][0].reshape(2560 * 8, 4096)), csv=A(inp["cache_slc_v"][0].reshape(2560 * 8, 4096)),
        wk1=A(inp["cmp_wk1"][0]), wk2=A(inp["cmp_wk2"][0]), wv1=A(inp["cmp_wv1"][0]), wv2=A(inp["cmp_wv2"][0]),
        w_out=A(inp["w_out"][0]), ffn_g=A(inp["ffn_norm_g"][0]),
        w_up=A(inp["ffn_w_up"][0]), convw=A(inp["ffn_conv_w"][0]), convb=A(inp["ffn_conv_b"][0]),
        w_dn=A(inp["ffn_w_down"][0]), plep=A(inp["ple_w_proj"][0]), pleg=A(inp["ple_w_gate"][0]),
        pleb=A(inp["ple_b_gate"][0]), fin_g=A(inp["final_norm_g"]))
    in_maps = []
    for cix in range(8):
        b = cix % 4
        sl = slice(16 * cix, 16 * cix + 16)
        m = dict(shared)
        m.update(
            xp=A(inp["x_prompt"][b]), xs=A(inp["x_sample"][sl].reshape(NS, D)),
            swk=A(inp["state_win_k"][0, sl].reshape(16, 512, 256)), swv=A(inp["state_win_v"][0, sl].reshape(16, 512, 256)),
            sgla=A(inp["state_gla"][0, sl]), pt=A(inp["page_table"][sl].astype(np.int32)),
            pp=A(inp["p_prompt"][0, b]), ps=A(inp["p_sample"][0, sl].reshape(NS, 256)),
            sconv=A(inp["state_ffn_conv"][0, sl].reshape(32, DFF)))
        in_maps.append(m)
    res = run_bass_kernel_spmd(nc, in_maps, core_ids=list(range(8)))
    R = res.results
    _CACHE["raw"] = R

    def pcat(name, shape):
        return np.stack([np.asarray(R[b][name]).reshape(shape) for b in range(4)], 0)[None]

    def scat(name, shape):
        return np.concatenate([np.asarray(R[cx][name]).reshape((16,) + shape) for cx in range(8)], 0)[None]

    y_prompt = np.stack([np.asarray(R[b]["y_p"]) for b in range(4)], 0)
    y_sample = np.concatenate([np.asarray(R[cx]["y_s"]).reshape(16, 4, D) for cx in range(8)], 0)
    outs = [y_prompt, y_sample]
    for n in ("ckp", "cvp", "skp", "svp"):
        outs.append(pcat(n, (SEQ, 2, 128)))
    outs.append(pcat("wkp", (512, 2, 128)))
    outs.append(pcat("wvp", (512, 2, 128)))
    outs.append(pcat("glap", (4, 128, 256)))
    outs.append(pcat("convp", (2, DFF)))
    for n in ("cks", "cvs", "sks", "svs"):
        outs.append(scat(n, (4, 2, 128)))
    outs.append(scat("wks", (512, 2, 128)))
    outs.append(scat("wvs", (512, 2, 128)))
    outs.append(scat("glas", (4, 128, 256)))
    outs.append(scat("convs", (2, DFF)))
    return tuple(o.astype(f32) for o in outs)
```
